# Optimizing a Trainium2 kernel written in Bass

```python
import jax, jax.numpy as jnp
from jax import lax
import numpy as np

D_MODEL = 1024
BATCH = 8
SEQ = 2048
DEPTH = 2
DEC_BATCH = 128
DEC_SEQ = 4
PAST_LEN = 16384
PAGE_SIZE = 128

HEAD_DIM = 64
N_HEADS = D_MODEL // HEAD_DIM
N_KV_HEADS = 4
Q_PER_KV = N_HEADS // N_KV_HEADS
WINDOW = 128
BLOCK = 128
C_CONV = D_MODEL
CONV_K = 31
D_FF = 2816
D_Q = N_HEADS * HEAD_DIM
D_KV = N_KV_HEADS * HEAD_DIM
D_IN = D_Q + 2 * D_KV + 2 * C_CONV + 2 * D_MODEL
SPLITS = (D_Q, D_Q + D_KV, D_Q + 2 * D_KV, D_Q + 2 * D_KV + 2 * C_CONV)
ALPHA = (2 * DEPTH) ** 0.25
BETA = (8 * DEPTH) ** -0.25
LN_EPS = 1e-5
ATTN_SCALE = HEAD_DIM ** -0.5
NEG_INF = -1e30

kernel_name = "hybrid_swa_sink_conformer_conv_deepnorm_step"


def layer_norm(x, g, b):
    xf = x.astype(jnp.float32)
    mu = jnp.mean(xf, axis=-1, keepdims=True)
    var = jnp.mean(jnp.square(xf - mu), axis=-1, keepdims=True)
    y = (xf - mu) * lax.rsqrt(var + LN_EPS) * g.astype(jnp.float32) + b.astype(jnp.float32)
    return y.astype(x.dtype)


def swiglu(x, w_up, w_down):
    a, u = jnp.split(x @ w_up, 2, axis=-1)
    return (jax.nn.silu(a) * u) @ w_down


def alibi_slopes():
    return jnp.asarray(2.0 ** (-8.0 * np.arange(1, N_HEADS + 1) / N_HEADS), dtype=jnp.float32)


def window_attention(q, k, v, dist, valid, sinks):
    slopes = alibi_slopes().reshape(N_KV_HEADS, Q_PER_KV)[:, :, None, None, None]
    s = jnp.einsum('nbqhgd,nbshd->nhgbqs', q, k, preferred_element_type=jnp.float32) * ATTN_SCALE
    s = jnp.where(valid, s - slopes * dist, NEG_INF)
    sink = sinks.astype(jnp.float32).reshape(N_KV_HEADS, Q_PER_KV)[:, :, None, None]
    m = jnp.maximum(jnp.max(s, axis=-1), sink)
    p = jnp.exp(s - m[..., None])
    denom = jnp.sum(p, axis=-1) + jnp.exp(sink - m)
    p = p / denom[..., None]
    return jnp.einsum('nhgbqs,nbshd->nbqhgd', p.astype(v.dtype), v)


def prompt_attention(q, k, v, sinks):
    n, s_len = k.shape[:2]
    nb = s_len // BLOCK
    qb = q.reshape(n, nb, BLOCK, N_KV_HEADS, Q_PER_KV, HEAD_DIM)
    kb = k.reshape(n, nb, BLOCK, N_KV_HEADS, HEAD_DIM)
    vb = v.reshape(n, nb, BLOCK, N_KV_HEADS, HEAD_DIM)

    def band(t):
        prev = jnp.concatenate([jnp.zeros_like(t[:, :1]), t[:, :-1]], axis=1)
        return jnp.concatenate([prev, t], axis=2)

    r = jnp.arange(BLOCK)[None, :, None]
    c = jnp.arange(2 * BLOCK)[None, None, :]
    blk = jnp.arange(nb)[:, None, None]
    dist = BLOCK + r - c
    valid = (dist >= 0) & (dist <= WINDOW) & (blk * BLOCK + c - BLOCK >= 0)
    o = window_attention(qb, band(kb), band(vb), dist.astype(jnp.float32), valid, sinks)
    return o.reshape(n, s_len, D_Q)


def sample_attention(q, k_all, v_all, sinks):
    n, t_len = q.shape[:2]
    tk = k_all.shape[1]
    j = jnp.arange(t_len)[:, None]
    c = jnp.arange(tk)[None, :]
    dist = (tk - t_len) + j - c
    valid = (dist >= 0) & (dist <= WINDOW)
    o = window_attention(q[:, None], k_all[:, None], v_all[:, None],
                         dist[None].astype(jnp.float32), valid[None], sinks)
    return o.reshape(n, t_len, D_Q)


def causal_depthwise(u_ext, w, b):
    y = lax.conv_general_dilated(u_ext, w[:, None, :].astype(u_ext.dtype), window_strides=(1,),
                                 padding='VALID', dimension_numbers=('NWC', 'WIO', 'NWC'),
                                 feature_group_count=u_ext.shape[-1])
    return y + b


def decoder_layer(x, k_past, v_past, conv_past, p):
    n, t_len = x.shape[:2]
    x = layer_norm(ALPHA * x + 0.5 * swiglu(x, p['ffn1_up'], p['ffn1_down']), p['ln1_g'], p['ln1_b'])
    proj = x @ p['w_in'] + p['b_in']
    q, k, v, glu, gates = jnp.split(proj, SPLITS, axis=-1)
    q = q.reshape(n, t_len, N_KV_HEADS, Q_PER_KV, HEAD_DIM)
    k = k.reshape(n, t_len, N_KV_HEADS, HEAD_DIM)
    v = v.reshape(n, t_len, N_KV_HEADS, HEAD_DIM)
    u_a, u_g = jnp.split(glu, 2, axis=-1)
    u = u_a * jax.nn.sigmoid(u_g)
    if k_past is None:
        attn = prompt_attention(q, k, v, p['sinks'])
        k_all, v_all = k, v
        u_ext = jnp.concatenate([jnp.zeros((n, CONV_K - 1, C_CONV), u.dtype), u], axis=1)
    else:
        k_all = jnp.concatenate([k_past, k], axis=1)
        v_all = jnp.concatenate([v_past, v], axis=1)
        attn = sample_attention(q, k_all, v_all, p['sinks'])
        u_ext = jnp.concatenate([conv_past, u], axis=1)
    c = causal_depthwise(u_ext, p['conv_dw_w'], p['conv_dw_b'])
    c = jax.nn.silu(layer_norm(c, p['conv_ln_g'], p['conv_ln_b']))
    g_attn, g_conv = jnp.split(gates, 2, axis=-1)
    mix = (jax.nn.sigmoid(g_attn) * (attn @ p['w_attn_out'])
           + jax.nn.sigmoid(g_conv) * (c @ p['w_conv_out']))
    x = layer_norm(ALPHA * x + mix @ p['w_out'], p['ln2_g'], p['ln2_b'])
    x = layer_norm(ALPHA * x + 0.5 * swiglu(x, p['ffn2_up'], p['ffn2_down']), p['ln3_g'], p['ln3_b'])
    return x, k_all[:, -WINDOW:], v_all[:, -WINDOW:], u_ext[:, -(CONV_K - 1):]


def setup_inputs(seed: int = 0) -> dict:
    key = jax.random.key(seed)
    ks = jax.random.split(key, 32)
    f32 = jnp.float32

    def w(k, shape, fan_in, scale=1.0):
        return jax.random.normal(k, shape, f32) * (fan_in ** -0.5) * scale

    def gain(k, shape):
        return 1.0 + 0.01 * jax.random.normal(k, shape, f32)

    def bias(k, shape):
        return 0.01 * jax.random.normal(k, shape, f32)

    L = DEPTH
    return {
        "x_prompt": jax.random.normal(ks[0], (BATCH, SEQ, D_MODEL), f32),
        "x_sample": jax.random.normal(ks[1], (DEC_BATCH, DEC_SEQ, D_MODEL), f32),
        "cache_k": jax.random.normal(ks[2], (L, DEC_BATCH, WINDOW, N_KV_HEADS, HEAD_DIM), f32),
        "cache_v": jax.random.normal(ks[3], (L, DEC_BATCH, WINDOW, N_KV_HEADS, HEAD_DIM), f32),
        "state_conv": 0.5 * jax.random.normal(ks[4], (L, DEC_BATCH, CONV_K - 1, C_CONV), f32),
        "ffn1_up": w(ks[5], (L, D_MODEL, 2 * D_FF), D_MODEL),
        "ffn1_down": w(ks[6], (L, D_FF, D_MODEL), D_FF, BETA),
        "ln1_g": gain(ks[7], (L, D_MODEL)),
        "ln1_b": bias(ks[8], (L, D_MODEL)),
        "w_in": w(ks[9], (L, D_MODEL, D_IN), D_MODEL),
        "b_in": bias(ks[10], (L, D_IN)),
        "attn_sinks": 0.5 * jax.random.normal(ks[11], (L, N_HEADS), f32),
        "conv_dw_w": w(ks[12], (L, CONV_K, C_CONV), CONV_K),
        "conv_dw_b": bias(ks[13], (L, C_CONV)),
        "conv_ln_g": gain(ks[14], (L, C_CONV)),
        "conv_ln_b": bias(ks[15], (L, C_CONV)),
        "w_attn_out": w(ks[16], (L, D_Q, D_MODEL), D_Q),
        "w_conv_out": w(ks[17], (L, C_CONV, D_MODEL), C_CONV),
        "w_out": w(ks[18], (L, D_MODEL, D_MODEL), D_MODEL, BETA),
        "ln2_g": gain(ks[19], (L, D_MODEL)),
        "ln2_b": bias(ks[20], (L, D_MODEL)),
        "ffn2_up": w(ks[21], (L, D_MODEL, 2 * D_FF), D_MODEL),
        "ffn2_down": w(ks[22], (L, D_FF, D_MODEL), D_FF, BETA),
        "ln3_g": gain(ks[23], (L, D_MODEL)),
        "ln3_b": bias(ks[24], (L, D_MODEL)),
    }


def reference(x_prompt, x_sample, cache_k, cache_v, state_conv,
              ffn1_up, ffn1_down, ln1_g, ln1_b, w_in, b_in, attn_sinks,
              conv_dw_w, conv_dw_b, conv_ln_g, conv_ln_b, w_attn_out, w_conv_out, w_out,
              ln2_g, ln2_b, ffn2_up, ffn2_down, ln3_g, ln3_b):
    xp, xs = x_prompt, x_sample
    kp_l, vp_l, cp_l, ks_l, vs_l, cs_l = [], [], [], [], [], []
    for l in range(DEPTH):
        p = {
            'ffn1_up': ffn1_up[l], 'ffn1_down': ffn1_down[l], 'ln1_g': ln1_g[l], 'ln1_b': ln1_b[l],
            'w_in': w_in[l], 'b_in': b_in[l], 'sinks': attn_sinks[l],
            'conv_dw_w': conv_dw_w[l], 'conv_dw_b': conv_dw_b[l],
            'conv_ln_g': conv_ln_g[l], 'conv_ln_b': conv_ln_b[l],
            'w_attn_out': w_attn_out[l], 'w_conv_out': w_conv_out[l], 'w_out': w_out[l],
            'ln2_g': ln2_g[l], 'ln2_b': ln2_b[l],
            'ffn2_up': ffn2_up[l], 'ffn2_down': ffn2_down[l], 'ln3_g': ln3_g[l], 'ln3_b': ln3_b[l],
        }
        xp, kp, vp, cp = decoder_layer(xp, None, None, None, p)
        xs, kn, vn, cn = decoder_layer(xs, cache_k[l], cache_v[l], state_conv[l], p)
        kp_l.append(kp); vp_l.append(vp); cp_l.append(cp)
        ks_l.append(kn); vs_l.append(vn); cs_l.append(cn)
    new_k_prompt = jnp.stack(kp_l)
    new_v_prompt = jnp.stack(vp_l)
    new_conv_prompt = jnp.stack(cp_l)
    new_k_sample = jnp.stack(ks_l)
    new_v_sample = jnp.stack(vs_l)
    new_conv_sample = jnp.stack(cs_l)
    return (xp, xs, new_k_prompt, new_v_prompt, new_conv_prompt, new_k_sample, new_v_sample, new_conv_sample)
```

```python
import os
import numpy as np
import concourse.bass as bass
import concourse.mybir as mybir
from concourse.bass_utils import run_bass_kernel_spmd

F32 = mybir.dt.float32
BF16 = mybir.dt.bfloat16
ALU = mybir.AluOpType
AF = mybir.ActivationFunctionType

NCORES = 8
D = 1024
DFF = 2816
DIN = 5632
NJ = 22
TP = 1024
TS = 32
T = TP + TS
NST = 2
NSEQ = 8
UNITS = [(0, 512), (512, 512), (1024, 32)]
ALPHA = 4.0 ** 0.25
LN_EPS = 1e-5
NEG = -1.0e6
SLOPES = [2.0 ** (-8.0 * (h + 1) / 16.0) for h in range(16)]
NS = 4
SLOT = 4096
TILES_PER_LAYER = 55
DEBUG = False


def chunk_heads(c):
    g2, i = c // 4, c % 4
    return (2 * g2) * 4 + i, (2 * g2 + 1) * 4 + i


def head_of(g, i):
    return g * 4 + i


class PL:
    cols = {}
    n = 0

    @classmethod
    def add(cls, name, w):
        cls.cols[name] = (cls.n, w)
        cls.n += w


for _l in range(2):
    for nm in ["ln1_g", "ln1_b", "ln2_g", "ln2_b", "ln3_g", "ln3_b", "cln_g", "cln_b", "conv_b",
               "bq", "bga", "bgg", "bta", "btc"]:
        PL.add(f"{nm}{_l}", 8)
    PL.add(f"bk{_l}", 2)
    PL.add(f"convw{_l}", 8 * 31)
    PL.add(f"bkv_b{_l}", 512)
    PL.add(f"sink_b{_l}", 16)
    PL.add(f"sinkp{_l}", 8)
NPAR = PL.n


def pack_params(inp):
    P = np.zeros((128, NPAR), np.float32)

    def pm(v):
        return np.ascontiguousarray(v.reshape(-1, 128).T)

    for l in range(2):
        def put(name, arr):
            o, w = PL.cols[f"{name}{l}"]
            assert arr.shape == (128, w), (name, arr.shape, w)
            P[:, o:o + w] = arr
        put("ln1_g", pm(inp["ln1_g"][l])); put("ln1_b", pm(inp["ln1_b"][l]))
        put("ln2_g", pm(inp["ln2_g"][l])); put("ln2_b", pm(inp["ln2_b"][l]))
        put("ln3_g", pm(inp["ln3_g"][l])); put("ln3_b", pm(inp["ln3_b"][l]))
        put("cln_g", pm(inp["conv_ln_g"][l])); put("cln_b", pm(inp["conv_ln_b"][l]))
        put("conv_b", pm(inp["conv_dw_b"][l]))
        b = inp["b_in"][l]
        bq = b[0:1024].reshape(16, 64)
        bqp = np.zeros((128, 8), np.float32)
        for c in range(8):
            ha, hb = chunk_heads(c)
            bqp[0:64, c] = bq[ha]
            bqp[64:128, c] = bq[hb]
        put("bq", bqp)
        put("bk", pm(b[1024:1280]))
        put("bga", pm(b[1536:2560])); put("bgg", pm(b[2560:3584]))
        put("bta", pm(b[3584:4608])); put("btc", pm(b[4608:5632]))
        cw = inp["conv_dw_w"][l]
        put("convw", np.ascontiguousarray(cw.reshape(31, 8, 128).transpose(2, 1, 0).reshape(128, 248)))
        put("bkv_b", np.broadcast_to(b[1024:1536][None, :], (128, 512)).copy())
        put("sink_b", np.broadcast_to(inp["attn_sinks"][l][None, :], (128, 16)).copy())
        sp_ = np.zeros((128, 8), np.float32)
        for c in range(8):
            ha, hb = chunk_heads(c)
            sp_[0:64, c] = inp["attn_sinks"][l][ha]
            sp_[64:128, c] = inp["attn_sinks"][l][hb]
        put("sinkp", sp_)
    return P


def pack_consts():
    C = {}
    C["ident"] = np.eye(128, dtype=np.float32)
    c = np.arange(128)[:, None]
    r = np.arange(128)[None, :]
    d_prev = 128 + r - c
    d_cur = r - c
    Dp = np.zeros((128, 2, 128), np.float32)
    Dp[:, 0, :] = np.where(c >= r, -d_prev, NEG)
    Dp[:, 1, :] = np.where(c <= r, -d_cur, NEG)
    C["dp"] = Dp.reshape(128, 256)
    t = np.arange(4)[None, :]
    Dc = np.where(c >= t, -(128 + t - c), NEG).astype(np.float32)
    C["dc"] = np.ascontiguousarray(Dc)
    kb, kt = np.divmod(np.arange(32), 4)
    Dn = np.where((kb[:, None] == kb[None, :]) & (kt[:, None] <= kt[None, :]),
                  -(kt[None, :] - kt[:, None]), NEG).astype(np.float32)
    Dn_full = np.zeros((128, 32), np.float32)
    Dn_full[0:32] = Dn
    C["dn"] = Dn_full
    return C


def pack_weights(inp):
    WT = np.zeros((2, TILES_PER_LAYER, 128, SLOT), np.float32)

    def kp(w):
        return w.reshape(-1, 128, w.shape[1]).transpose(1, 0, 2)

    for l in range(2):
        ti = 0

        def put(arr):
            nonlocal ti
            a = arr.reshape(128, -1)
            WT[l, ti, :, :a.shape[1]] = a
            ti += 1

        def ffn(up, down):
            upk = kp(up)
            for t in range(11):
                tile = np.stack([upk[:, :, t * 256:(t + 1) * 256],
                                 upk[:, :, 2816 + t * 256: 2816 + (t + 1) * 256]], axis=2)
                put(tile)
            dk = kp(down)
            for op in range(4):
                for h in range(2):
                    put(dk[:, h * 11:(h + 1) * 11, op * 256:(op + 1) * 256])

        ffn(inp["ffn1_up"][l], inp["ffn1_down"][l])
        win = kp(inp["w_in"][l])
        qcols = []
        for c in range(8):
            ha, hb = chunk_heads(c)
            qcols += list(range(ha * 64, ha * 64 + 64)) + list(range(hb * 64, hb * 64 + 64))
        wq = win[:, :, qcols]
        put(wq[:, :, 0:512]); put(wq[:, :, 512:1024])
        put(win[:, :, 1024:1536])
        for t in range(4):
            put(np.stack([win[:, :, 1536 + t * 256:1536 + (t + 1) * 256],
                          win[:, :, 2560 + t * 256:2560 + (t + 1) * 256]], axis=2))
        wa = inp["w_attn_out"][l]
        rows = []
        for kc in range(8):
            ha, hb = chunk_heads(kc)
            rows.append(np.concatenate([wa[ha * 64:ha * 64 + 64], wa[hb * 64:hb * 64 + 64]], axis=0))
        wap = np.stack(rows, axis=1)
        wc = kp(inp["w_conv_out"][l])
        for pr in range(4):
            put(np.stack([win[:, :, 3584 + pr * 256:3584 + (pr + 1) * 256], wap[:, :, pr * 256:(pr + 1) * 256]], axis=2))
            put(np.stack([win[:, :, 4608 + pr * 256:4608 + (pr + 1) * 256], wc[:, :, pr * 256:(pr + 1) * 256]], axis=2))
        wo = kp(inp["w_out"][l])
        put(wo[:, :, 0:512]); put(wo[:, :, 512:1024])
        ffn(inp["ffn2_up"][l], inp["ffn2_down"][l])
        assert ti == TILES_PER_LAYER, ti
    return WT


class Prog:
    ENG = ["pe", "act", "dve", "pool", "sp"]

    def __init__(self):
        self.ops = []
        self.rr = {}

    def add(self, eng, fn, r=(), w=(), dma=None, regions=(), wtile=None):
        if dma is not None and dma.endswith("*"):
            base = dma[:-1]
            n = self.rr.get(base, 0)
            self.rr[base] = n + 1
            dma = f"{base}_{n % 8}"
        self.ops.append(dict(eng=eng, fn=fn, r=tuple(r), w=tuple(w), dma=dma, regions=tuple(regions),
                             wtile=wtile))
        return len(self.ops) - 1

    def finalize(self):
        ops = self.ops
        last_reader = {}
        for i, op in enumerate(ops):
            for k in op["r"]:
                if k[0] == "wt":
                    last_reader[k[1]] = i
        loads = {}
        for i, op in enumerate(ops):
            if op["wtile"] is not None:
                loads.setdefault(op["wtile"], []).append(i)
        load_idx = set(i for v in loads.values() for i in v)
        after = {}
        head = []
        for n in sorted(loads):
            if n < NS:
                head += loads[n]
            else:
                after.setdefault(last_reader[n - NS], []).extend(loads[n])
        order = []
        first_nonload = True
        for i, op in enumerate(ops):
            if i in load_idx:
                continue
            if first_nonload:
                order += head
                first_nonload = False
            order.append(i)
            if i in after:
                order += after[i]
        assert len(order) == len(ops)
        self.order = order
        last_write = {}
        readers = {}
        region = {}
        pos = {}
        for p, i in enumerate(order):
            pos[i] = p
        deps_of = {}
        last_dma = {}
        for i in order:
            op = ops[i]
            deps = set()
            for k in op["r"]:
                if k in last_write:
                    deps.add(last_write[k])
                if k[0] == "ps":
                    for rr in readers.get(k, ()):
                        if ops[rr]["eng"] != op["eng"]:
                            deps.add(rr)
            for k in op["w"]:
                if k in last_write:
                    deps.add(last_write[k])
                deps.update(readers.get(k, ()))
            for (rn, ident) in op["regions"]:
                st = region.setdefault(rn, dict(ident=ident, users={}, barrier=set()))
                if st["ident"] != ident:
                    st["barrier"] = set(st["users"].values())
                    st["users"] = {}
                    st["ident"] = ident
                deps |= st["barrier"]
                ukey = op["dma"] if op["dma"] else op["eng"]
                st["users"][ukey] = i
            if op["dma"]:
                if op["dma"] in last_dma:
                    deps.add(last_dma[op["dma"]])
                last_dma[op["dma"]] = i
            deps.discard(i)
            for k in op["r"]:
                readers.setdefault(k, []).append(i)
            for k in op["w"]:
                last_write[k] = i
                readers[k] = []
            deps_of[i] = deps
        signal = set()
        for i in order:
            for d in deps_of[i]:
                signal.add(d)
        eng_count = {e: 0 for e in self.ENG}
        dma_count = {}
        sigval = {}
        for i in order:
            op = ops[i]
            if op["dma"]:
                dma_count[op["dma"]] = dma_count.get(op["dma"], 0) + 1
                sigval[i] = ("dma:" + op["dma"], 16 * dma_count[op["dma"]])
            elif i in signal:
                eng_count[op["eng"]] += 1
                sigval[i] = ("eng:" + op["eng"], eng_count[op["eng"]])
        self.dma_sems = sorted(dma_count)
        self.dma_total = {k: 16 * v for k, v in dma_count.items()}
        waited = {e: {} for e in self.ENG}
        for i in order:
            op = ops[i]
            e = op["eng"]
            need = {}
            for d in deps_of[i]:
                dop = ops[d]
                if (not dop["dma"]) and dop["eng"] == e and not op["dma"] and (e == "pe" or os.environ.get("KNOSELF", "0") == "1"):
                    continue
                sname, val = sigval[d]
                if need.get(sname, 0) < val:
                    need[sname] = val
            waits = []
            for sname, val in need.items():
                if waited[e].get(sname, 0) >= val:
                    continue
                waited[e][sname] = val
                waits.append((sname, val))
            op["waits"] = waits
            op["sig"] = sigval.get(i)
        self.maxcount = dict(eng_count)

    def emit(self, nc):
        import contextlib
        ops = self.ops
        sem_names = ["eng:" + e for e in self.ENG] + ["dma:" + s for s in self.dma_sems]
        with contextlib.ExitStack() as es:
            sems = {}
            for sn in sem_names:
                sems[sn] = es.enter_context(nc.semaphore(sn.replace(":", "_")))
            block = es.enter_context(nc.Block())

            def run(eng_name, E):
                for i in self.order:
                    op = ops[i]
                    if op["eng"] != eng_name:
                        continue
                    for (sname, val) in op["waits"]:
                        E.wait_ge(sems[sname], val)
                    ins = op["fn"](E)
                    if op["sig"] is not None:
                        sname, val = op["sig"]
                        ins.then_inc(sems[sname], 16 if op["dma"] else 1)
                if eng_name == "sp":
                    for s in self.dma_sems:
                        if s.startswith("out"):
                            E.wait_ge(sems["dma:" + s], self.dma_total[s])

            @block.tensor
            def _(e):
                run("pe", e)

            @block.scalar
            def _(e):
                run("act", e)

            @block.vector
            def _(e):
                run("dve", e)

            @block.gpsimd
            def _(e):
                run("pool", e)

            @block.sync
            def _(e):
                run("sp", e)


def build_program():
    import contextlib
    nc = bass.Bass("TRN2", target_bir_lowering=False)
    es = contextlib.ExitStack()
    P = Prog()

    def din(name, shape):
        return nc.dram_tensor(name, list(shape), F32, kind="ExternalInput").ap()

    def dout(name, shape):
        return nc.dram_tensor(name, list(shape), F32, kind="ExternalOutput").ap()

    xp = din("xp", [2048, D]); xs = din("xs", [64, D])
    ck = din("ck", [2, 16, 128, 256]); cv = din("cv", [2, 16, 128, 256])
    scv = din("scv", [2, 16, 30, D])
    wt = din("wt", [2, TILES_PER_LAYER, 128, SLOT])
    par = din("par", [128, NPAR])
    c_ident = din("c_ident", [128, 128]); c_dp = din("c_dp", [128, 256])
    c_dc = din("c_dc", [128, 4]); c_dn = din("c_dn", [128, 32])
    yp = dout("yp", [2048, D]); ys = dout("ys", [64, D])
    nkp = dout("nkp", [2, 128, 256]); nvp = dout("nvp", [2, 128, 256]); ncp = dout("ncp", [2, 30, D])
    nks = dout("nks", [2, 16, 128, 256]); nvs = dout("nvs", [2, 16, 128, 256]); ncs = dout("ncs", [2, 16, 30, D])
    dbg = {}
    if DEBUG:
        for nm in ["d_x1", "d_q", "d_attn", "d_cs", "d_x2", "d_x3"]:
            dbg[nm] = dout(nm, [128, 8, T])

    def sb(name, shape, dt):
        return es.enter_context(nc.sbuf_tensor(name, list(shape), dt))

    x32 = sb("x32", [128, 8, T], F32)
    xb = sb("xb", [128, 8, T], BF16)
    ring = sb("ring", [128, NS, SLOT], BF16)
    ARENA_B = 67584
    arena = sb("arena", [128, ARENA_B // 2], BF16)
    SCR_B = 20480
    scr = sb("scr", [128, SCR_B // 2], BF16)
    prm = sb("prm", [128, NPAR], F32)
    identf = sb("identf", [128, 128], F32)
    identb = sb("identb", [128, 128], BF16)
    onesb = sb("onesb", [128, 128], BF16)
    eps1 = sb("eps1", [128, 1], F32)
    eps4 = sb("eps4", [128, 1], F32)
    dp = sb("dp", [128, 2, 128], F32)
    dc = sb("dc", [128, 4], F32)
    dn = sb("dn", [128, 32], F32)
    nsink = sb("nsink", [128, 2, 16], F32)
    bq8 = sb("bq8", [128, 2, 8], F32)
    esT = sb("esT", [128, 2, 8], F32)
    uextb = sb("uextb", [128, 2, 30 + TP], BF16)
    dg = sb("dg", [128, 31, 128], BF16)
    u32tail = sb("u32tail", [128, 8, 30], F32)
    cacc = sb("cacc", [128, TS], F32)
    sstage = sb("sstage", [128, D], F32)
    uexts = sb("uexts", [128, 8, NSEQ, 34], F32)
    ucarry = sb("ucarry", [128, 2, 8, 30], BF16)
    kcarry = sb("kcarry", [128, 2, 2, 128], BF16)
    vcarry = sb("vcarry", [128, 2, 256], BF16)
    psum = es.enter_context(nc.psum_tensor("psum", [128, 8, 512], F32))

    def av(off_b, shape, dt):
        n = int(np.prod(shape[1:]))
        if dt == BF16:
            v = arena[:, off_b // 2: off_b // 2 + n]
        else:
            v = arena[:, off_b // 2: off_b // 2 + 2 * n].bitcast(F32)
        names = "abcdef"[:len(shape) - 1]
        if len(shape) > 2:
            kw = {names[i]: shape[i + 1] for i in range(len(shape) - 1)}
            v = v.rearrange("p (%s) -> p %s" % (" ".join(names), " ".join(names)), **kw)
        return v

    def sv(off_b, shape, dt):
        n = int(np.prod(shape[1:]))
        if dt == BF16:
            v = scr[:, off_b // 2: off_b // 2 + n]
        else:
            v = scr[:, off_b // 2: off_b // 2 + 2 * n].bitcast(F32)
        names = "abcdef"[:len(shape) - 1]
        if len(shape) > 2:
            kw = {names[i]: shape[i + 1] for i in range(len(shape) - 1)}
            v = v.rearrange("p (%s) -> p %s" % (" ".join(names), " ".join(names)), **kw)
        return v

    gT = av(0, [128, NJ, T], BF16)
    qT = av(0, [128, 8, T], BF16)
    mixT = qT
    kT = av(16896, [128, 2, 128 + T], BF16)
    vtok = av(21632, [128, 9, 256], BF16)
    cT = av(0, [128, 8, T], F32)
    attnT = av(33792, [128, 8, T], BF16)
    csT = av(50688, [128, 8, T], BF16)
    RA_FFN = ("arena", "ffn")
    RA_ATT = ("arena", "att")
    RQ = ("cat", "qkv")
    RC = ("cat", "c")
    RM = ("cat", "mix")

    def pcol(name, l, k=None):
        o, w = PL.cols[f"{name}{l}"]
        if k is None:
            return prm[:, o:o + w]
        return prm[:, o + k:o + k + 1]

    def PS(b, n=512, p0=0, p1=128):
        return psum[p0:p1, b, 0:n]

    wstate = dict(n=0)

    def wtile(l, idx, length):
        n = wstate["n"]
        wstate["n"] += 1
        s = n % NS
        L = length
        src = wt[l, idx, :, 0:L].rearrange("p (a b) -> p a b", b=256)
        dst = ring[:, s, 0:L].rearrange("p (a b) -> p a b", b=256)
        P.add("pool", lambda E, dst=dst, src=src: E.dma_start(out=dst, in_=src),
              r=(), w=(("wslot", s), ("wt", n)), dma=f"w{s}", wtile=n)
        return ring[:, s, :], (("wt", n), ("wslot", s))

    def setup():
        P.add("sp", lambda E: E.dma_start(out=prm[:], in_=par), w=(("prm",),), dma="ld0")
        P.add("sp", lambda E: E.dma_start(out=identf[:], in_=c_ident), w=(("identf",),), dma="ld1")
        P.add("sp", lambda E: E.dma_start(out=dp[:].rearrange("p a b -> p (a b)"), in_=c_dp), w=(("dp",),), dma="ld2")
        P.add("sp", lambda E: E.dma_start(out=dc[:], in_=c_dc), w=(("dc",),), dma="ld3")
        P.add("sp", lambda E: E.dma_start(out=dn[:], in_=c_dn), w=(("dn",),), dma="ld4")
        P.add("dve", lambda E: E.tensor_copy(out=identb[:], in_=identf[:]), r=(("identf",),), w=(("identb",),))
        P.add("dve", lambda E: E.memset(onesb[:], 1.0), w=(("onesb",),))
        P.add("dve", lambda E: E.memset(eps1[:], LN_EPS), w=(("epsT",),))
        P.add("dve", lambda E: E.memset(eps4[:], 4.0 * LN_EPS), w=(("epsT",),))
        for l in range(2):
            P.add("dve", lambda E, l=l: E.tensor_scalar(out=nsink[:, l, :], in0=pcol("sink_b", l), scalar1=-1.0,
                                                         scalar2=None, op0=ALU.mult),
                  r=(("prm",),), w=(("nsink", l),))
            P.add("act", lambda E, l=l: E.activation(out=esT[:, l, :], in_=pcol("sinkp", l), func=AF.Exp),
                  r=(("prm",),), w=(("esT", l),))
            P.add("dve", lambda E, l=l: E.tensor_scalar(out=bq8[:, l, :], in0=pcol("bq", l), scalar1=0.125,
                                                         scalar2=None, op0=ALU.mult),
                  r=(("prm",),), w=(("bq8", l),))
        import os
        for l in range(0 if os.environ.get('KNOCOPY', '0') != '1' else 2, 2):
            P.add("sp", lambda E, l=l: E.dma_start(out=nks[l, :, 0:124, :], in_=ck[l, :, 4:128, :]), dma=f"out0a{l}")
            P.add("sp", lambda E, l=l: E.dma_start(out=nvs[l, :, 0:124, :], in_=cv[l, :, 4:128, :]), dma=f"out0b{l}")
            P.add("sp", lambda E, l=l: E.dma_start(out=ncs[l, :, 0:26, :], in_=scv[l, :, 4:30, :]), dma=f"out0c{l}")

    def load_x(st):
        R = ("scr", "xio")
        stage = sv(0, [128, 4, D], F32)
        import os
        for tb in range(int(os.environ.get("KNTB", "9"))):
            buf = tb % 4
            if tb < 8:
                rows = 128
                src = xp[st * TP + tb * 128: st * TP + (tb + 1) * 128, :]
                col0 = tb * 128
            else:
                rows = 32
                src = xs[st * TS:(st + 1) * TS, :]
                col0 = TP
            P.add("sp", lambda E, src=src, buf=buf, rows=rows: E.dma_start(out=stage[0:rows, buf, :], in_=src),
                  w=(("stage", buf, 0), ("stage", buf, 1)), dma=f"ldx{buf}", regions=(R,))
            banks = (2 * (tb % 4), 2 * (tb % 4) + 1)

            def tr(E, buf=buf, rows=rows, banks=banks):
                ins = None
                for k in range(8):
                    b = banks[k // 4]
                    ins = E.transpose(psum[:, b, (k % 4) * 128:(k % 4) * 128 + rows],
                                      stage[0:rows, buf, k * 128:(k + 1) * 128], identf[0:rows, 0:rows])
                return ins
            P.add("pe", tr, r=(("stage", buf, 0), ("stage", buf, 1), ("identf",)), w=(("ps", banks[0]), ("ps", banks[1])), regions=(R,))
            for hb in range(2):
                b = banks[hb]
                src_ps = psum[:, b, :].rearrange("p (k t) -> p k t", t=128)[:, :, 0:rows]
                wx = tuple(("x32", k, u) for k in range(hb * 4, hb * 4 + 4) for u in range(3))
                wb = tuple(("xb", k, u) for k in range(hb * 4, hb * 4 + 4) for u in range(3))
                if hb == 0:
                    P.add("act", lambda E, src_ps=src_ps, hb=hb, col0=col0, rows=rows:
                          E.activation(out=x32[:, hb * 4:(hb + 1) * 4, col0:col0 + rows], in_=src_ps, func=AF.Identity),
                          r=(("ps", b),), w=wx)
                    P.add("act", lambda E, src_ps=src_ps, hb=hb, col0=col0, rows=rows:
                          E.activation(out=xb[:, hb * 4:(hb + 1) * 4, col0:col0 + rows], in_=src_ps, func=AF.Identity),
                          r=(("ps", b),), w=wb)
                else:
                    P.add("dve", lambda E, src_ps=src_ps, hb=hb, col0=col0, rows=rows:
                          E.tensor_copy(out=x32[:, hb * 4:(hb + 1) * 4, col0:col0 + rows], in_=src_ps),
                          r=(("ps", b),), w=wx)
                    P.add("dve", lambda E, src_ps=src_ps, hb=hb, col0=col0, rows=rows:
                          E.tensor_copy(out=xb[:, hb * 4:(hb + 1) * 4, col0:col0 + rows], in_=src_ps),
                          r=(("ps", b),), w=wb)

    def store_y(st):
        R = ("scr", "xio")
        stage = sv(0, [128, 4, D], F32)
        for tb in range(9):
            buf = tb % 4
            if tb < 8:
                rows = 128
                dst = yp[st * TP + tb * 128: st * TP + (tb + 1) * 128, :]
                col0 = tb * 128
            else:
                rows = 32
                dst = ys[st * TS:(st + 1) * TS, :]
                col0 = TP
            banks = (2 * (tb % 4), 2 * (tb % 4) + 1)

            def tr(E, rows=rows, banks=banks, col0=col0):
                ins = None
                for k in range(8):
                    b = banks[k // 4]
                    ins = E.transpose(psum[0:rows, b, (k % 4) * 128:(k % 4 + 1) * 128],
                                      x32[:, k, col0:col0 + rows], identf[:, :])
                return ins
            P.add("pe", tr, r=tuple(("x32", k, u) for k in range(8) for u in range(3)) + (("identf",),),
                  w=(("ps", banks[0]), ("ps", banks[1])))
            for hb in range(2):
                b = banks[hb]
                eng = "act" if hb == 0 else "dve"
                if eng == "act":
                    P.add("act", lambda E, b=b, hb=hb, buf=buf, rows=rows:
                          E.activation(out=stage[0:rows, buf, hb * 512:(hb + 1) * 512], in_=psum[0:rows, b, :], func=AF.Identity),
                          r=(("ps", b),), w=(("stage", buf, hb),), regions=(R,))
                else:
                    P.add("dve", lambda E, b=b, hb=hb, buf=buf, rows=rows:
                          E.tensor_copy(out=stage[0:rows, buf, hb * 512:(hb + 1) * 512], in_=psum[0:rows, b, :]),
                          r=(("ps", b),), w=(("stage", buf, hb),), regions=(R,))
            P.add("sp", lambda E, dst=dst, buf=buf, rows=rows: E.dma_start(out=dst, in_=stage[0:rows, buf, :]),
                  r=(("stage", buf, 0), ("stage", buf, 1)), dma="out1*", regions=(R,))

    LNR = ("scr", "ln")
    ln_zb = sv(0, [128, 4, 512], BF16)
    ln_sq = sv(4096, [128, 4, 512], BF16)
    ln_zs2 = sv(8192, [128, 2, 64], BF16)
    ln_mt = sv(8704, [128, 512], F32)
    ln_vt = sv(10752, [128, 512], F32)
    ln_bt = sv(12800, [128, 512], F32)
    ln_cnt = dict(n=0)
    LN_BANKS = {0: (3, 4), 1: (5, 6)}

    def ln_stats_chunk(src, srckey, k, mode, units=(0, 1, 2)):
        R = LNR
        XR = (RA_ATT, RC) if mode == "cs" else ()
        for u, (off, n) in enumerate(UNITS):
            if u not in units:
                continue
            if u < 2:
                rb = ln_cnt["n"] % 4
                ln_cnt["n"] += 1
                b1, b2 = LN_BANKS[u]
                P.add("act", lambda E, k=k, rb=rb, off=off, n=n:
                      E.activation(out=ln_sq[:, rb, 0:n], in_=src[:, k, off:off + n], func=AF.Square),
                      r=((srckey, k, u),), w=(("lnsq", rb),), regions=(R,) + XR)
                P.add("dve", lambda E, k=k, rb=rb, off=off, n=n:
                      E.tensor_copy(out=ln_zb[:, rb, 0:n], in_=src[:, k, off:off + n]),
                      r=((srckey, k, u),), w=(("lnzb", rb),), regions=(R,) + XR)

                def mm(E, k=k, rb=rb, n=n, b1=b1, b2=b2):
                    E.matmul(PS(b1, n), lhsT=onesb[:, :], rhs=ln_zb[:, rb, 0:n], start=(k == 0), stop=(k == 7))
                    return E.matmul(PS(b2, n), lhsT=onesb[:, :], rhs=ln_sq[:, rb, 0:n], start=(k == 0), stop=(k == 7))
                P.add("pe", mm, r=(("lnzb", rb), ("lnsq", rb), ("onesb",)), w=(("ps", b1), ("ps", b2)), regions=(R,))
            else:
                rb = k % 2
                P.add("act", lambda E, k=k, rb=rb, off=off, n=n:
                      E.activation(out=ln_zs2[:, rb, 32:64], in_=src[:, k, off:off + n], func=AF.Square),
                      r=((srckey, k, u),), w=(("lnzs2q", rb),), regions=(R,) + XR)
                P.add("dve", lambda E, k=k, rb=rb, off=off, n=n:
                      E.tensor_copy(out=ln_zs2[:, rb, 0:32], in_=src[:, k, off:off + n]),
                      r=((srckey, k, u),), w=(("lnzs2z", rb),), regions=(R,) + XR)
                P.add("pe", lambda E, k=k, rb=rb:
                      E.matmul(psum[:, 7, 0:64], lhsT=onesb[:, :], rhs=ln_zs2[:, rb, :], start=(k == 0), stop=(k == 7)),
                      r=(("lnzs2q", rb), ("lnzs2z", rb), ("onesb",)), w=(("ps", 7),), regions=(R,))

    def ln_finish(src, srckey, eps, gname, bname, l, mode, units=(0, 1, 2)):
        R = LNR
        XR = (RA_ATT, RC) if mode == "cs" else ()
        epsT = eps1 if abs(eps - LN_EPS) < 1e-12 else eps4
        mt, vt, bt = ln_mt, ln_vt, ln_bt
        for u, (off, n) in enumerate(UNITS):
            if u not in units:
                continue
            if u < 2:
                b1, b2 = LN_BANKS[u]
                s1, s2 = PS(b1, n), PS(b2, n)
            else:
                b1 = b2 = 7
                s1, s2 = psum[:, 7, 0:32], psum[:, 7, 32:64]
            P.add("dve", lambda E, n=n, s1=s1: E.tensor_scalar(out=mt[:, 0:n], in0=s1, scalar1=1.0 / D, scalar2=None, op0=ALU.mult),
                  r=(("ps", b1),), w=(("lnm",),), regions=(R,))
            P.add("dve", lambda E, n=n: E.tensor_tensor(out=bt[:, 0:n], in0=mt[:, 0:n], in1=mt[:, 0:n], op=ALU.mult),
                  r=(("lnm",),), w=(("lnb",),), regions=(R,))
            P.add("dve", lambda E, n=n, s2=s2: E.scalar_tensor_tensor(out=vt[:, 0:n], in0=s2, scalar=1.0 / D, in1=bt[:, 0:n],
                                                                      op0=ALU.mult, op1=ALU.subtract),
                  r=(("ps", b2), ("lnb",)), w=(("lnv",),), regions=(R,))
            P.add("act", lambda E, n=n: E.activation(out=bt[:, 0:n], in_=vt[:, 0:n], func=AF.Ln, bias=epsT[:, 0:1]),
                  r=(("lnv",), ("epsT",)), w=(("lnb",),), regions=(R,))
            P.add("act", lambda E, n=n: E.activation(out=vt[:, 0:n], in_=bt[:, 0:n], func=AF.Exp, scale=-0.5),
                  r=(("lnb",),), w=(("lnv",),), regions=(R,))
            for k in range(8):
                P.add("dve", lambda E, k=k, off=off, n=n:
                      E.tensor_tensor(out=src[:, k, off:off + n], in0=src[:, k, off:off + n], in1=mt[:, 0:n], op=ALU.subtract),
                      r=((srckey, k, u), ("lnm",)), w=((srckey, k, u),), regions=(R,) + XR)
                P.add("dve", lambda E, k=k, off=off, n=n:
                      E.tensor_tensor(out=src[:, k, off:off + n], in0=src[:, k, off:off + n], in1=vt[:, 0:n], op=ALU.mult),
                      r=((srckey, k, u), ("lnv",)), w=((srckey, k, u),), regions=(R,) + XR)
                if mode == "x":
                    P.add("act", lambda E, k=k, off=off, n=n:
                          E.activation(out=xb[:, k, off:off + n], in_=src[:, k, off:off + n], func=AF.Identity,
                                       scale=pcol(gname, l, k), bias=pcol(bname, l, k)),
                          r=((srckey, k, u), ("prm",)), w=(("xb", k, u),))
                    P.add("act", lambda E, k=k, off=off, n=n:
                          E.activation(out=x32[:, k, off:off + n], in_=src[:, k, off:off + n], func=AF.Identity,
                                       scale=pcol(gname, l, k), bias=pcol(bname, l, k)),
                          r=((srckey, k, u), ("prm",)), w=(("x32", k, u),))
                else:
                    P.add("act", lambda E, k=k, off=off, n=n:
                          E.activation(out=csT[:, k, off:off + n], in_=src[:, k, off:off + n], func=AF.Silu,
                                       scale=pcol(gname, l, k), bias=pcol(bname, l, k)),
                          r=((srckey, k, u), ("prm",)), w=(("cs", k, u),), regions=(RA_ATT, RC))

    def layer_norm(src, srckey, eps, gname, bname, l, mode):
        for k in range(8):
            ln_stats_chunk(src, srckey, k, mode)
        ln_finish(src, srckey, eps, gname, bname, l, mode)

    def dense_ui(lhs_fn, rhs_t, rhs_key, wkeys, banks, regions=()):
        def mm(E):
            ins = None
            for k in range(8):
                lt = lhs_fn(k)
                for u, (off, n) in enumerate(UNITS):
                    ins = E.matmul(PS(banks[u], n), lhsT=lt, rhs=rhs_t[:, k, off:off + n], start=(k == 0), stop=(k == 7))
            return ins
        P.add("pe", mm, r=tuple(wkeys) + tuple((rhs_key, k, u) for k in range(8) for u in range(3)),
              w=tuple(("ps", banks[u]) for u in range(3)), regions=regions)

    def ffn(l, which, tile0):
        R = ("scr", "ffn")
        sa = sv(0, [128, 4, 512], BF16)
        cnt = 0
        for t in range(11):
            wv, wk = wtile(l, tile0 + t, 4096)
            w4 = wv.rearrange("p (k s c) -> p k s c", k=8, s=2, c=256)
            if t >= 1:
                for jj in range(2):
                    j = 2 * t + jj
                    dense_ui(lambda k, w4=w4, jj=jj: w4[:, k, 0, jj * 128:(jj + 1) * 128], xb, "xb", wk, (0, 1, 2))
                    dense_ui(lambda k, w4=w4, jj=jj: w4[:, k, 1, jj * 128:(jj + 1) * 128], xb, "xb", wk, (3, 4, 5))
                    for u, (off, n) in enumerate(UNITS):
                        rb = cnt % 4
                        cnt += 1
                        P.add("act", lambda E, u=u, rb=rb, n=n: E.activation(out=sa[:, rb, 0:n], in_=PS(u, n), func=AF.Silu),
                              r=(("ps", u),), w=(("sa", rb),), regions=(R,))
                        P.add("dve", lambda E, u=u, rb=rb, n=n, j=j, off=off:
                              E.tensor_tensor(out=gT[:, j, off:off + n], in0=PS(3 + u, n), in1=sa[:, rb, 0:n], op=ALU.mult),
                              r=(("ps", 3 + u), ("sa", rb)), w=(("g", j, u),), regions=(R, RA_FFN))
                continue
            for u, (off, n) in enumerate(UNITS):
                for jj in range(2):
                    j = 2 * t + jj
                    pb = (cnt % 3) * 2
                    rb = cnt % 4
                    cnt += 1

                    def mm(E, w4=w4, jj=jj, off=off, n=n, pb=pb):
                        ins = None
                        for s in range(2):
                            for k in range(8):
                                ins = E.matmul(PS(pb + s, n), lhsT=w4[:, k, s, jj * 128:(jj + 1) * 128],
                                               rhs=xb[:, k, off:off + n], start=(k == 0), stop=(k == 7))
                        return ins
                    P.add("pe", mm, r=wk + tuple(("xb", k, u) for k in range(8)), w=(("ps", pb), ("ps", pb + 1)))
                    P.add("act", lambda E, pb=pb, rb=rb, n=n: E.activation(out=sa[:, rb, 0:n], in_=PS(pb, n), func=AF.Silu),
                          r=(("ps", pb),), w=(("sa", rb),), regions=(R,))
                    P.add("dve", lambda E, pb=pb, rb=rb, n=n, j=j, off=off:
                          E.tensor_tensor(out=gT[:, j, off:off + n], in0=PS(pb + 1, n), in1=sa[:, rb, 0:n], op=ALU.mult),
                          r=(("ps", pb + 1), ("sa", rb)), w=(("g", j, u),), regions=(R, RA_FFN))
        ti = tile0 + 11
        for op_ in range(4):
            wv0, wk0 = wtile(l, ti, 2816); ti += 1
            wv1, wk1 = wtile(l, ti, 2816); ti += 1
            wh = [wv0[:, 0:2816].rearrange("p (j c) -> p j c", c=256), wv1[:, 0:2816].rearrange("p (j c) -> p j c", c=256)]
            for o2 in range(2):
                oc = op_ * 2 + o2
                base = 0

                if oc < 7:
                    def mm(E, wh=wh, o2=o2, base=base):
                        ins = None
                        for j in range(NJ):
                            h, jl = j // 11, j % 11
                            for u, (off, n) in enumerate(UNITS):
                                ins = E.matmul(PS(base + u, n), lhsT=wh[h][:, jl, o2 * 128:(o2 + 1) * 128],
                                               rhs=gT[:, j, off:off + n], start=(j == 0), stop=(j == NJ - 1))
                        return ins
                    P.add("pe", mm, r=wk0 + wk1 + tuple(("g", j, u) for j in range(NJ) for u in range(3)),
                          w=tuple(("ps", base + u) for u in range(3)), regions=(RA_FFN,))
                else:
                    gn, bn = ("ln1_g", "ln1_b") if which == 1 else ("ln3_g", "ln3_b")

                    def mm_u(u, wh=wh, o2=o2, base=base):
                        off, n = UNITS[u]

                        def mm(E):
                            ins = None
                            for j in range(NJ):
                                h, jl = j // 11, j % 11
                                ins = E.matmul(PS(base + u, n), lhsT=wh[h][:, jl, o2 * 128:(o2 + 1) * 128],
                                               rhs=gT[:, j, off:off + n], start=(j == 0), stop=(j == NJ - 1))
                            return ins
                        P.add("pe", mm, r=wk0 + wk1 + tuple(("g", j, u) for j in range(NJ)), w=(("ps", base + u),), regions=(RA_FFN,))

                    def ev_u(u, oc=oc, base=base):
                        off, n = UNITS[u]
                        P.add("dve", lambda E: E.scalar_tensor_tensor(out=x32[:, oc, off:off + n], in0=x32[:, oc, off:off + n], scalar=2.0 * ALPHA,
                                                                      in1=PS(base + u, n), op0=ALU.mult, op1=ALU.add),
                              r=(("ps", base + u), ("x32", oc, u)), w=(("x32", oc, u),))
                    fa = (x32, "x32", 4.0 * LN_EPS, gn, bn, l, "x")
                    mm_u(0); ln_stats_chunk(x32, "x32", 6, "x"); ev_u(0)
                    mm_u(1); ln_stats_chunk(x32, "x32", 7, "x", units=(0,)); ev_u(1)
                    mm_u(2); ln_stats_chunk(x32, "x32", 7, "x", units=(1,)); ev_u(2)
                    ln_stats_chunk(x32, "x32", 7, "x", units=(2,)); ln_finish(*fa)
                    continue
                if oc > 0:
                    ln_stats_chunk(x32, "x32", oc - 1, "x")
                for u, (off, n) in enumerate(UNITS):
                    P.add("dve", lambda E, oc=oc, off=off, n=n, base=base, u=u:
                          E.scalar_tensor_tensor(out=x32[:, oc, off:off + n], in0=x32[:, oc, off:off + n], scalar=2.0 * ALPHA,
                                                 in1=PS(base + u, n), op0=ALU.mult, op1=ALU.add),
                          r=(("ps", base + u), ("x32", oc, u)), w=(("x32", oc, u),))

    def state_dma(st, l, grp):
        P.add("sp", lambda E: E.dma_start(out=sstage[0:120, :], in_=scv[l, st * NSEQ + grp * 4: st * NSEQ + grp * 4 + 4].rearrange("b r c -> (b r) c")),
              w=(("sstage",),), dma="lds2")

    def qkv_attention(st, l, tile0):
        R = ("scr", "att")
        state_dma(st, l, 0)
        if st == 0:
            pass
        else:
            P.add("pool", lambda E: E.tensor_copy(out=kT[:, :, 0:128], in_=kcarry[:, l, :, :]),
                  r=(("kcarry", l),), w=(("kprev",),), regions=(RA_ATT, RQ))
            P.add("pool", lambda E: E.tensor_copy(out=vtok[:, 0, :], in_=vcarry[:, l, :]),
                  r=(("vcarry", l),), w=(("v", 0),), regions=(RA_ATT, RQ))
        cnt = 0
        for tq in range(2):
            wv, wk = wtile(l, tile0 + tq, 4096)
            w3 = wv.rearrange("p (k c) -> p k c", k=8)
            if tq == 1:
                for cl in range(4):
                    c = tq * 4 + cl
                    bk = tuple((cl % 2) * 3 + u for u in range(3))
                    dense_ui(lambda k, w3=w3, cl=cl: w3[:, k, cl * 128:(cl + 1) * 128], xb, "xb", wk, bk)
                    for u, (off, n) in enumerate(UNITS):
                        P.add("act", lambda E, c=c, off=off, n=n, b=bk[u]:
                              E.activation(out=qT[:, c, off:off + n], in_=PS(b, n), func=AF.Identity, scale=0.125, bias=bq8[:, l, c:c + 1]),
                              r=(("ps", bk[u]), ("bq8", l)), w=(("q", c, u),), regions=(RA_ATT, RQ))
                continue
            for u, (off, n) in enumerate(UNITS):
                for cl in range(4):
                    c = tq * 4 + cl
                    pb = cnt % 4
                    cnt += 1

                    def mm(E, w3=w3, cl=cl, off=off, n=n, pb=pb):
                        ins = None
                        for k in range(8):
                            ins = E.matmul(PS(pb, n), lhsT=w3[:, k, cl * 128:(cl + 1) * 128], rhs=xb[:, k, off:off + n],
                                           start=(k == 0), stop=(k == 7))
                        return ins
                    P.add("pe", mm, r=wk + tuple(("xb", k, u) for k in range(8)), w=(("ps", pb),))
                    P.add("act", lambda E, c=c, off=off, n=n, pb=pb:
                          E.activation(out=qT[:, c, off:off + n], in_=PS(pb, n), func=AF.Identity, scale=0.125, bias=bq8[:, l, c:c + 1]),
                          r=(("ps", pb), ("bq8", l)), w=(("q", c, u),), regions=(RA_ATT, RQ))
        wv, wk = wtile(l, tile0 + 2, 4096)
        w3 = wv.rearrange("p (k c) -> p k c", k=8)
        for g2 in range(2):
            bk = tuple((g2 % 2) * 3 + u for u in range(3))
            dense_ui(lambda k, g2=g2: w3[:, k, g2 * 128:(g2 + 1) * 128], xb, "xb", wk, bk)
            for u, (off, n) in enumerate(UNITS):
                P.add("act", lambda E, g2=g2, off=off, n=n, b=bk[u]:
                      E.activation(out=kT[:, g2, 128 + off:128 + off + n], in_=PS(b, n), func=AF.Identity,
                                   bias=pcol("bk", l, g2)),
                      r=(("ps", bk[u]), ("prm",)), w=(("k", g2, u),), regions=(RA_ATT, RQ))
        ostage = sv(0, [128, 2, 512], F32)
        o_bkv, _ = PL.cols[f"bkv_b{l}"]
        for tb in range(9):
            rows = 128 if tb < 8 else TS
            col0 = tb * 128 if tb < 8 else TP
            need_k = (tb == 8) or (st == 1 and tb == 7)
            pb = 4 + (tb % 2)
            ncol = 512 if need_k else 256
            c0 = 0 if need_k else 256

            def mm(E, col0=col0, rows=rows, pb=pb, c0=c0, ncol=ncol):
                ins = None
                for k in range(8):
                    ins = E.matmul(psum[0:rows, pb, c0:c0 + ncol], lhsT=xb[:, k, col0:col0 + rows], rhs=w3[:, k, c0:c0 + ncol],
                                   start=(k == 0), stop=(k == 7))
                return ins
            P.add("pe", mm, r=wk + tuple(("xb", k, u) for k in range(8) for u in range(3)), w=(("ps", pb),))
            vdst = vtok[:, 1 + tb, :] if tb < 8 else sb_vn[:, :]
            vkey = ("v", 1 + tb) if tb < 8 else ("vn2",)
            P.add("dve", lambda E, rows=rows, pb=pb, vdst=vdst:
                  E.tensor_tensor(out=vdst[0:rows], in0=psum[0:rows, pb, 256:512], in1=prm[0:rows, o_bkv + 256:o_bkv + 512], op=ALU.add),
                  r=(("ps", pb), ("prm",)), w=(vkey,), regions=(RA_ATT, R, RQ))
            if need_k:
                ob = tb % 2
                P.add("dve", lambda E, rows=rows, pb=pb, ob=ob:
                      E.tensor_tensor(out=ostage[0:rows, ob, :], in0=psum[0:rows, pb, 0:512], in1=prm[0:rows, o_bkv:o_bkv + 512], op=ALU.add),
                      r=(("ps", pb), ("prm",)), w=(("ostage", ob),), regions=(R,))
                if tb == 7:
                    P.add("sp", lambda E, ob=ob: E.dma_start(out=nkp[l], in_=ostage[:, ob, 0:256]), r=(("ostage", ob),), dma="out1*", regions=(R,))
                    P.add("sp", lambda E, ob=ob: E.dma_start(out=nvp[l], in_=ostage[:, ob, 256:512]), r=(("ostage", ob),), dma="out1*", regions=(R,))
                else:
                    for (dst_t, c_0) in ((nks, 0), (nvs, 256)):
                        for b in range(NSEQ):
                            P.add("sp", lambda E, ob=ob, dst_t=dst_t, c_0=c_0, b=b:
                                  E.dma_start(out=dst_t[l, st * NSEQ + b, 124:128, :], in_=ostage[4 * b:4 * b + 4, ob, c_0:c_0 + 256]),
                                  r=(("ostage", ob),), dma="out1*", regions=(R,))
        if st == 0:
            P.add("pool", lambda E: E.tensor_copy(out=kcarry[:, l, :, :], in_=kT[:, :, TP:TP + 128]),
                  r=tuple(("k", g2, 1) for g2 in range(2)), w=(("kcarry", l),), regions=(RA_ATT, RQ))
            P.add("pool", lambda E: E.tensor_copy(out=vcarry[:, l, :], in_=vtok[:, 8, :]),
                  r=(("v", 8),), w=(("vcarry", l),), regions=(RA_ATT, RQ))
        d1 = sv(0, [128, 2, 512], F32)
        etab = sv(4096, [128, 16, 2, 128], BF16)
        ptb = sv(12288, [128, 2, 2, 512], BF16)
        praw = sv(16384, [128, 2, 2, 512], BF16)
        for h in range(16):
            P.add("act", lambda E, h=h: E.activation(out=etab[:, h, :, :].rearrange("p a b -> p (a b)"),
                                                     in_=dp[:, :, :].rearrange("p a b -> p (a b)"), func=AF.Exp, scale=float(SLOPES[h])),
                  r=(("dp",),), w=(("etab", h),), regions=(R,))
        iters = [(qb, g) for qb in range(8) for g in range(4)]

        def blks_of(qb):
            return (0, 1) if (st * 8 + qb) > 0 else (1,)

        def emit_scores(n):
            qb, g = iters[n]
            g2, g1 = g // 2, g % 2
            sb0 = (n % 2) * 2
            blks = blks_of(qb)

            def mm(E):
                ins = None
                for blk in blks:
                    kcol = qb * 128 + blk * 128
                    ins = E.matmul(PS(sb0 + blk), lhsT=kT[g1 * 64:(g1 + 1) * 64, g2, kcol:kcol + 128],
                                   rhs=qT[g1 * 64:(g1 + 1) * 64, g2 * 4:(g2 + 1) * 4, qb * 128:(qb + 1) * 128],
                                   start=True, stop=True)
                return ins
            ku = 0 if qb < 4 else 1
            kreads = [("k", g2, ku)]
            if len(blks) == 2:
                kreads.append(("kprev",) if qb == 0 else ("k", g2, 0 if (qb - 1) < 4 else 1))
            P.add("pe", mm, r=tuple(kreads) + tuple(("q", g2 * 4 + i, ku) for i in range(4)),
                  w=tuple(("ps", sb0 + blk) for blk in blks), regions=(RA_ATT, RQ))

        def emit_softmax(n):
            qb, g = iters[n]
            buf = n % 2
            sb0 = buf * 2
            for blk in blks_of(qb):
                P.add("act", lambda E, buf=buf, blk=blk, sb0=sb0:
                      E.activation(out=praw[:, buf, blk, :], in_=PS(sb0 + blk), func=AF.Exp),
                      r=(("ps", sb0 + blk),), w=(("praw", buf, blk),), regions=(R,))
                P.add("dve", lambda E, buf=buf, blk=blk, g=g:
                      E.tensor_tensor(out=ptb[:, buf, blk, :].rearrange("p (i q) -> p i q", q=128),
                                      in0=praw[:, buf, blk, :].rearrange("p (i q) -> p i q", q=128),
                                      in1=etab[:, g * 4:(g + 1) * 4, blk, :], op=ALU.mult),
                      r=(("praw", buf, blk),) + tuple(("etab", g * 4 + i) for i in range(4)), w=(("pt", buf, blk),), regions=(R,))

        def emit_pv(n):
            qb, g = iters[n]
            g2, g1 = g // 2, g % 2
            buf = n % 2
            blks = blks_of(qb)

            def pv(E):
                ins = None
                for bi, blk in enumerate(blks):
                    ins = E.matmul(psum[g1 * 64:(g1 + 1) * 64, 4 + g2, :], lhsT=vtok[:, qb + blk, g * 64:(g + 1) * 64],
                                   rhs=ptb[:, buf, blk, :], start=(bi == 0), stop=(bi == len(blks) - 1))
                for bi, blk in enumerate(blks):
                    ins = E.matmul(psum[g1 * 64:(g1 + 1) * 64, 6 + g2, :], lhsT=onesb[:, 0:64],
                                   rhs=ptb[:, buf, blk, :], start=(bi == 0), stop=(bi == len(blks) - 1))
                return ins
            P.add("pe", pv, r=tuple(("pt", buf, blk) for blk in blks) + tuple(("v", qb + blk) for blk in blks) + (("onesb",),),
                  w=(("ps", 4 + g2), ("ps", 6 + g2)), regions=(RA_ATT, R, RQ))

        def emit_evac(qb, g2):
            if True:
                P.add("dve", lambda E, g2=g2:
                      E.tensor_tensor(out=d1[:, g2, :].rearrange("p (i q) -> p i q", q=128),
                                      in0=psum[:, 6 + g2, :].rearrange("p (i q) -> p i q", q=128),
                                      in1=esT[:, l, g2 * 4:(g2 + 1) * 4].unsqueeze(2).to_broadcast([128, 4, 128]), op=ALU.add),
                      r=(("ps", 6 + g2), ("esT", l)), w=(("d1", g2), ("ostage", 0), ("ostage", 1)), regions=(R,))
                P.add("act", lambda E, g2=g2: E.activation(out=d1[:, g2, :], in_=d1[:, g2, :], func=AF.Ln),
                      r=(("d1", g2),), w=(("d1", g2),), regions=(R,))
                P.add("act", lambda E, g2=g2: E.activation(out=d1[:, g2, :], in_=d1[:, g2, :], func=AF.Exp, scale=-1.0),
                      r=(("d1", g2),), w=(("d1", g2),), regions=(R,))
                P.add("dve", lambda E, g2=g2, qb=qb:
                      E.tensor_tensor(out=attnT[:, g2 * 4:(g2 + 1) * 4, qb * 128:(qb + 1) * 128],
                                      in0=psum[:, 4 + g2, :].rearrange("p (i q) -> p i q", q=128),
                                      in1=d1[:, g2, :].rearrange("p (i q) -> p i q", q=128), op=ALU.mult),
                      r=(("ps", 4 + g2), ("d1", g2)),
                      w=tuple(("at", g2 * 4 + i, 0 if qb < 4 else 1) for i in range(4)),
                      regions=(R, RA_ATT))

        emit_scores(0)
        for n in range(len(iters)):
            if n + 1 < len(iters):
                emit_scores(n + 1)
            emit_softmax(n)
            emit_pv(n)
            if n >= 1 and iters[n - 1][1] in (1, 3):
                emit_evac(iters[n - 1][0], iters[n - 1][1] // 2)
        emit_evac(iters[-1][0], iters[-1][1] // 2)
        RS = ("scr", "satt")
        kcs = sv(0, [128, NSEQ, 256], BF16)
        vcs = sv(4096, [128, NSEQ, 256], BF16)
        kcT = sv(8192, [128, 2, NSEQ, 128], BF16)
        scc = sv(12288, [128, NSEQ, 4, 4, 4], F32)
        scn = sv(14336, [128, 16, 32], F32)
        pcs = sv(16384, [128, NSEQ, 4, 4, 4], BF16)
        pns = sv(17408, [128, 16, 32], BF16)
        tA = sv(18432, [128, 256], F32)
        tD = sv(19456, [128, 256], F32)
        vn2 = sb_vn
        P.add("pool", lambda E: E.dma_start(out=kcs[:], in_=ck[l, st * NSEQ:(st + 1) * NSEQ].rearrange("b k c -> k b c")),
              w=(("kcs",),), dma="ldk", regions=(RS,))
        P.add("pool", lambda E: E.dma_start(out=vcs[:], in_=cv[l, st * NSEQ:(st + 1) * NSEQ].rearrange("b k c -> k b c")),
              w=(("vcs",),), dma="ldv", regions=(RS,))
        pbf = psum[:, 0:2, :].rearrange("p a b -> p (a b)").bitcast(BF16)

        def trk(E):
            ins = None
            for g2 in range(2):
                for b in range(NSEQ):
                    idx = g2 * NSEQ + b
                    ins = E.transpose(pbf[:, idx * 128:(idx + 1) * 128], kcs[:, b, g2 * 128:(g2 + 1) * 128], identb[:, :])
            return ins
        P.add("pe", trk, r=(("kcs",), ("identb",)), w=(("ps", 0), ("ps", 1)), regions=(RS,))
        for g2 in range(2):
            P.add("act", lambda E, g2=g2: E.activation(out=kcT[:, g2].rearrange("p b k -> p (b k)"), in_=pbf[:, g2 * 1024:(g2 + 1) * 1024], func=AF.Identity),
                  r=(("ps", g2),), w=(("kcT", g2),), regions=(RS,))

        def sc_c(E):
            ins = None
            for g1 in range(2):
                for b in range(NSEQ):
                    for g2 in range(2):
                        o = (b * 2 + g2) * 16
                        ins = E.matmul(psum[:, 2 + g1, o:o + 16], lhsT=kcT[g1 * 64:(g1 + 1) * 64, g2, b, :],
                                       rhs=qT[g1 * 64:(g1 + 1) * 64, g2 * 4:(g2 + 1) * 4, TP + b * 4:TP + b * 4 + 4],
                                       start=True, stop=True)
            return ins
        P.add("pe", sc_c, r=(("kcT", 0), ("kcT", 1)) + tuple(("q", c, 2) for c in range(8)), w=(("ps", 2), ("ps", 3)), regions=(RS, RA_ATT, RQ))

        def sc_n(E):
            ins = None
            for g1 in range(2):
                for g2 in range(2):
                    for i in range(4):
                        o = (g2 * 4 + i) * 32
                        ins = E.matmul(psum[0:TS, 4 + g1, o:o + 32], lhsT=kT[g1 * 64:(g1 + 1) * 64, g2, 128 + TP:128 + TP + TS],
                                       rhs=qT[g1 * 64:(g1 + 1) * 64, g2 * 4 + i, TP:TP + TS], start=True, stop=True)
            return ins
        P.add("pe", sc_n, r=tuple(("k", g2, 2) for g2 in range(2)) + tuple(("q", c, 2) for c in range(8)), w=(("ps", 4), ("ps", 5)),
              regions=(RS, RA_ATT, RQ))
        for g in range(4):
            g2, g1 = g // 2, g % 2
            ps_c = psum[:, 2 + g1, 0:256].rearrange("p (b g i t) -> p b g i t", b=NSEQ, g=2, i=4)
            for i in range(4):
                h = head_of(g, i)
                P.add("dve", lambda E, g=g, g2=g2, i=i, h=h, ps_c=ps_c:
                      E.scalar_tensor_tensor(out=scc[:, :, g, i, :], in0=dc[:, :].unsqueeze(1).to_broadcast([128, NSEQ, 4]), scalar=float(SLOPES[h]),
                                             in1=ps_c[:, :, g2, i, :], op0=ALU.mult, op1=ALU.add),
                      r=(("ps", 2 + g1), ("dc",)), w=(("scc", h),), regions=(RS,))
                P.add("act", lambda E, g=g, i=i, h=h:
                      E.activation(out=pcs[:, :, g, i, :], in_=scc[:, :, g, i, :], func=AF.Exp, bias=nsink[:, l, h:h + 1]),
                      r=(("scc", h), ("nsink", l)), w=(("pcs", h),), regions=(RS,))
                o = (g2 * 4 + i) * 32
                P.add("dve", lambda E, h=h, g1=g1, o=o:
                      E.scalar_tensor_tensor(out=scn[0:TS, h, :], in0=dn[0:TS, :], scalar=float(SLOPES[h]),
                                             in1=psum[0:TS, 4 + g1, o:o + 32], op0=ALU.mult, op1=ALU.add),
                      r=(("ps", 4 + g1), ("dn",)), w=(("scn", h),), regions=(RS,))
                P.add("act", lambda E, h=h:
                      E.activation(out=pns[0:TS, h, :], in_=scn[0:TS, h, :], func=AF.Exp, bias=nsink[0:TS, l, h:h + 1]),
                      r=(("scn", h), ("nsink", l)), w=(("pns", h),), regions=(RS,))
        def pv_s(E):
            ins = None
            for g in range(4):
                g2, g1 = g // 2, g % 2
                for i in range(4):
                    h = head_of(g, i)
                    c = g2 * 4 + i
                    ins = E.matmul(psum[g1 * 64:(g1 + 1) * 64, 0, c * 32:(c + 1) * 32], lhsT=vn2[0:TS, g * 64:(g + 1) * 64],
                                   rhs=pns[0:TS, h, :], start=True, stop=True)
                    ins = E.matmul(psum[g1 * 64:(g1 + 1) * 64, 1, c * 32:(c + 1) * 32], lhsT=onesb[0:TS, 0:64],
                                   rhs=pns[0:TS, h, :], start=True, stop=True)
            for b in range(NSEQ):
                for g in range(4):
                    g2, g1 = g // 2, g % 2
                    o = 256 + (g2 * NSEQ + b) * 16
                    ins = E.matmul(psum[g1 * 64:(g1 + 1) * 64, 0, o:o + 16], lhsT=vcs[:, b, g * 64:(g + 1) * 64],
                                   rhs=pcs[:, b, g, :, :], start=True, stop=True)
                    ins = E.matmul(psum[g1 * 64:(g1 + 1) * 64, 1, o:o + 16], lhsT=onesb[:, 0:64],
                                   rhs=pcs[:, b, g, :, :], start=True, stop=True)
            return ins
        P.add("pe", pv_s, r=tuple(("pns", h) for h in range(16)) + tuple(("pcs", h) for h in range(16)) + (("vn2",), ("vcs",), ("onesb",)), w=(("ps", 0), ("ps", 1)), regions=(RS,))
        P.add("act", lambda E: E.activation(out=tA[:, :], in_=psum[:, 0, 0:256], func=AF.Identity), r=(("ps", 0),), w=(("tA",),), regions=(RS,))
        P.add("act", lambda E: E.activation(out=tD[:, :], in_=psum[:, 1, 0:256], func=AF.Identity, bias=1.0), r=(("ps", 1),), w=(("tD",),), regions=(RS,))
        for g2 in range(2):
            nb = psum[:, 0, 256 + g2 * 128:256 + (g2 + 1) * 128].rearrange("p (b i t) -> p i b t", b=NSEQ, i=4)
            db = psum[:, 1, 256 + g2 * 128:256 + (g2 + 1) * 128].rearrange("p (b i t) -> p i b t", b=NSEQ, i=4)
            ta = tA[:, g2 * 128:(g2 + 1) * 128].rearrange("p (i b t) -> p i b t", i=4, b=NSEQ)
            td = tD[:, g2 * 128:(g2 + 1) * 128].rearrange("p (i b t) -> p i b t", i=4, b=NSEQ)
            P.add("dve", lambda E, nb=nb, ta=ta: E.tensor_tensor(out=ta, in0=nb, in1=ta, op=ALU.add), r=(("ps", 0), ("tA",)), w=(("tA",),), regions=(RS,))
            P.add("dve", lambda E, db=db, td=td: E.tensor_tensor(out=td, in0=db, in1=td, op=ALU.add), r=(("ps", 1), ("tD",)), w=(("tD",),), regions=(RS,))
        P.add("dve", lambda E: E.reciprocal(out=tD[:, :], in_=tD[:, :]), r=(("tD",),), w=(("tD",),), regions=(RS,))
        P.add("dve", lambda E: E.tensor_tensor(out=attnT[:, :, TP:TP + TS], in0=tA[:, :].rearrange("p (c n) -> p c n", n=TS),
                                               in1=tD[:, :].rearrange("p (c n) -> p c n", n=TS), op=ALU.mult),
              r=(("tA",), ("tD",)), w=tuple(("at", c, 2) for c in range(8)), regions=(RS, RA_ATT))

    sb_vn = sb("sb_vn", [128, 256], BF16)

    def glu_conv(st, l, tile0):
        R = ("scr", "glu")
        sg = sv(0, [128, 4, 512], F32)
        ostg = sv(8192, [128, D], F32)
        ow, _ = PL.cols[f"convw{l}"]
        for grp in range(2):
            if grp == 1:
                state_dma(st, l, 1)

            def tr(E):
                ins = None
                for k in range(8):
                    b = 6 + k // 4
                    ins = E.transpose(psum[:, b, (k % 4) * 128:(k % 4) * 128 + 120], sstage[0:120, k * 128:(k + 1) * 128], identf[0:120, 0:120])
                return ins
            P.add("pe", tr, r=(("sstage",), ("identf",)), w=(("ps", 6), ("ps", 7)), regions=(R,))
            for hb in range(2):
                src_ps = psum[:, 6 + hb, :].rearrange("p (k t) -> p k t", t=128)[:, :, 0:120].rearrange("p k (b r) -> p k b r", r=30)
                P.add("act", lambda E, src_ps=src_ps, hb=hb, grp=grp:
                      E.activation(out=uexts[:, hb * 4:(hb + 1) * 4, grp * 4:(grp + 1) * 4, 0:30], in_=src_ps, func=AF.Identity),
                      r=(("ps", 6 + hb),), w=tuple(("uexts", k) for k in range(hb * 4, hb * 4 + 4)))
        for cb in range(2):
            pass
        cnt = 0
        for t in range(4):
            wv, wk = wtile(l, tile0 + t, 4096)
            w4 = wv.rearrange("p (k s c) -> p k s c", k=8, s=2, c=256)
            for cl in range(2):
                c = t * 2 + cl
                ub = c % 2
                if st == 0:
                    P.add("pool", lambda E, ub=ub: E.memset(uextb[:, ub, 0:30], 0.0), w=(("uext", ub, "pre"),))
                else:
                    P.add("pool", lambda E, ub=ub, c=c: E.tensor_copy(out=uextb[:, ub, 0:30], in_=ucarry[:, l, c, :]),
                          r=(("ucarry", l, c),), w=(("uext", ub, "pre"),))
                P.add("dve", lambda E, c=c:
                      E.tensor_tensor(out=dg[:, :, :], in0=identb[:, :].unsqueeze(1).to_broadcast([128, 31, 128]),
                                      in1=prm[:, ow + c * 31:ow + c * 31 + 31].unsqueeze(2).to_broadcast([128, 31, 128]), op=ALU.mult),
                      r=(("identb",), ("prm",)), w=(("dg",),))
                abank = (0, 1, 2)
                gbank = (3, 6, 7)
                dense_ui(lambda k, w4=w4, cl=cl: w4[:, k, 0, cl * 128:(cl + 1) * 128], xb, "xb", wk, abank)
                dense_ui(lambda k, w4=w4, cl=cl: w4[:, k, 1, cl * 128:(cl + 1) * 128], xb, "xb", wk, gbank)
                for u, (off, n) in enumerate(UNITS):
                    rb = cnt % 4
                    cnt += 1
                    pa, pg = abank[u], gbank[u]
                    P.add("act", lambda E, pg=pg, rb=rb, n=n, c=c:
                          E.activation(out=sg[:, rb, 0:n], in_=PS(pg, n), func=AF.Sigmoid, bias=pcol("bgg", l, c)),
                          r=(("ps", pg), ("prm",)), w=(("sg", rb),), regions=(R,))
                    if u < 2:
                        dst = uextb[:, ub, 30 + off:30 + off + n]
                        src1 = sg[:, rb, 0:n]
                        src0 = PS(pa, n)
                        wkey = ("uext", ub, u)
                    else:
                        dst = uexts[:, c, :, 30:34]
                        src1 = sg[:, rb, 0:n].rearrange("p (b t) -> p b t", t=4)
                        src0 = PS(pa, n).rearrange("p (b t) -> p b t", t=4)
                        wkey = ("uexts_new", c)
                    P.add("dve", lambda E, dst=dst, src0=src0, src1=src1, c=c:
                          E.scalar_tensor_tensor(out=dst, in0=src0, scalar=pcol("bga", l, c), in1=src1, op0=ALU.add, op1=ALU.mult),
                          r=(("ps", pa), ("sg", rb), ("prm",)), w=(wkey,), regions=(R,))
                    if st == 1 and u == 1:
                        P.add("dve", lambda E, pa=pa, rb=rb, n=n, c=c:
                              E.scalar_tensor_tensor(out=u32tail[:, c, :], in0=psum[:, pa, n - 30:n], scalar=pcol("bga", l, c),
                                                     in1=sg[:, rb, n - 30:n], op0=ALU.add, op1=ALU.mult),
                              r=(("ps", pa), ("sg", rb), ("prm",)), w=(("u32tail", c),), regions=(R,))
                ukeys = (("uext", ub, "pre"), ("uext", ub, 0), ("uext", ub, 1))
                for u in range(2):
                    off = u * 512

                    def cmm(E, ub=ub, off=off, u=u):
                        ins = None
                        for j in range(31):
                            ins = E.matmul(PS(4 + u), lhsT=dg[:, j, :], rhs=uextb[:, ub, off + j:off + j + 512],
                                           start=(j == 0), stop=(j == 30))
                        return ins
                    P.add("pe", cmm, r=ukeys + (("dg",),), w=(("ps", 4 + u),))
                    P.add("act", lambda E, c=c, off=off, u=u:
                          E.activation(out=cT[:, c, off:off + 512], in_=PS(4 + u), func=AF.Identity, bias=pcol("conv_b", l, c)),
                          r=(("ps", 4 + u), ("prm",)), w=(("c", c, u),), regions=(RA_ATT, RC))

                def tap_s(j, c=c):
                    dstv = (cT[:, c, TP:TP + TS] if j % 2 == 0 else cacc[:, 0:TS]).rearrange("p (b t) -> p b t", t=4)
                    akey = (("c", c, 2),) if j % 2 == 0 else (("cacc", 1),)
                    wj = prm[:, ow + c * 31 + j:ow + c * 31 + j + 1]
                    if j == 0:
                        f = lambda E: E.tensor_scalar(out=dstv, in0=uexts[:, c, :, 0:4], scalar1=wj, scalar2=pcol("conv_b", l, c),
                                                      op0=ALU.mult, op1=ALU.add)
                        rk = ()
                    elif j == 1:
                        f = lambda E: E.tensor_scalar(out=dstv, in0=uexts[:, c, :, 1:5], scalar1=wj, scalar2=None, op0=ALU.mult)
                        rk = ()
                    else:
                        f = lambda E: E.scalar_tensor_tensor(out=dstv, in0=uexts[:, c, :, j:j + 4], scalar=wj, in1=dstv,
                                                             op0=ALU.mult, op1=ALU.add)
                        rk = akey
                    P.add("dve", f, r=(("uexts", c), ("uexts_new", c), ("prm",)) + rk, w=akey, regions=(RA_ATT, RC))
                for j in range(31):
                    tap_s(j)
                P.add("dve", lambda E, c=c: E.tensor_tensor(out=cT[:, c, TP:TP + TS], in0=cT[:, c, TP:TP + TS], in1=cacc[:, 0:TS], op=ALU.add),
                      r=(("c", c, 2), ("cacc", 1)), w=(("c", c, 2),), regions=(RA_ATT, RC))
                if st == 0:
                    P.add("pool", lambda E, ub=ub, c=c: E.tensor_copy(out=ucarry[:, l, c, :], in_=uextb[:, ub, TP:TP + 30]),
                          r=(("uext", ub, 1),), w=(("ucarry", l, c),))
                P.add("pool", lambda E, c=c: E.tensor_copy(out=sv(12288, [128, 8, TS], F32)[:, c, :].rearrange("p (b t) -> p b t", t=4),
                                                           in_=uexts[:, c, :, 30:34]),
                      r=(("uexts_new", c),), w=(("unew", c),), regions=(R,))
        unew = sv(12288, [128, 8, TS], F32)
        if st == 1:
            def trp(E):
                ins = None
                for c in range(8):
                    ins = E.transpose(psum[0:30, 4 + c // 4, (c % 4) * 128:(c % 4 + 1) * 128], u32tail[:, c, :], identf[:, :])
                return ins
            P.add("pe", trp, r=tuple(("u32tail", c) for c in range(8)) + (("identf",),), w=(("ps", 4), ("ps", 5)), regions=(R,))
            for hb in range(2):
                P.add("act", lambda E, hb=hb: E.activation(out=ostg[0:30, hb * 512:(hb + 1) * 512], in_=psum[0:30, 4 + hb, :], func=AF.Identity),
                      r=(("ps", 4 + hb),), w=(("ostg",),), regions=(R,))
            P.add("sp", lambda E: E.dma_start(out=ncp[l], in_=ostg[0:30, :]), r=(("ostg",),), dma="out1*", regions=(R,))

        def trs(E):
            ins = None
            for c in range(8):
                ins = E.transpose(psum[0:TS, 6 + c // 4, (c % 4) * 128:(c % 4 + 1) * 128], unew[:, c, :], identf[:, :])
            return ins
        P.add("pe", trs, r=tuple(("unew", c) for c in range(8)) + (("identf",),), w=(("ps", 6), ("ps", 7)), regions=(R,))
        ostg2 = sv(13312, [128, D], F32)
        for hb in range(2):
            P.add("act", lambda E, hb=hb: E.activation(out=ostg2[0:TS, hb * 512:(hb + 1) * 512], in_=psum[0:TS, 6 + hb, :], func=AF.Identity),
                  r=(("ps", 6 + hb),), w=(("ostg2",),), regions=(R,))
        for b in range(NSEQ):
            P.add("sp", lambda E, b=b: E.dma_start(out=ncs[l, st * NSEQ + b, 26:30, :], in_=ostg2[4 * b:4 * b + 4, :]),
                  r=(("ostg2",),), dma="out1*", regions=(R,))
        layer_norm(cT, "c", LN_EPS, "cln_g", "cln_b", l, "cs")

    def out_proj(l, tile0):
        R = ("scr", "op")
        sga = sv(0, [128, T], F32)
        sgc = sv(4224, [128, T], F32)
        m1 = sv(8448, [128, T], F32)
        m2 = sv(12672, [128, T], F32)
        for pr in range(4):
            wA, kA = wtile(l, tile0 + pr * 2 + 0, 4096)
            wC, kC = wtile(l, tile0 + pr * 2 + 1, 4096)
            vA = wA.rearrange("p (k s c) -> p k s c", k=8, s=2, c=256)
            vC = wC.rearrange("p (k s c) -> p k s c", k=8, s=2, c=256)
            for cl in range(2):
                c = pr * 2 + cl
                dense_ui(lambda k, vA=vA, cl=cl: vA[:, k, 0, cl * 128:(cl + 1) * 128], xb, "xb", kA, (0, 1, 2))
                for u, (off, n) in enumerate(UNITS):
                    P.add("act", lambda E, u=u, off=off, n=n, c=c:
                          E.activation(out=sga[:, off:off + n], in_=PS(u, n), func=AF.Sigmoid, bias=pcol("bta", l, c)),
                          r=(("ps", u), ("prm",)), w=(("sga", u),), regions=(R,))
                dense_ui(lambda k, vA=vA, cl=cl: vA[:, k, 1, cl * 128:(cl + 1) * 128], attnT, "at", kA, (3, 4, 5), regions=(RA_ATT,))
                for u, (off, n) in enumerate(UNITS):
                    P.add("dve", lambda E, u=u, off=off, n=n:
                          E.tensor_tensor(out=m1[:, off:off + n], in0=PS(3 + u, n), in1=sga[:, off:off + n], op=ALU.mult),
                          r=(("ps", 3 + u), ("sga", u)), w=(("m1", u),), regions=(R,))
                dense_ui(lambda k, vC=vC, cl=cl: vC[:, k, 0, cl * 128:(cl + 1) * 128], xb, "xb", kC, (0, 1, 2))
                for u, (off, n) in enumerate(UNITS):
                    P.add("act", lambda E, u=u, off=off, n=n, c=c:
                          E.activation(out=sgc[:, off:off + n], in_=PS(u, n), func=AF.Sigmoid, bias=pcol("btc", l, c)),
                          r=(("ps", u), ("prm",)), w=(("sgc", u),), regions=(R,))
                dense_ui(lambda k, vC=vC, cl=cl: vC[:, k, 1, cl * 128:(cl + 1) * 128], csT, "cs", kC, (3, 4, 5), regions=(RA_ATT,))
                for u, (off, n) in enumerate(UNITS):
                    P.add("dve", lambda E, u=u, off=off, n=n:
                          E.tensor_tensor(out=m2[:, off:off + n], in0=PS(3 + u, n), in1=sgc[:, off:off + n], op=ALU.mult),
                          r=(("ps", 3 + u), ("sgc", u)), w=(("m2", u),), regions=(R,))
                    P.add("dve", lambda E, u=u, off=off, n=n, c=c:
                          E.tensor_tensor(out=mixT[:, c, off:off + n], in0=m1[:, off:off + n], in1=m2[:, off:off + n], op=ALU.add),
                          r=(("m1", u), ("m2", u)), w=(("mix", c, u),), regions=(R, RA_ATT, RM))
        for tq in range(2):
            wv, wk = wtile(l, tile0 + 8 + tq, 4096)
            w3 = wv.rearrange("p (k c) -> p k c", k=8)
            for cl in range(4):
                c = tq * 4 + cl

                if c == 7:
                    def mm_u(u, w3=w3, cl=cl):
                        off, n = UNITS[u]

                        def mm(E):
                            ins = None
                            for k in range(8):
                                ins = E.matmul(PS(u, n), lhsT=w3[:, k, cl * 128:(cl + 1) * 128], rhs=mixT[:, k, off:off + n],
                                               start=(k == 0), stop=(k == 7))
                            return ins
                        P.add("pe", mm, r=wk + tuple(("mix", k, u) for k in range(8)), w=(("ps", u),), regions=(RA_ATT, RM))

                    def ev_u(u, c=c):
                        off, n = UNITS[u]
                        P.add("dve", lambda E: E.scalar_tensor_tensor(out=x32[:, c, off:off + n], in0=x32[:, c, off:off + n], scalar=ALPHA,
                                                                      in1=PS(u, n), op0=ALU.mult, op1=ALU.add),
                              r=(("ps", u), ("x32", c, u)), w=(("x32", c, u),))
                    fa = (x32, "x32", LN_EPS, "ln2_g", "ln2_b", l, "x")
                    mm_u(0); ln_stats_chunk(x32, "x32", 6, "x"); ev_u(0)
                    mm_u(1); ln_stats_chunk(x32, "x32", 7, "x", units=(0,)); ev_u(1)
                    mm_u(2); ln_stats_chunk(x32, "x32", 7, "x", units=(1,)); ev_u(2)
                    ln_stats_chunk(x32, "x32", 7, "x", units=(2,)); ln_finish(*fa)
                    continue

                def mm(E, w3=w3, cl=cl):
                    ins = None
                    for k in range(8):
                        for u, (off, n) in enumerate(UNITS):
                            ins = E.matmul(PS(u, n), lhsT=w3[:, k, cl * 128:(cl + 1) * 128], rhs=mixT[:, k, off:off + n],
                                           start=(k == 0), stop=(k == 7))
                    return ins
                P.add("pe", mm, r=wk + tuple(("mix", k, u) for k in range(8) for u in range(3)), w=tuple(("ps", u) for u in range(3)),
                      regions=(RA_ATT, RM))
                if c > 0:
                    ln_stats_chunk(x32, "x32", c - 1, "x")
                for u, (off, n) in enumerate(UNITS):
                    P.add("dve", lambda E, c=c, off=off, n=n, u=u:
                          E.scalar_tensor_tensor(out=x32[:, c, off:off + n], in0=x32[:, c, off:off + n], scalar=ALPHA,
                                                 in1=PS(u, n), op0=ALU.mult, op1=ALU.add),
                          r=(("ps", u), ("x32", c, u)), w=(("x32", c, u),))

    def dump(name, src):
        if not DEBUG:
            return
        if src.dtype == F32:
            P.add("sp", lambda E: E.dma_start(out=dbg[name], in_=src), r=tuple((kk, k, u) for kk in ("x32", "c") for k in range(8) for u in range(3)), dma="out1*")

    import os
    STOP = int(os.environ.get("KSTOP", "99"))
    NOCOPY = os.environ.get("KNOCOPY", "0") == "1"
    setup()
    for st in range(int(os.environ.get('KNST', NST))):
        if STOP >= -1:
            load_x(st)
        for l in range(2):
            if STOP >= 1:
                ffn(l, 1, 0)
            if STOP >= 2:
                qkv_attention(st, l, 19)
            if STOP >= 3:
                glu_conv(st, l, 22)
            if STOP >= 4:
                out_proj(l, 26)
            if STOP >= 5:
                ffn(l, 2, 36)
        if STOP >= 0:
            store_y(st)
    print('SBUF bytes remaining:', nc.sbuf_bytes_remaining, 'ops:', len(P.ops))
    P.finalize()
    P.emit(nc)
    es.close()
    return nc


_CACHE = {}


def kernel(**inp):
    inp = {k: np.asarray(v) for k, v in inp.items()}
    WT = pack_weights(inp)
    PAR = pack_params(inp)
    C = pack_consts()
    if "nc" not in _CACHE:
        _CACHE["nc"] = build_program()
    nc = _CACHE["nc"]
    in_maps = []
    for c in range(NCORES):
        in_maps.append({
            "xp": np.ascontiguousarray(inp["x_prompt"][c]),
            "xs": np.ascontiguousarray(inp["x_sample"][c * 16:(c + 1) * 16].reshape(64, D)),
            "ck": np.ascontiguousarray(inp["cache_k"][:, c * 16:(c + 1) * 16].reshape(2, 16, 128, 256)),
            "cv": np.ascontiguousarray(inp["cache_v"][:, c * 16:(c + 1) * 16].reshape(2, 16, 128, 256)),
            "scv": np.ascontiguousarray(inp["state_conv"][:, c * 16:(c + 1) * 16]),
            "wt": WT, "par": PAR,
            "c_ident": C["ident"], "c_dp": C["dp"], "c_dc": C["dc"], "c_dn": C["dn"],
        })
    res = run_bass_kernel_spmd(nc, in_maps, core_ids=list(range(NCORES)))
    R = res.results
    y_prompt = np.stack([R[c]["yp"] for c in range(NCORES)], 0)
    y_sample = np.concatenate([R[c]["ys"].reshape(16, 4, D) for c in range(NCORES)], 0)
    nkp = np.stack([R[c]["nkp"].reshape(2, 128, 4, 64) for c in range(NCORES)], 1)
    nvp = np.stack([R[c]["nvp"].reshape(2, 128, 4, 64) for c in range(NCORES)], 1)
    ncp = np.stack([R[c]["ncp"] for c in range(NCORES)], 1)
    nks = np.concatenate([R[c]["nks"].reshape(2, 16, 128, 4, 64) for c in range(NCORES)], 1)
    nvs = np.concatenate([R[c]["nvs"].reshape(2, 16, 128, 4, 64) for c in range(NCORES)], 1)
    ncs = np.concatenate([R[c]["ncs"] for c in range(NCORES)], 1)
    _CACHE["last"] = R
    return (y_prompt.astype(np.float32), y_sample.astype(np.float32), nkp.astype(np.float32), nvp.astype(np.float32),
            ncp.astype(np.float32), nks.astype(np.float32), nvs.astype(np.float32), ncs.astype(np.float32))
```

```python
import os
import numpy as np
import concourse.bass as bass
import concourse.mybir as mybir
from concourse.bass_utils import run_bass_kernel_spmd

F32 = mybir.dt.float32
BF16 = mybir.dt.bfloat16
ALU = mybir.AluOpType
AF = mybir.ActivationFunctionType

NCORES = 8
D = 1024
DFF = 2816
DIN = 5632
NJ = 22
TP = 1024
TS = 32
T = TP + TS
NST = 2
NSEQ = 8
UNITS = [(0, 512), (512, 512), (1024, 32)]
ALPHA = 4.0 ** 0.25
LN_EPS = 1e-5
NEG = -1.0e6
SLOPES = [2.0 ** (-8.0 * (h + 1) / 16.0) for h in range(16)]
NS = 4
SLOT = 4096
TILES_PER_LAYER = 55
DEBUG = False


def chunk_heads(c):
    g2, i = c // 4, c % 4
    return (2 * g2) * 4 + i, (2 * g2 + 1) * 4 + i


def head_of(g, i):
    return g * 4 + i


class PL:
    cols = {}
    n = 0

    @classmethod
    def add(cls, name, w):
        cls.cols[name] = (cls.n, w)
        cls.n += w


for _l in range(2):
    for nm in ["ln1_g", "ln1_b", "ln2_g", "ln2_b", "ln3_g", "ln3_b", "cln_g", "cln_b", "conv_b",
               "bq", "bga", "bgg", "bta", "btc"]:
        PL.add(f"{nm}{_l}", 8)
    PL.add(f"bk{_l}", 2)
    PL.add(f"convw{_l}", 8 * 31)
    PL.add(f"bkv_b{_l}", 512)
    PL.add(f"sink_b{_l}", 16)
    PL.add(f"sinkp{_l}", 8)
NPAR = PL.n


def pack_params(inp):
    P = np.zeros((128, NPAR), np.float32)

    def pm(v):
        return np.ascontiguousarray(v.reshape(-1, 128).T)

    for l in range(2):
        def put(name, arr):
            o, w = PL.cols[f"{name}{l}"]
            assert arr.shape == (128, w), (name, arr.shape, w)
            P[:, o:o + w] = arr
        put("ln1_g", pm(inp["ln1_g"][l])); put("ln1_b", pm(inp["ln1_b"][l]))
        put("ln2_g", pm(inp["ln2_g"][l])); put("ln2_b", pm(inp["ln2_b"][l]))
        put("ln3_g", pm(inp["ln3_g"][l])); put("ln3_b", pm(inp["ln3_b"][l]))
        put("cln_g", pm(inp["conv_ln_g"][l])); put("cln_b", pm(inp["conv_ln_b"][l]))
        put("conv_b", pm(inp["conv_dw_b"][l]))
        b = inp["b_in"][l]
        bq = b[0:1024].reshape(16, 64)
        bqp = np.zeros((128, 8), np.float32)
        for c in range(8):
            ha, hb = chunk_heads(c)
            bqp[0:64, c] = bq[ha]
            bqp[64:128, c] = bq[hb]
        put("bq", bqp)
        put("bk", pm(b[1024:1280]))
        put("bga", pm(b[1536:2560])); put("bgg", pm(b[2560:3584]))
        put("bta", pm(b[3584:4608])); put("btc", pm(b[4608:5632]))
        cw = inp["conv_dw_w"][l]
        put("convw", np.ascontiguousarray(cw.reshape(31, 8, 128).transpose(2, 1, 0).reshape(128, 248)))
        put("bkv_b", np.broadcast_to(b[1024:1536][None, :], (128, 512)).copy())
        put("sink_b", np.broadcast_to(inp["attn_sinks"][l][None, :], (128, 16)).copy())
        sp_ = np.zeros((128, 8), np.float32)
        for c in range(8):
            ha, hb = chunk_heads(c)
            sp_[0:64, c] = inp["attn_sinks"][l][ha]
            sp_[64:128, c] = inp["attn_sinks"][l][hb]
        put("sinkp", sp_)
    return P


def pack_consts():
    C = {}
    C["ident"] = np.eye(128, dtype=np.float32)
    c = np.arange(128)[:, None]
    r = np.arange(128)[None, :]
    d_prev = 128 + r - c
    d_cur = r - c
    Dp = np.zeros((128, 2, 128), np.float32)
    Dp[:, 0, :] = np.where(c >= r, -d_prev, NEG)
    Dp[:, 1, :] = np.where(c <= r, -d_cur, NEG)
    C["dp"] = Dp.reshape(128, 256)
    t = np.arange(4)[None, :]
    Dc = np.where(c >= t, -(128 + t - c), NEG).astype(np.float32)
    C["dc"] = np.ascontiguousarray(Dc)
    kb, kt = np.divmod(np.arange(32), 4)
    Dn = np.where((kb[:, None] == kb[None, :]) & (kt[:, None] <= kt[None, :]),
                  -(kt[None, :] - kt[:, None]), NEG).astype(np.float32)
    Dn_full = np.zeros((128, 32), np.float32)
    Dn_full[0:32] = Dn
    C["dn"] = Dn_full
    return C


def pack_weights(inp):
    WT = np.zeros((2, TILES_PER_LAYER, 128, SLOT), np.float32)

    def kp(w):
        return w.reshape(-1, 128, w.shape[1]).transpose(1, 0, 2)

    for l in range(2):
        ti = 0

        def put(arr):
            nonlocal ti
            a = arr.reshape(128, -1)
            WT[l, ti, :, :a.shape[1]] = a
            ti += 1

        def ffn(up, down):
            upk = kp(up)
            for t in range(11):
                tile = np.stack([upk[:, :, t * 256:(t + 1) * 256],
                                 upk[:, :, 2816 + t * 256: 2816 + (t + 1) * 256]], axis=2)
                put(tile)
            dk = kp(down)
            for op in range(4):
                for h in range(2):
                    put(dk[:, h * 11:(h + 1) * 11, op * 256:(op + 1) * 256])

        ffn(inp["ffn1_up"][l], inp["ffn1_down"][l])
        win = kp(inp["w_in"][l])
        qcols = []
        for c in range(8):
            ha, hb = chunk_heads(c)
            qcols += list(range(ha * 64, ha * 64 + 64)) + list(range(hb * 64, hb * 64 + 64))
        wq = win[:, :, qcols]
        put(wq[:, :, 0:512]); put(wq[:, :, 512:1024])
        put(win[:, :, 1024:1536])
        for t in range(4):
            put(np.stack([win[:, :, 1536 + t * 256:1536 + (t + 1) * 256],
                          win[:, :, 2560 + t * 256:2560 + (t + 1) * 256]], axis=2))
        wa = inp["w_attn_out"][l]
        rows = []
        for kc in range(8):
            ha, hb = chunk_heads(kc)
            rows.append(np.concatenate([wa[ha * 64:ha * 64 + 64], wa[hb * 64:hb * 64 + 64]], axis=0))
        wap = np.stack(rows, axis=1)
        wc = kp(inp["w_conv_out"][l])
        for pr in range(4):
            put(np.stack([win[:, :, 3584 + pr * 256:3584 + (pr + 1) * 256], wap[:, :, pr * 256:(pr + 1) * 256]], axis=2))
            put(np.stack([win[:, :, 4608 + pr * 256:4608 + (pr + 1) * 256], wc[:, :, pr * 256:(pr + 1) * 256]], axis=2))
        wo = kp(inp["w_out"][l])
        put(wo[:, :, 0:512]); put(wo[:, :, 512:1024])
        ffn(inp["ffn2_up"][l], inp["ffn2_down"][l])
        assert ti == TILES_PER_LAYER, ti
    return WT


class Prog:
    ENG = ["pe", "act", "dve", "pool", "sp"]

    def __init__(self):
        self.ops = []
        self.rr = {}

    def add(self, eng, fn, r=(), w=(), dma=None, regions=(), wtile=None):
        if dma is not None and dma.endswith("*"):
            base = dma[:-1]
            n = self.rr.get(base, 0)
            self.rr[base] = n + 1
            dma = f"{base}_{n % 8}"
        self.ops.append(dict(eng=eng, fn=fn, r=tuple(r), w=tuple(w), dma=dma, regions=tuple(regions),
                             wtile=wtile))
        return len(self.ops) - 1

    def finalize(self):
        ops = self.ops
        last_reader = {}
        for i, op in enumerate(ops):
            for k in op["r"]:
                if k[0] == "wt":
                    last_reader[k[1]] = i
        loads = {}
        for i, op in enumerate(ops):
            if op["wtile"] is not None:
                loads.setdefault(op["wtile"], []).append(i)
        load_idx = set(i for v in loads.values() for i in v)
        after = {}
        head = []
        for n in sorted(loads):
            if n < NS:
                head += loads[n]
            else:
                after.setdefault(last_reader[n - NS], []).extend(loads[n])
        order = []
        first_nonload = True
        for i, op in enumerate(ops):
            if i in load_idx:
                continue
            if first_nonload:
                order += head
                first_nonload = False
            order.append(i)
            if i in after:
                order += after[i]
        assert len(order) == len(ops)
        self.order = order
        last_write = {}
        readers = {}
        region = {}
        pos = {}
        for p, i in enumerate(order):
            pos[i] = p
        deps_of = {}
        last_dma = {}
        for i in order:
            op = ops[i]
            deps = set()
            for k in op["r"]:
                if k in last_write:
                    deps.add(last_write[k])
                if k[0] == "ps":
                    for rr in readers.get(k, ()):
                        if ops[rr]["eng"] != op["eng"]:
                            deps.add(rr)
            for k in op["w"]:
                if k in last_write:
                    deps.add(last_write[k])
                deps.update(readers.get(k, ()))
            for (rn, ident) in op["regions"]:
                st = region.setdefault(rn, dict(ident=ident, users={}, barrier=set()))
                if st["ident"] != ident:
                    st["barrier"] = set(st["users"].values())
                    st["users"] = {}
                    st["ident"] = ident
                deps |= st["barrier"]
                ukey = op["dma"] if op["dma"] else op["eng"]
                st["users"][ukey] = i
            if op["dma"]:
                if op["dma"] in last_dma:
                    deps.add(last_dma[op["dma"]])
                last_dma[op["dma"]] = i
            deps.discard(i)
            for k in op["r"]:
                readers.setdefault(k, []).append(i)
            for k in op["w"]:
                last_write[k] = i
                readers[k] = []
            deps_of[i] = deps
        signal = set()
        for i in order:
            for d in deps_of[i]:
                signal.add(d)
        eng_count = {e: 0 for e in self.ENG}
        dma_count = {}
        sigval = {}
        for i in order:
            op = ops[i]
            if op["dma"]:
                dma_count[op["dma"]] = dma_count.get(op["dma"], 0) + 1
                sigval[i] = ("dma:" + op["dma"], 16 * dma_count[op["dma"]])
            elif i in signal:
                eng_count[op["eng"]] += 1
                sigval[i] = ("eng:" + op["eng"], eng_count[op["eng"]])
        self.dma_sems = sorted(dma_count)
        self.dma_total = {k: 16 * v for k, v in dma_count.items()}
        waited = {e: {} for e in self.ENG}
        for i in order:
            op = ops[i]
            e = op["eng"]
            need = {}
            for d in deps_of[i]:
                dop = ops[d]
                if (not dop["dma"]) and dop["eng"] == e and not op["dma"] and (e == "pe" or os.environ.get("KNOSELF", "0") == "1"):
                    continue
                sname, val = sigval[d]
                if need.get(sname, 0) < val:
                    need[sname] = val
            waits = []
            for sname, val in need.items():
                if waited[e].get(sname, 0) >= val:
                    continue
                waited[e][sname] = val
                waits.append((sname, val))
            op["waits"] = waits
            op["sig"] = sigval.get(i)
        self.maxcount = dict(eng_count)

    def emit(self, nc):
        import contextlib
        ops = self.ops
        sem_names = ["eng:" + e for e in self.ENG] + ["dma:" + s for s in self.dma_sems]
        with contextlib.ExitStack() as es:
            sems = {}
            for sn in sem_names:
                sems[sn] = es.enter_context(nc.semaphore(sn.replace(":", "_")))
            block = es.enter_context(nc.Block())

            def run(eng_name, E):
                for i in self.order:
                    op = ops[i]
                    if op["eng"] != eng_name:
                        continue
                    for (sname, val) in op["waits"]:
                        E.wait_ge(sems[sname], val)
                    ins = op["fn"](E)
                    if op["sig"] is not None:
                        sname, val = op["sig"]
                        ins.then_inc(sems[sname], 16 if op["dma"] else 1)
                if eng_name == "sp":
                    for s in self.dma_sems:
                        if s.startswith("out"):
                            E.wait_ge(sems["dma:" + s], self.dma_total[s])

            @block.tensor
            def _(e):
                run("pe", e)

            @block.scalar
            def _(e):
                run("act", e)

            @block.vector
            def _(e):
                run("dve", e)

            @block.gpsimd
            def _(e):
                run("pool", e)

            @block.sync
            def _(e):
                run("sp", e)


def build_program():
    import contextlib
    nc = bass.Bass("TRN2", target_bir_lowering=False)
    es = contextlib.ExitStack()
    P = Prog()

    def din(name, shape):
        return nc.dram_tensor(name, list(shape), F32, kind="ExternalInput").ap()

    def dout(name, shape):
        return nc.dram_tensor(name, list(shape), F32, kind="ExternalOutput").ap()

    xp = din("xp", [2048, D]); xs = din("xs", [64, D])
    ck = din("ck", [2, 16, 128, 256]); cv = din("cv", [2, 16, 128, 256])
    scv = din("scv", [2, 16, 30, D])
    wt = din("wt", [2, TILES_PER_LAYER, 128, SLOT])
    par = din("par", [128, NPAR])
    c_ident = din("c_ident", [128, 128]); c_dp = din("c_dp", [128, 256])
    c_dc = din("c_dc", [128, 4]); c_dn = din("c_dn", [128, 32])
    yp = dout("yp", [2048, D]); ys = dout("ys", [64, D])
    nkp = dout("nkp", [2, 128, 256]); nvp = dout("nvp", [2, 128, 256]); ncp = dout("ncp", [2, 30, D])
    nks = dout("nks", [2, 16, 128, 256]); nvs = dout("nvs", [2, 16, 128, 256]); ncs = dout("ncs", [2, 16, 30, D])
    dbg = {}
    if DEBUG:
        for nm in ["d_x1", "d_q", "d_attn", "d_cs", "d_x2", "d_x3"]:
            dbg[nm] = dout(nm, [128, 8, T])

    def sb(name, shape, dt):
        return es.enter_context(nc.sbuf_tensor(name, list(shape), dt))

    x32 = sb("x32", [128, 8, T], F32)
    xb = sb("xb", [128, 8, T], BF16)
    ring = sb("ring", [128, NS, SLOT], BF16)
    ARENA_B = 67584
    arena = sb("arena", [128, ARENA_B // 2], BF16)
    SCR_B = 20480
    scr = sb("scr", [128, SCR_B // 2], BF16)
    prm = sb("prm", [128, NPAR], F32)
    identf = sb("identf", [128, 128], F32)
    identb = sb("identb", [128, 128], BF16)
    onesb = sb("onesb", [128, 128], BF16)
    eps1 = sb("eps1", [128, 1], F32)
    eps4 = sb("eps4", [128, 1], F32)
    dp = sb("dp", [128, 2, 128], F32)
    dc = sb("dc", [128, 4], F32)
    dn = sb("dn", [128, 32], F32)
    nsink = sb("nsink", [128, 2, 16], F32)
    bq8 = sb("bq8", [128, 2, 8], F32)
    esT = sb("esT", [128, 2, 8], F32)
    uextb = sb("uextb", [128, 2, 30 + TP], BF16)
    dg = sb("dg", [128, 31, 128], BF16)
    u32tail = sb("u32tail", [128, 8, 30], F32)
    cacc = sb("cacc", [128, TS], F32)
    sstage = sb("sstage", [128, D], F32)
    uexts = sb("uexts", [128, 8, NSEQ, 34], F32)
    ucarry = sb("ucarry", [128, 2, 8, 30], BF16)
    kcarry = sb("kcarry", [128, 2, 2, 128], BF16)
    vcarry = sb("vcarry", [128, 2, 256], BF16)
    psum = es.enter_context(nc.psum_tensor("psum", [128, 8, 512], F32))

    def av(off_b, shape, dt):
        n = int(np.prod(shape[1:]))
        if dt == BF16:
            v = arena[:, off_b // 2: off_b // 2 + n]
        else:
            v = arena[:, off_b // 2: off_b // 2 + 2 * n].bitcast(F32)
        names = "abcdef"[:len(shape) - 1]
        if len(shape) > 2:
            kw = {names[i]: shape[i + 1] for i in range(len(shape) - 1)}
            v = v.rearrange("p (%s) -> p %s" % (" ".join(names), " ".join(names)), **kw)
        return v

    def sv(off_b, shape, dt):
        n = int(np.prod(shape[1:]))
        if dt == BF16:
            v = scr[:, off_b // 2: off_b // 2 + n]
        else:
            v = scr[:, off_b // 2: off_b // 2 + 2 * n].bitcast(F32)
        names = "abcdef"[:len(shape) - 1]
        if len(shape) > 2:
            kw = {names[i]: shape[i + 1] for i in range(len(shape) - 1)}
            v = v.rearrange("p (%s) -> p %s" % (" ".join(names), " ".join(names)), **kw)
        return v

    gT = av(0, [128, NJ, T], BF16)
    qT = av(0, [128, 8, T], BF16)
    mixT = qT
    kT = av(16896, [128, 2, 128 + T], BF16)
    vtok = av(21632, [128, 9, 256], BF16)
    cT = av(0, [128, 8, T], F32)
    attnT = av(33792, [128, 8, T], BF16)
    csT = av(50688, [128, 8, T], BF16)
    RA_FFN = ("arena", "ffn")
    RA_ATT = ("arena", "att")
    RQ = ("cat", "qkv")
    RC = ("cat", "c")
    RM = ("cat", "mix")

    def pcol(name, l, k=None):
        o, w = PL.cols[f"{name}{l}"]
        if k is None:
            return prm[:, o:o + w]
        return prm[:, o + k:o + k + 1]

    def PS(b, n=512, p0=0, p1=128):
        return psum[p0:p1, b, 0:n]

    wstate = dict(n=0)

    def wtile(l, idx, length):
        n = wstate["n"]
        wstate["n"] += 1
        s = n % NS
        L = length
        src = wt[l, idx, :, 0:L].rearrange("p (a b) -> p a b", b=256)
        dst = ring[:, s, 0:L].rearrange("p (a b) -> p a b", b=256)
        P.add("pool", lambda E, dst=dst, src=src: E.dma_start(out=dst, in_=src),
              r=(), w=(("wslot", s), ("wt", n)), dma=f"w{s}", wtile=n)
        return ring[:, s, :], (("wt", n), ("wslot", s))

    def setup():
        P.add("sp", lambda E: E.dma_start(out=prm[:], in_=par), w=(("prm",),), dma="ld0")
        P.add("sp", lambda E: E.dma_start(out=identf[:], in_=c_ident), w=(("identf",),), dma="ld1")
        P.add("sp", lambda E: E.dma_start(out=dp[:].rearrange("p a b -> p (a b)"), in_=c_dp), w=(("dp",),), dma="ld2")
        P.add("sp", lambda E: E.dma_start(out=dc[:], in_=c_dc), w=(("dc",),), dma="ld3")
        P.add("sp", lambda E: E.dma_start(out=dn[:], in_=c_dn), w=(("dn",),), dma="ld4")
        P.add("dve", lambda E: E.tensor_copy(out=identb[:], in_=identf[:]), r=(("identf",),), w=(("identb",),))
        P.add("dve", lambda E: E.memset(onesb[:], 1.0), w=(("onesb",),))
        P.add("dve", lambda E: E.memset(eps1[:], LN_EPS), w=(("epsT",),))
        P.add("dve", lambda E: E.memset(eps4[:], 4.0 * LN_EPS), w=(("epsT",),))
        for l in range(2):
            P.add("dve", lambda E, l=l: E.tensor_scalar(out=nsink[:, l, :], in0=pcol("sink_b", l), scalar1=-1.0,
                                                         scalar2=None, op0=ALU.mult),
                  r=(("prm",),), w=(("nsink", l),))
            P.add("act", lambda E, l=l: E.activation(out=esT[:, l, :], in_=pcol("sinkp", l), func=AF.Exp),
                  r=(("prm",),), w=(("esT", l),))
            P.add("dve", lambda E, l=l: E.tensor_scalar(out=bq8[:, l, :], in0=pcol("bq", l), scalar1=0.125,
                                                         scalar2=None, op0=ALU.mult),
                  r=(("prm",),), w=(("bq8", l),))
        import os
        for l in range(0 if os.environ.get('KNOCOPY', '0') != '1' else 2, 2):
            P.add("sp", lambda E, l=l: E.dma_start(out=nks[l, :, 0:124, :], in_=ck[l, :, 4:128, :]), dma=f"out0a{l}")
            P.add("sp", lambda E, l=l: E.dma_start(out=nvs[l, :, 0:124, :], in_=cv[l, :, 4:128, :]), dma=f"out0b{l}")
            P.add("sp", lambda E, l=l: E.dma_start(out=ncs[l, :, 0:26, :], in_=scv[l, :, 4:30, :]), dma=f"out0c{l}")

    def load_x(st):
        R = ("scr", "xio")
        stage = sv(0, [128, 4, D], F32)
        import os
        for tb in range(int(os.environ.get("KNTB", "9"))):
            buf = tb % 4
            if tb < 8:
                rows = 128
                src = xp[st * TP + tb * 128: st * TP + (tb + 1) * 128, :]
                col0 = tb * 128
            else:
                rows = 32
                src = xs[st * TS:(st + 1) * TS, :]
                col0 = TP
            P.add("sp", lambda E, src=src, buf=buf, rows=rows: E.dma_start(out=stage[0:rows, buf, :], in_=src),
                  w=(("stage", buf, 0), ("stage", buf, 1)), dma=f"ldx{buf}", regions=(R,))
            banks = (2 * (tb % 4), 2 * (tb % 4) + 1)

            def tr(E, buf=buf, rows=rows, banks=banks):
                ins = None
                for k in range(8):
                    b = banks[k // 4]
                    ins = E.transpose(psum[:, b, (k % 4) * 128:(k % 4) * 128 + rows],
                                      stage[0:rows, buf, k * 128:(k + 1) * 128], identf[0:rows, 0:rows])
                return ins
            P.add("pe", tr, r=(("stage", buf, 0), ("stage", buf, 1), ("identf",)), w=(("ps", banks[0]), ("ps", banks[1])), regions=(R,))
            for hb in range(2):
                b = banks[hb]
                src_ps = psum[:, b, :].rearrange("p (k t) -> p k t", t=128)[:, :, 0:rows]
                wx = tuple(("x32", k, u) for k in range(hb * 4, hb * 4 + 4) for u in range(3))
                wb = tuple(("xb", k, u) for k in range(hb * 4, hb * 4 + 4) for u in range(3))
                if hb == 0:
                    P.add("act", lambda E, src_ps=src_ps, hb=hb, col0=col0, rows=rows:
                          E.activation(out=x32[:, hb * 4:(hb + 1) * 4, col0:col0 + rows], in_=src_ps, func=AF.Identity),
                          r=(("ps", b),), w=wx)
                    P.add("act", lambda E, src_ps=src_ps, hb=hb, col0=col0, rows=rows:
                          E.activation(out=xb[:, hb * 4:(hb + 1) * 4, col0:col0 + rows], in_=src_ps, func=AF.Identity),
                          r=(("ps", b),), w=wb)
                else:
                    P.add("dve", lambda E, src_ps=src_ps, hb=hb, col0=col0, rows=rows:
                          E.tensor_copy(out=x32[:, hb * 4:(hb + 1) * 4, col0:col0 + rows], in_=src_ps),
                          r=(("ps", b),), w=wx)
                    P.add("dve", lambda E, src_ps=src_ps, hb=hb, col0=col0, rows=rows:
                          E.tensor_copy(out=xb[:, hb * 4:(hb + 1) * 4, col0:col0 + rows], in_=src_ps),
                          r=(("ps", b),), w=wb)

    def store_y(st):
        R = ("scr", "xio")
        stage = sv(0, [128, 4, D], F32)
        for tb in range(9):
            buf = tb % 4
            if tb < 8:
                rows = 128
                dst = yp[st * TP + tb * 128: st * TP + (tb + 1) * 128, :]
                col0 = tb * 128
            else:
                rows = 32
                dst = ys[st * TS:(st + 1) * TS, :]
                col0 = TP
            banks = (2 * (tb % 4), 2 * (tb % 4) + 1)

            def tr(E, rows=rows, banks=banks, col0=col0):
                ins = None
                for k in range(8):
                    b = banks[k // 4]
                    ins = E.transpose(psum[0:rows, b, (k % 4) * 128:(k % 4 + 1) * 128],
                                      x32[:, k, col0:col0 + rows], identf[:, :])
                return ins
            P.add("pe", tr, r=tuple(("x32", k, u) for k in range(8) for u in range(3)) + (("identf",),),
                  w=(("ps", banks[0]), ("ps", banks[1])))
            for hb in range(2):
                b = banks[hb]
                eng = "act" if hb == 0 else "dve"
                if eng == "act":
                    P.add("act", lambda E, b=b, hb=hb, buf=buf, rows=rows:
                          E.activation(out=stage[0:rows, buf, hb * 512:(hb + 1) * 512], in_=psum[0:rows, b, :], func=AF.Identity),
                          r=(("ps", b),), w=(("stage", buf, hb),), regions=(R,))
                else:
                    P.add("dve", lambda E, b=b, hb=hb, buf=buf, rows=rows:
                          E.tensor_copy(out=stage[0:rows, buf, hb * 512:(hb + 1) * 512], in_=psum[0:rows, b, :]),
                          r=(("ps", b),), w=(("stage", buf, hb),), regions=(R,))
            P.add("sp", lambda E, dst=dst, buf=buf, rows=rows: E.dma_start(out=dst, in_=stage[0:rows, buf, :]),
                  r=(("stage", buf, 0), ("stage", buf, 1)), dma="out1*", regions=(R,))

    LNR = ("scr", "ln")
    ln_zb = sv(0, [128, 4, 512], BF16)
    ln_sq = sv(4096, [128, 4, 512], BF16)
    ln_zs2 = sv(8192, [128, 2, 64], BF16)
    ln_mt = sv(8704, [128, 512], F32)
    ln_vt = sv(10752, [128, 512], F32)
    ln_bt = sv(12800, [128, 512], F32)
    ln_cnt = dict(n=0)
    LN_BANKS = {0: (3, 4), 1: (5, 6)}

    def ln_stats_chunk(src, srckey, k, mode, units=(0, 1, 2)):
        R = LNR
        XR = (RA_ATT, RC) if mode == "cs" else ()
        for u, (off, n) in enumerate(UNITS):
            if u not in units:
                continue
            if u < 2:
                rb = ln_cnt["n"] % 4
                ln_cnt["n"] += 1
                b1, b2 = LN_BANKS[u]
                P.add("act", lambda E, k=k, rb=rb, off=off, n=n:
                      E.activation(out=ln_sq[:, rb, 0:n], in_=src[:, k, off:off + n], func=AF.Square),
                      r=((srckey, k, u),), w=(("lnsq", rb),), regions=(R,) + XR)
                P.add("dve", lambda E, k=k, rb=rb, off=off, n=n:
                      E.tensor_copy(out=ln_zb[:, rb, 0:n], in_=src[:, k, off:off + n]),
                      r=((srckey, k, u),), w=(("lnzb", rb),), regions=(R,) + XR)

                def mm(E, k=k, rb=rb, n=n, b1=b1, b2=b2):
                    E.matmul(PS(b1, n), lhsT=onesb[:, :], rhs=ln_zb[:, rb, 0:n], start=(k == 0), stop=(k == 7))
                    return E.matmul(PS(b2, n), lhsT=onesb[:, :], rhs=ln_sq[:, rb, 0:n], start=(k == 0), stop=(k == 7))
                P.add("pe", mm, r=(("lnzb", rb), ("lnsq", rb), ("onesb",)), w=(("ps", b1), ("ps", b2)), regions=(R,))
            else:
                rb = k % 2
                P.add("act", lambda E, k=k, rb=rb, off=off, n=n:
                      E.activation(out=ln_zs2[:, rb, 32:64], in_=src[:, k, off:off + n], func=AF.Square),
                      r=((srckey, k, u),), w=(("lnzs2q", rb),), regions=(R,) + XR)
                P.add("dve", lambda E, k=k, rb=rb, off=off, n=n:
                      E.tensor_copy(out=ln_zs2[:, rb, 0:32], in_=src[:, k, off:off + n]),
                      r=((srckey, k, u),), w=(("lnzs2z", rb),), regions=(R,) + XR)
                P.add("pe", lambda E, k=k, rb=rb:
                      E.matmul(psum[:, 7, 0:64], lhsT=onesb[:, :], rhs=ln_zs2[:, rb, :], start=(k == 0), stop=(k == 7)),
                      r=(("lnzs2q", rb), ("lnzs2z", rb), ("onesb",)), w=(("ps", 7),), regions=(R,))

    def ln_finish(src, srckey, eps, gname, bname, l, mode, units=(0, 1, 2)):
        R = LNR
        XR = (RA_ATT, RC) if mode == "cs" else ()
        epsT = eps1 if abs(eps - LN_EPS) < 1e-12 else eps4
        mt, vt, bt = ln_mt, ln_vt, ln_bt
        for u, (off, n) in enumerate(UNITS):
            if u not in units:
                continue
            if u < 2:
                b1, b2 = LN_BANKS[u]
                s1, s2 = PS(b1, n), PS(b2, n)
            else:
                b1 = b2 = 7
                s1, s2 = psum[:, 7, 0:32], psum[:, 7, 32:64]
            P.add("dve", lambda E, n=n, s1=s1: E.tensor_scalar(out=mt[:, 0:n], in0=s1, scalar1=1.0 / D, scalar2=None, op0=ALU.mult),
                  r=(("ps", b1),), w=(("lnm",),), regions=(R,))
            P.add("dve", lambda E, n=n: E.tensor_tensor(out=bt[:, 0:n], in0=mt[:, 0:n], in1=mt[:, 0:n], op=ALU.mult),
                  r=(("lnm",),), w=(("lnb",),), regions=(R,))
            P.add("dve", lambda E, n=n, s2=s2: E.scalar_tensor_tensor(out=vt[:, 0:n], in0=s2, scalar=1.0 / D, in1=bt[:, 0:n],
                                                                      op0=ALU.mult, op1=ALU.subtract),
                  r=(("ps", b2), ("lnb",)), w=(("lnv",),), regions=(R,))
            P.add("act", lambda E, n=n: E.activation(out=bt[:, 0:n], in_=vt[:, 0:n], func=AF.Ln, bias=epsT[:, 0:1]),
                  r=(("lnv",), ("epsT",)), w=(("lnb",),), regions=(R,))
            P.add("act", lambda E, n=n: E.activation(out=vt[:, 0:n], in_=bt[:, 0:n], func=AF.Exp, scale=-0.5),
                  r=(("lnb",),), w=(("lnv",),), regions=(R,))
            for k in range(8):
                P.add("dve", lambda E, k=k, off=off, n=n:
                      E.tensor_tensor(out=src[:, k, off:off + n], in0=src[:, k, off:off + n], in1=mt[:, 0:n], op=ALU.subtract),
                      r=((srckey, k, u), ("lnm",)), w=((srckey, k, u),), regions=(R,) + XR)
                P.add("dve", lambda E, k=k, off=off, n=n:
                      E.tensor_tensor(out=src[:, k, off:off + n], in0=src[:, k, off:off + n], in1=vt[:, 0:n], op=ALU.mult),
                      r=((srckey, k, u), ("lnv",)), w=((srckey, k, u),), regions=(R,) + XR)
                if mode == "x":
                    P.add("act", lambda E, k=k, off=off, n=n:
                          E.activation(out=xb[:, k, off:off + n], in_=src[:, k, off:off + n], func=AF.Identity,
                                       scale=pcol(gname, l, k), bias=pcol(bname, l, k)),
                          r=((srckey, k, u), ("prm",)), w=(("xb", k, u),))
                    P.add("act", lambda E, k=k, off=off, n=n:
                          E.activation(out=x32[:, k, off:off + n], in_=src[:, k, off:off + n], func=AF.Identity,
                                       scale=pcol(gname, l, k), bias=pcol(bname, l, k)),
                          r=((srckey, k, u), ("prm",)), w=(("x32", k, u),))
                else:
                    P.add("act", lambda E, k=k, off=off, n=n:
                          E.activation(out=csT[:, k, off:off + n], in_=src[:, k, off:off + n], func=AF.Silu,
                                       scale=pcol(gname, l, k), bias=pcol(bname, l, k)),
                          r=((srckey, k, u), ("prm",)), w=(("cs", k, u),), regions=(RA_ATT, RC))

    def layer_norm(src, srckey, eps, gname, bname, l, mode):
        for k in range(8):
            ln_stats_chunk(src, srckey, k, mode)
        ln_finish(src, srckey, eps, gname, bname, l, mode)

    def dense_ui(lhs_fn, rhs_t, rhs_key, wkeys, banks, regions=()):
        def mm(E):
            ins = None
            for k in range(8):
                lt = lhs_fn(k)
                for u, (off, n) in enumerate(UNITS):
                    ins = E.matmul(PS(banks[u], n), lhsT=lt, rhs=rhs_t[:, k, off:off + n], start=(k == 0), stop=(k == 7))
            return ins
        P.add("pe", mm, r=tuple(wkeys) + tuple((rhs_key, k, u) for k in range(8) for u in range(3)),
              w=tuple(("ps", banks[u]) for u in range(3)), regions=regions)

    def ffn(l, which, tile0):
        R = ("scr", "ffn")
        sa = sv(0, [128, 4, 512], BF16)
        cnt = 0
        for t in range(11):
            wv, wk = wtile(l, tile0 + t, 4096)
            w4 = wv.rearrange("p (k s c) -> p k s c", k=8, s=2, c=256)
            if t >= 1:
                for jj in range(2):
                    j = 2 * t + jj
                    dense_ui(lambda k, w4=w4, jj=jj: w4[:, k, 0, jj * 128:(jj + 1) * 128], xb, "xb", wk, (0, 1, 2))
                    dense_ui(lambda k, w4=w4, jj=jj: w4[:, k, 1, jj * 128:(jj + 1) * 128], xb, "xb", wk, (3, 4, 5))
                    for u, (off, n) in enumerate(UNITS):
                        rb = cnt % 4
                        cnt += 1
                        P.add("act", lambda E, u=u, rb=rb, n=n: E.activation(out=sa[:, rb, 0:n], in_=PS(u, n), func=AF.Silu),
                              r=(("ps", u),), w=(("sa", rb),), regions=(R,))
                        P.add("dve", lambda E, u=u, rb=rb, n=n, j=j, off=off:
                              E.tensor_tensor(out=gT[:, j, off:off + n], in0=PS(3 + u, n), in1=sa[:, rb, 0:n], op=ALU.mult),
                              r=(("ps", 3 + u), ("sa", rb)), w=(("g", j, u),), regions=(R, RA_FFN))
                continue
            for u, (off, n) in enumerate(UNITS):
                for jj in range(2):
                    j = 2 * t + jj
                    pb = (cnt % 3) * 2
                    rb = cnt % 4
                    cnt += 1

                    def mm(E, w4=w4, jj=jj, off=off, n=n, pb=pb):
                        ins = None
                        for s in range(2):
                            for k in range(8):
                                ins = E.matmul(PS(pb + s, n), lhsT=w4[:, k, s, jj * 128:(jj + 1) * 128],
                                               rhs=xb[:, k, off:off + n], start=(k == 0), stop=(k == 7))
                        return ins
                    P.add("pe", mm, r=wk + tuple(("xb", k, u) for k in range(8)), w=(("ps", pb), ("ps", pb + 1)))
                    P.add("act", lambda E, pb=pb, rb=rb, n=n: E.activation(out=sa[:, rb, 0:n], in_=PS(pb, n), func=AF.Silu),
                          r=(("ps", pb),), w=(("sa", rb),), regions=(R,))
                    P.add("dve", lambda E, pb=pb, rb=rb, n=n, j=j, off=off:
                          E.tensor_tensor(out=gT[:, j, off:off + n], in0=PS(pb + 1, n), in1=sa[:, rb, 0:n], op=ALU.mult),
                          r=(("ps", pb + 1), ("sa", rb)), w=(("g", j, u),), regions=(R, RA_FFN))
        ti = tile0 + 11
        for op_ in range(4):
            wv0, wk0 = wtile(l, ti, 2816); ti += 1
            wv1, wk1 = wtile(l, ti, 2816); ti += 1
            wh = [wv0[:, 0:2816].rearrange("p (j c) -> p j c", c=256), wv1[:, 0:2816].rearrange("p (j c) -> p j c", c=256)]
            for o2 in range(2):
                oc = op_ * 2 + o2
                base = 0

                if oc < 7:
                    def mm(E, wh=wh, o2=o2, base=base):
                        ins = None
                        for j in range(NJ):
                            h, jl = j // 11, j % 11
                            for u, (off, n) in enumerate(UNITS):
                                ins = E.matmul(PS(base + u, n), lhsT=wh[h][:, jl, o2 * 128:(o2 + 1) * 128],
                                               rhs=gT[:, j, off:off + n], start=(j == 0), stop=(j == NJ - 1))
                        return ins
                    P.add("pe", mm, r=wk0 + wk1 + tuple(("g", j, u) for j in range(NJ) for u in range(3)),
                          w=tuple(("ps", base + u) for u in range(3)), regions=(RA_FFN,))
                else:
                    gn, bn = ("ln1_g", "ln1_b") if which == 1 else ("ln3_g", "ln3_b")

                    def mm_u(u, wh=wh, o2=o2, base=base):
                        off, n = UNITS[u]

                        def mm(E):
                            ins = None
                            for j in range(NJ):
                                h, jl = j // 11, j % 11
                                ins = E.matmul(PS(base + u, n), lhsT=wh[h][:, jl, o2 * 128:(o2 + 1) * 128],
                                               rhs=gT[:, j, off:off + n], start=(j == 0), stop=(j == NJ - 1))
                            return ins
                        P.add("pe", mm, r=wk0 + wk1 + tuple(("g", j, u) for j in range(NJ)), w=(("ps", base + u),), regions=(RA_FFN,))

                    def ev_u(u, oc=oc, base=base):
                        off, n = UNITS[u]
                        P.add("dve", lambda E: E.scalar_tensor_tensor(out=x32[:, oc, off:off + n], in0=x32[:, oc, off:off + n], scalar=2.0 * ALPHA,
                                                                      in1=PS(base + u, n), op0=ALU.mult, op1=ALU.add),
                              r=(("ps", base + u), ("x32", oc, u)), w=(("x32", oc, u),))
                    fa = (x32, "x32", 4.0 * LN_EPS, gn, bn, l, "x")
                    mm_u(0); ln_stats_chunk(x32, "x32", 6, "x"); ev_u(0)
                    mm_u(1); ln_stats_chunk(x32, "x32", 7, "x", units=(0,)); ev_u(1)
                    mm_u(2); ln_stats_chunk(x32, "x32", 7, "x", units=(1,)); ev_u(2)
                    ln_stats_chunk(x32, "x32", 7, "x", units=(2,)); ln_finish(*fa)
                    continue
                if oc > 0:
                    ln_stats_chunk(x32, "x32", oc - 1, "x")
                for u, (off, n) in enumerate(UNITS):
                    P.add("dve", lambda E, oc=oc, off=off, n=n, base=base, u=u:
                          E.scalar_tensor_tensor(out=x32[:, oc, off:off + n], in0=x32[:, oc, off:off + n], scalar=2.0 * ALPHA,
                                                 in1=PS(base + u, n), op0=ALU.mult, op1=ALU.add),
                          r=(("ps", base + u), ("x32", oc, u)), w=(("x32", oc, u),))

    def state_dma(st, l, grp):
        P.add("sp", lambda E: E.dma_start(out=sstage[0:120, :], in_=scv[l, st * NSEQ + grp * 4: st * NSEQ + grp * 4 + 4].rearrange("b r c -> (b r) c")),
              w=(("sstage",),), dma="lds2")

    def qkv_attention(st, l, tile0):
        R = ("scr", "att")
        state_dma(st, l, 0)
        if st == 0:
            pass
        else:
            P.add("pool", lambda E: E.tensor_copy(out=kT[:, :, 0:128], in_=kcarry[:, l, :, :]),
                  r=(("kcarry", l),), w=(("kprev",),), regions=(RA_ATT, RQ))
            P.add("pool", lambda E: E.tensor_copy(out=vtok[:, 0, :], in_=vcarry[:, l, :]),
                  r=(("vcarry", l),), w=(("v", 0),), regions=(RA_ATT, RQ))
        cnt = 0
        for tq in range(2):
            wv, wk = wtile(l, tile0 + tq, 4096)
            w3 = wv.rearrange("p (k c) -> p k c", k=8)
            if tq == 1:
                for cl in range(4):
                    c = tq * 4 + cl
                    bk = tuple((cl % 2) * 3 + u for u in range(3))
                    dense_ui(lambda k, w3=w3, cl=cl: w3[:, k, cl * 128:(cl + 1) * 128], xb, "xb", wk, bk)
                    for u, (off, n) in enumerate(UNITS):
                        P.add("act", lambda E, c=c, off=off, n=n, b=bk[u]:
                              E.activation(out=qT[:, c, off:off + n], in_=PS(b, n), func=AF.Identity, scale=0.125, bias=bq8[:, l, c:c + 1]),
                              r=(("ps", bk[u]), ("bq8", l)), w=(("q", c, u),), regions=(RA_ATT, RQ))
                continue
            for u, (off, n) in enumerate(UNITS):
                for cl in range(4):
                    c = tq * 4 + cl
                    pb = cnt % 4
                    cnt += 1

                    def mm(E, w3=w3, cl=cl, off=off, n=n, pb=pb):
                        ins = None
                        for k in range(8):
                            ins = E.matmul(PS(pb, n), lhsT=w3[:, k, cl * 128:(cl + 1) * 128], rhs=xb[:, k, off:off + n],
                                           start=(k == 0), stop=(k == 7))
                        return ins
                    P.add("pe", mm, r=wk + tuple(("xb", k, u) for k in range(8)), w=(("ps", pb),))
                    P.add("act", lambda E, c=c, off=off, n=n, pb=pb:
                          E.activation(out=qT[:, c, off:off + n], in_=PS(pb, n), func=AF.Identity, scale=0.125, bias=bq8[:, l, c:c + 1]),
                          r=(("ps", pb), ("bq8", l)), w=(("q", c, u),), regions=(RA_ATT, RQ))
        wv, wk = wtile(l, tile0 + 2, 4096)
        w3 = wv.rearrange("p (k c) -> p k c", k=8)
        for g2 in range(2):
            bk = tuple((g2 % 2) * 3 + u for u in range(3))
            dense_ui(lambda k, g2=g2: w3[:, k, g2 * 128:(g2 + 1) * 128], xb, "xb", wk, bk)
            for u, (off, n) in enumerate(UNITS):
                P.add("act", lambda E, g2=g2, off=off, n=n, b=bk[u]:
                      E.activation(out=kT[:, g2, 128 + off:128 + off + n], in_=PS(b, n), func=AF.Identity,
                                   bias=pcol("bk", l, g2)),
                      r=(("ps", bk[u]), ("prm",)), w=(("k", g2, u),), regions=(RA_ATT, RQ))
        ostage = sv(0, [128, 2, 512], F32)
        o_bkv, _ = PL.cols[f"bkv_b{l}"]
        for tb in range(9):
            rows = 128 if tb < 8 else TS
            col0 = tb * 128 if tb < 8 else TP
            need_k = (tb == 8) or (st == 1 and tb == 7)
            pb = 4 + (tb % 2)
            ncol = 512 if need_k else 256
            c0 = 0 if need_k else 256

            def mm(E, col0=col0, rows=rows, pb=pb, c0=c0, ncol=ncol):
                ins = None
                for k in range(8):
                    ins = E.matmul(psum[0:rows, pb, c0:c0 + ncol], lhsT=xb[:, k, col0:col0 + rows], rhs=w3[:, k, c0:c0 + ncol],
                                   start=(k == 0), stop=(k == 7))
                return ins
            P.add("pe", mm, r=wk + tuple(("xb", k, u) for k in range(8) for u in range(3)), w=(("ps", pb),))
            vdst = vtok[:, 1 + tb, :] if tb < 8 else sb_vn[:, :]
            vkey = ("v", 1 + tb) if tb < 8 else ("vn2",)
            P.add("dve", lambda E, rows=rows, pb=pb, vdst=vdst:
                  E.tensor_tensor(out=vdst[0:rows], in0=psum[0:rows, pb, 256:512], in1=prm[0:rows, o_bkv + 256:o_bkv + 512], op=ALU.add),
                  r=(("ps", pb), ("prm",)), w=(vkey,), regions=(RA_ATT, R, RQ))
            if need_k:
                ob = tb % 2
                P.add("dve", lambda E, rows=rows, pb=pb, ob=ob:
                      E.tensor_tensor(out=ostage[0:rows, ob, :], in0=psum[0:rows, pb, 0:512], in1=prm[0:rows, o_bkv:o_bkv + 512], op=ALU.add),
                      r=(("ps", pb), ("prm",)), w=(("ostage", ob),), regions=(R,))
                if tb == 7:
                    P.add("sp", lambda E, ob=ob: E.dma_start(out=nkp[l], in_=ostage[:, ob, 0:256]), r=(("ostage", ob),), dma="out1*", regions=(R,))
                    P.add("sp", lambda E, ob=ob: E.dma_start(out=nvp[l], in_=ostage[:, ob, 256:512]), r=(("ostage", ob),), dma="out1*", regions=(R,))
                else:
                    for (dst_t, c_0) in ((nks, 0), (nvs, 256)):
                        for b in range(NSEQ):
                            P.add("sp", lambda E, ob=ob, dst_t=dst_t, c_0=c_0, b=b:
                                  E.dma_start(out=dst_t[l, st * NSEQ + b, 124:128, :], in_=ostage[4 * b:4 * b + 4, ob, c_0:c_0 + 256]),
                                  r=(("ostage", ob),), dma="out1*", regions=(R,))
        if st == 0:
            P.add("pool", lambda E: E.tensor_copy(out=kcarry[:, l, :, :], in_=kT[:, :, TP:TP + 128]),
                  r=tuple(("k", g2, 1) for g2 in range(2)), w=(("kcarry", l),), regions=(RA_ATT, RQ))
            P.add("pool", lambda E: E.tensor_copy(out=vcarry[:, l, :], in_=vtok[:, 8, :]),
                  r=(("v", 8),), w=(("vcarry", l),), regions=(RA_ATT, RQ))
        d1 = sv(0, [128, 2, 512], F32)
        etab = sv(4096, [128, 16, 2, 128], BF16)
        ptb = sv(12288, [128, 2, 2, 512], BF16)
        praw = sv(16384, [128, 2, 2, 512], BF16)
        for h in range(16):
            P.add("act", lambda E, h=h: E.activation(out=etab[:, h, :, :].rearrange("p a b -> p (a b)"),
                                                     in_=dp[:, :, :].rearrange("p a b -> p (a b)"), func=AF.Exp, scale=float(SLOPES[h])),
                  r=(("dp",),), w=(("etab", h),), regions=(R,))
        iters = [(qb, g) for qb in range(8) for g in range(4)]

        def blks_of(qb):
            return (0, 1) if (st * 8 + qb) > 0 else (1,)

        def emit_scores(n):
            qb, g = iters[n]
            g2, g1 = g // 2, g % 2
            sb0 = (n % 2) * 2
            blks = blks_of(qb)

            def mm(E):
                ins = None
                for blk in blks:
                    kcol = qb * 128 + blk * 128
                    ins = E.matmul(PS(sb0 + blk), lhsT=kT[g1 * 64:(g1 + 1) * 64, g2, kcol:kcol + 128],
                                   rhs=qT[g1 * 64:(g1 + 1) * 64, g2 * 4:(g2 + 1) * 4, qb * 128:(qb + 1) * 128],
                                   start=True, stop=True)
                return ins
            ku = 0 if qb < 4 else 1
            kreads = [("k", g2, ku)]
            if len(blks) == 2:
                kreads.append(("kprev",) if qb == 0 else ("k", g2, 0 if (qb - 1) < 4 else 1))
            P.add("pe", mm, r=tuple(kreads) + tuple(("q", g2 * 4 + i, ku) for i in range(4)),
                  w=tuple(("ps", sb0 + blk) for blk in blks), regions=(RA_ATT, RQ))

        def emit_softmax(n):
            qb, g = iters[n]
            buf = n % 2
            sb0 = buf * 2
            for blk in blks_of(qb):
                P.add("act", lambda E, buf=buf, blk=blk, sb0=sb0:
                      E.activation(out=praw[:, buf, blk, :], in_=PS(sb0 + blk), func=AF.Exp),
                      r=(("ps", sb0 + blk),), w=(("praw", buf, blk),), regions=(R,))
                P.add("dve", lambda E, buf=buf, blk=blk, g=g:
                      E.tensor_tensor(out=ptb[:, buf, blk, :].rearrange("p (i q) -> p i q", q=128),
                                      in0=praw[:, buf, blk, :].rearrange("p (i q) -> p i q", q=128),
                                      in1=etab[:, g * 4:(g + 1) * 4, blk, :], op=ALU.mult),
                      r=(("praw", buf, blk),) + tuple(("etab", g * 4 + i) for i in range(4)), w=(("pt", buf, blk),), regions=(R,))

        def emit_pv(n):
            qb, g = iters[n]
            g2, g1 = g // 2, g % 2
            buf = n % 2
            blks = blks_of(qb)

            def pv(E):
                ins = None
                for bi, blk in enumerate(blks):
                    ins = E.matmul(psum[g1 * 64:(g1 + 1) * 64, 4 + g2, :], lhsT=vtok[:, qb + blk, g * 64:(g + 1) * 64],
                                   rhs=ptb[:, buf, blk, :], start=(bi == 0), stop=(bi == len(blks) - 1))
                for bi, blk in enumerate(blks):
                    ins = E.matmul(psum[g1 * 64:(g1 + 1) * 64, 6 + g2, :], lhsT=onesb[:, 0:64],
                                   rhs=ptb[:, buf, blk, :], start=(bi == 0), stop=(bi == len(blks) - 1))
                return ins
            P.add("pe", pv, r=tuple(("pt", buf, blk) for blk in blks) + tuple(("v", qb + blk) for blk in blks) + (("onesb",),),
                  w=(("ps", 4 + g2), ("ps", 6 + g2)), regions=(RA_ATT, R, RQ))

        def emit_evac(qb, g2):
            if True:
                P.add("dve", lambda E, g2=g2:
                      E.tensor_tensor(out=d1[:, g2, :].rearrange("p (i q) -> p i q", q=128),
                                      in0=psum[:, 6 + g2, :].rearrange("p (i q) -> p i q", q=128),
                                      in1=esT[:, l, g2 * 4:(g2 + 1) * 4].unsqueeze(2).to_broadcast([128, 4, 128]), op=ALU.add),
                      r=(("ps", 6 + g2), ("esT", l)), w=(("d1", g2), ("ostage", 0), ("ostage", 1)), regions=(R,))
                P.add("act", lambda E, g2=g2: E.activation(out=d1[:, g2, :], in_=d1[:, g2, :], func=AF.Ln),
                      r=(("d1", g2),), w=(("d1", g2),), regions=(R,))
                P.add("act", lambda E, g2=g2: E.activation(out=d1[:, g2, :], in_=d1[:, g2, :], func=AF.Exp, scale=-1.0),
                      r=(("d1", g2),), w=(("d1", g2),), regions=(R,))
                P.add("dve", lambda E, g2=g2, qb=qb:
                      E.tensor_tensor(out=attnT[:, g2 * 4:(g2 + 1) * 4, qb * 128:(qb + 1) * 128],
                                      in0=psum[:, 4 + g2, :].rearrange("p (i q) -> p i q", q=128),
                                      in1=d1[:, g2, :].rearrange("p (i q) -> p i q", q=128), op=ALU.mult),
                      r=(("ps", 4 + g2), ("d1", g2)),
                      w=tuple(("at", g2 * 4 + i, 0 if qb < 4 else 1) for i in range(4)),
                      regions=(R, RA_ATT))

        emit_scores(0)
        for n in range(len(iters)):
            if n + 1 < len(iters):
                emit_scores(n + 1)
            emit_softmax(n)
            emit_pv(n)
            if n >= 1 and iters[n - 1][1] in (1, 3):
                emit_evac(iters[n - 1][0], iters[n - 1][1] // 2)
        emit_evac(iters[-1][0], iters[-1][1] // 2)
        RS = ("scr", "satt")
        kcs = sv(0, [128, NSEQ, 256], BF16)
        vcs = sv(4096, [128, NSEQ, 256], BF16)
        kcT = sv(8192, [128, 2, NSEQ, 128], BF16)
        scc = sv(12288, [128, NSEQ, 4, 4, 4], F32)
        scn = sv(14336, [128, 16, 32], F32)
        pcs = sv(16384, [128, NSEQ, 4, 4, 4], BF16)
        pns = sv(17408, [128, 16, 32], BF16)
        tA = sv(18432, [128, 256], F32)
        tD = sv(19456, [128, 256], F32)
        vn2 = sb_vn
        P.add("pool", lambda E: E.dma_start(out=kcs[:], in_=ck[l, st * NSEQ:(st + 1) * NSEQ].rearrange("b k c -> k b c")),
              w=(("kcs",),), dma="ldk", regions=(RS,))
        P.add("pool", lambda E: E.dma_start(out=vcs[:], in_=cv[l, st * NSEQ:(st + 1) * NSEQ].rearrange("b k c -> k b c")),
              w=(("vcs",),), dma="ldv", regions=(RS,))
        pbf = psum[:, 0:2, :].rearrange("p a b -> p (a b)").bitcast(BF16)

        def trk(E):
            ins = None
            for g2 in range(2):
                for b in range(NSEQ):
                    idx = g2 * NSEQ + b
                    ins = E.transpose(pbf[:, idx * 128:(idx + 1) * 128], kcs[:, b, g2 * 128:(g2 + 1) * 128], identb[:, :])
            return ins
        P.add("pe", trk, r=(("kcs",), ("identb",)), w=(("ps", 0), ("ps", 1)), regions=(RS,))
        for g2 in range(2):
            P.add("act", lambda E, g2=g2: E.activation(out=kcT[:, g2].rearrange("p b k -> p (b k)"), in_=pbf[:, g2 * 1024:(g2 + 1) * 1024], func=AF.Identity),
                  r=(("ps", g2),), w=(("kcT", g2),), regions=(RS,))

        def sc_c(E):
            ins = None
            for g1 in range(2):
                for b in range(NSEQ):
                    for g2 in range(2):
                        o = (b * 2 + g2) * 16
                        ins = E.matmul(psum[:, 2 + g1, o:o + 16], lhsT=kcT[g1 * 64:(g1 + 1) * 64, g2, b, :],
                                       rhs=qT[g1 * 64:(g1 + 1) * 64, g2 * 4:(g2 + 1) * 4, TP + b * 4:TP + b * 4 + 4],
                                       start=True, stop=True)
            return ins
        P.add("pe", sc_c, r=(("kcT", 0), ("kcT", 1)) + tuple(("q", c, 2) for c in range(8)), w=(("ps", 2), ("ps", 3)), regions=(RS, RA_ATT, RQ))

        def sc_n(E):
            ins = None
            for g1 in range(2):
                for g2 in range(2):
                    for i in range(4):
                        o = (g2 * 4 + i) * 32
                        ins = E.matmul(psum[0:TS, 4 + g1, o:o + 32], lhsT=kT[g1 * 64:(g1 + 1) * 64, g2, 128 + TP:128 + TP + TS],
                                       rhs=qT[g1 * 64:(g1 + 1) * 64, g2 * 4 + i, TP:TP + TS], start=True, stop=True)
            return ins
        P.add("pe", sc_n, r=tuple(("k", g2, 2) for g2 in range(2)) + tuple(("q", c, 2) for c in range(8)), w=(("ps", 4), ("ps", 5)),
              regions=(RS, RA_ATT, RQ))
        for g in range(4):
            g2, g1 = g // 2, g % 2
            ps_c = psum[:, 2 + g1, 0:256].rearrange("p (b g i t) -> p b g i t", b=NSEQ, g=2, i=4)
            for i in range(4):
                h = head_of(g, i)
                P.add("dve", lambda E, g=g, g2=g2, i=i, h=h, ps_c=ps_c:
                      E.scalar_tensor_tensor(out=scc[:, :, g, i, :], in0=dc[:, :].unsqueeze(1).to_broadcast([128, NSEQ, 4]), scalar=float(SLOPES[h]),
                                             in1=ps_c[:, :, g2, i, :], op0=ALU.mult, op1=ALU.add),
                      r=(("ps", 2 + g1), ("dc",)), w=(("scc", h),), regions=(RS,))
                P.add("act", lambda E, g=g, i=i, h=h:
                      E.activation(out=pcs[:, :, g, i, :], in_=scc[:, :, g, i, :], func=AF.Exp, bias=nsink[:, l, h:h + 1]),
                      r=(("scc", h), ("nsink", l)), w=(("pcs", h),), regions=(RS,))
                o = (g2 * 4 + i) * 32
                P.add("dve", lambda E, h=h, g1=g1, o=o:
                      E.scalar_tensor_tensor(out=scn[0:TS, h, :], in0=dn[0:TS, :], scalar=float(SLOPES[h]),
                                             in1=psum[0:TS, 4 + g1, o:o + 32], op0=ALU.mult, op1=ALU.add),
                      r=(("ps", 4 + g1), ("dn",)), w=(("scn", h),), regions=(RS,))
                P.add("act", lambda E, h=h:
                      E.activation(out=pns[0:TS, h, :], in_=scn[0:TS, h, :], func=AF.Exp, bias=nsink[0:TS, l, h:h + 1]),
                      r=(("scn", h), ("nsink", l)), w=(("pns", h),), regions=(RS,))
        def pv_s(E):
            ins = None
            for g in range(4):
                g2, g1 = g // 2, g % 2
                for i in range(4):
                    h = head_of(g, i)
                    c = g2 * 4 + i
                    ins = E.matmul(psum[g1 * 64:(g1 + 1) * 64, 0, c * 32:(c + 1) * 32], lhsT=vn2[0:TS, g * 64:(g + 1) * 64],
                                   rhs=pns[0:TS, h, :], start=True, stop=True)
                    ins = E.matmul(psum[g1 * 64:(g1 + 1) * 64, 1, c * 32:(c + 1) * 32], lhsT=onesb[0:TS, 0:64],
                                   rhs=pns[0:TS, h, :], start=True, stop=True)
            for b in range(NSEQ):
                for g in range(4):
                    g2, g1 = g // 2, g % 2
                    o = 256 + (g2 * NSEQ + b) * 16
                    ins = E.matmul(psum[g1 * 64:(g1 + 1) * 64, 0, o:o + 16], lhsT=vcs[:, b, g * 64:(g + 1) * 64],
                                   rhs=pcs[:, b, g, :, :], start=True, stop=True)
                    ins = E.matmul(psum[g1 * 64:(g1 + 1) * 64, 1, o:o + 16], lhsT=onesb[:, 0:64],
                                   rhs=pcs[:, b, g, :, :], start=True, stop=True)
            return ins
        P.add("pe", pv_s, r=tuple(("pns", h) for h in range(16)) + tuple(("pcs", h) for h in range(16)) + (("vn2",), ("vcs",), ("onesb",)), w=(("ps", 0), ("ps", 1)), regions=(RS,))
        P.add("act", lambda E: E.activation(out=tA[:, :], in_=psum[:, 0, 0:256], func=AF.Identity), r=(("ps", 0),), w=(("tA",),), regions=(RS,))
        P.add("act", lambda E: E.activation(out=tD[:, :], in_=psum[:, 1, 0:256], func=AF.Identity, bias=1.0), r=(("ps", 1),), w=(("tD",),), regions=(RS,))
        for g2 in range(2):
            nb = psum[:, 0, 256 + g2 * 128:256 + (g2 + 1) * 128].rearrange("p (b i t) -> p i b t", b=NSEQ, i=4)
            db = psum[:, 1, 256 + g2 * 128:256 + (g2 + 1) * 128].rearrange("p (b i t) -> p i b t", b=NSEQ, i=4)
            ta = tA[:, g2 * 128:(g2 + 1) * 128].rearrange("p (i b t) -> p i b t", i=4, b=NSEQ)
            td = tD[:, g2 * 128:(g2 + 1) * 128].rearrange("p (i b t) -> p i b t", i=4, b=NSEQ)
            P.add("dve", lambda E, nb=nb, ta=ta: E.tensor_tensor(out=ta, in0=nb, in1=ta, op=ALU.add), r=(("ps", 0), ("tA",)), w=(("tA",),), regions=(RS,))
            P.add("dve", lambda E, db=db, td=td: E.tensor_tensor(out=td, in0=db, in1=td, op=ALU.add), r=(("ps", 1), ("tD",)), w=(("tD",),), regions=(RS,))
        P.add("dve", lambda E: E.reciprocal(out=tD[:, :], in_=tD[:, :]), r=(("tD",),), w=(("tD",),), regions=(RS,))
        P.add("dve", lambda E: E.tensor_tensor(out=attnT[:, :, TP:TP + TS], in0=tA[:, :].rearrange("p (c n) -> p c n", n=TS),
                                               in1=tD[:, :].rearrange("p (c n) -> p c n", n=TS), op=ALU.mult),
              r=(("tA",), ("tD",)), w=tuple(("at", c, 2) for c in range(8)), regions=(RS, RA_ATT))

    sb_vn = sb("sb_vn", [128, 256], BF16)

    def glu_conv(st, l, tile0):
        R = ("scr", "glu")
        sg = sv(0, [128, 4, 512], F32)
        ostg = sv(8192, [128, D], F32)
        ow, _ = PL.cols[f"convw{l}"]
        for grp in range(2):
            if grp == 1:
                state_dma(st, l, 1)

            def tr(E):
                ins = None
                for k in range(8):
                    b = 6 + k // 4
                    ins = E.transpose(psum[:, b, (k % 4) * 128:(k % 4) * 128 + 120], sstage[0:120, k * 128:(k + 1) * 128], identf[0:120, 0:120])
                return ins
            P.add("pe", tr, r=(("sstage",), ("identf",)), w=(("ps", 6), ("ps", 7)), regions=(R,))
            for hb in range(2):
                src_ps = psum[:, 6 + hb, :].rearrange("p (k t) -> p k t", t=128)[:, :, 0:120].rearrange("p k (b r) -> p k b r", r=30)
                P.add("act", lambda E, src_ps=src_ps, hb=hb, grp=grp:
                      E.activation(out=uexts[:, hb * 4:(hb + 1) * 4, grp * 4:(grp + 1) * 4, 0:30], in_=src_ps, func=AF.Identity),
                      r=(("ps", 6 + hb),), w=tuple(("uexts", k) for k in range(hb * 4, hb * 4 + 4)))
        for cb in range(2):
            pass
        cntb = dict(n=0)
        wts = {}

        def chunk(c, phase):
            t, cl = c // 2, c % 2
            ub = c % 2
            if t not in wts:
                wv, wk_ = wtile(l, tile0 + t, 4096)
                wts[t] = (wv.rearrange("p (k s c) -> p k s c", k=8, s=2, c=256), wk_)
            w4, wk = wts[t]
            if phase == 'A':
                if st == 0:
                    P.add("pool", lambda E, ub=ub: E.memset(uextb[:, ub, 0:30], 0.0), w=(("uext", ub, "pre"),))
                else:
                    P.add("pool", lambda E, ub=ub, c=c: E.tensor_copy(out=uextb[:, ub, 0:30], in_=ucarry[:, l, c, :]),
                          r=(("ucarry", l, c),), w=(("uext", ub, "pre"),))
                abank = (0, 1, 2)
                gbank = (3, 6, 7)
                dense_ui(lambda k, w4=w4, cl=cl: w4[:, k, 0, cl * 128:(cl + 1) * 128], xb, "xb", wk, abank)
                dense_ui(lambda k, w4=w4, cl=cl: w4[:, k, 1, cl * 128:(cl + 1) * 128], xb, "xb", wk, gbank)
                for u, (off, n) in enumerate(UNITS):
                    rb = cntb['n'] % 4
                    cntb['n'] += 1
                    pa, pg = abank[u], gbank[u]
                    P.add("act", lambda E, pg=pg, rb=rb, n=n, c=c:
                          E.activation(out=sg[:, rb, 0:n], in_=PS(pg, n), func=AF.Sigmoid, bias=pcol("bgg", l, c)),
                          r=(("ps", pg), ("prm",)), w=(("sg", rb),), regions=(R,))
                    if u < 2:
                        dst = uextb[:, ub, 30 + off:30 + off + n]
                        src1 = sg[:, rb, 0:n]
                        src0 = PS(pa, n)
                        wkey = ("uext", ub, u)
                    else:
                        dst = uexts[:, c, :, 30:34]
                        src1 = sg[:, rb, 0:n].rearrange("p (b t) -> p b t", t=4)
                        src0 = PS(pa, n).rearrange("p (b t) -> p b t", t=4)
                        wkey = ("uexts_new", c)
                    P.add("dve", lambda E, dst=dst, src0=src0, src1=src1, c=c:
                          E.scalar_tensor_tensor(out=dst, in0=src0, scalar=pcol("bga", l, c), in1=src1, op0=ALU.add, op1=ALU.mult),
                          r=(("ps", pa), ("sg", rb), ("prm",)), w=(wkey,), regions=(R,))
                    if st == 1 and u == 1:
                        P.add("dve", lambda E, pa=pa, rb=rb, n=n, c=c:
                              E.scalar_tensor_tensor(out=u32tail[:, c, :], in0=psum[:, pa, n - 30:n], scalar=pcol("bga", l, c),
                                                     in1=sg[:, rb, n - 30:n], op0=ALU.add, op1=ALU.mult),
                              r=(("ps", pa), ("sg", rb), ("prm",)), w=(("u32tail", c),), regions=(R,))
            if phase == 'DG':
                P.add("dve", lambda E, c=c:
                      E.tensor_tensor(out=dg[:, :, :], in0=identb[:, :].unsqueeze(1).to_broadcast([128, 31, 128]),
                                      in1=prm[:, ow + c * 31:ow + c * 31 + 31].unsqueeze(2).to_broadcast([128, 31, 128]), op=ALU.mult),
                      r=(("identb",), ("prm",)), w=(("dg",),))
            if phase == 'B':
                ukeys = (("uext", ub, "pre"), ("uext", ub, 0), ("uext", ub, 1))
                for u in range(2):
                    off = u * 512

                    def cmm(E, ub=ub, off=off, u=u):
                        ins = None
                        for j in range(31):
                            ins = E.matmul(PS(4 + u), lhsT=dg[:, j, :], rhs=uextb[:, ub, off + j:off + j + 512],
                                           start=(j == 0), stop=(j == 30))
                        return ins
                    P.add("pe", cmm, r=ukeys + (("dg",),), w=(("ps", 4 + u),))
                    P.add("act", lambda E, c=c, off=off, u=u:
                          E.activation(out=cT[:, c, off:off + 512], in_=PS(4 + u), func=AF.Identity, bias=pcol("conv_b", l, c)),
                          r=(("ps", 4 + u), ("prm",)), w=(("c", c, u),), regions=(RA_ATT, RC))

                def tap_s(j, c=c):
                    dstv = (cT[:, c, TP:TP + TS] if j % 2 == 0 else cacc[:, 0:TS]).rearrange("p (b t) -> p b t", t=4)
                    akey = (("c", c, 2),) if j % 2 == 0 else (("cacc", 1),)
                    wj = prm[:, ow + c * 31 + j:ow + c * 31 + j + 1]
                    if j == 0:
                        f = lambda E: E.tensor_scalar(out=dstv, in0=uexts[:, c, :, 0:4], scalar1=wj, scalar2=pcol("conv_b", l, c),
                                                      op0=ALU.mult, op1=ALU.add)
                        rk = ()
                    elif j == 1:
                        f = lambda E: E.tensor_scalar(out=dstv, in0=uexts[:, c, :, 1:5], scalar1=wj, scalar2=None, op0=ALU.mult)
                        rk = ()
                    else:
                        f = lambda E: E.scalar_tensor_tensor(out=dstv, in0=uexts[:, c, :, j:j + 4], scalar=wj, in1=dstv,
                                                             op0=ALU.mult, op1=ALU.add)
                        rk = akey
                    P.add("dve", f, r=(("uexts", c), ("uexts_new", c), ("prm",)) + rk, w=akey, regions=(RA_ATT, RC))
                for j in range(31):
                    tap_s(j)
                P.add("dve", lambda E, c=c: E.tensor_tensor(out=cT[:, c, TP:TP + TS], in0=cT[:, c, TP:TP + TS], in1=cacc[:, 0:TS], op=ALU.add),
                      r=(("c", c, 2), ("cacc", 1)), w=(("c", c, 2),), regions=(RA_ATT, RC))
                if st == 0:
                    P.add("pool", lambda E, ub=ub, c=c: E.tensor_copy(out=ucarry[:, l, c, :], in_=uextb[:, ub, TP:TP + 30]),
                          r=(("uext", ub, 1),), w=(("ucarry", l, c),))
                P.add("pool", lambda E, c=c: E.tensor_copy(out=sv(12288, [128, 8, TS], F32)[:, c, :].rearrange("p (b t) -> p b t", t=4),
                                                           in_=uexts[:, c, :, 30:34]),
                      r=(("uexts_new", c),), w=(("unew", c),), regions=(R,))


        chunk(0, 'A'); chunk(0, 'DG')
        for c in range(8):
            if c + 1 < 8:
                chunk(c + 1, 'A')
            chunk(c, 'B')
            if c + 1 < 8:
                chunk(c + 1, 'DG')
        unew = sv(12288, [128, 8, TS], F32)
        if st == 1:
            def trp(E):
                ins = None
                for c in range(8):
                    ins = E.transpose(psum[0:30, 4 + c // 4, (c % 4) * 128:(c % 4 + 1) * 128], u32tail[:, c, :], identf[:, :])
                return ins
            P.add("pe", trp, r=tuple(("u32tail", c) for c in range(8)) + (("identf",),), w=(("ps", 4), ("ps", 5)), regions=(R,))
            for hb in range(2):
                P.add("act", lambda E, hb=hb: E.activation(out=ostg[0:30, hb * 512:(hb + 1) * 512], in_=psum[0:30, 4 + hb, :], func=AF.Identity),
                      r=(("ps", 4 + hb),), w=(("ostg",),), regions=(R,))
            P.add("sp", lambda E: E.dma_start(out=ncp[l], in_=ostg[0:30, :]), r=(("ostg",),), dma="out1*", regions=(R,))

        def trs(E):
            ins = None
            for c in range(8):
                ins = E.transpose(psum[0:TS, 6 + c // 4, (c % 4) * 128:(c % 4 + 1) * 128], unew[:, c, :], identf[:, :])
            return ins
        P.add("pe", trs, r=tuple(("unew", c) for c in range(8)) + (("identf",),), w=(("ps", 6), ("ps", 7)), regions=(R,))
        ostg2 = sv(13312, [128, D], F32)
        for hb in range(2):
            P.add("act", lambda E, hb=hb: E.activation(out=ostg2[0:TS, hb * 512:(hb + 1) * 512], in_=psum[0:TS, 6 + hb, :], func=AF.Identity),
                  r=(("ps", 6 + hb),), w=(("ostg2",),), regions=(R,))
        for b in range(NSEQ):
            P.add("sp", lambda E, b=b: E.dma_start(out=ncs[l, st * NSEQ + b, 26:30, :], in_=ostg2[4 * b:4 * b + 4, :]),
                  r=(("ostg2",),), dma="out1*", regions=(R,))
        layer_norm(cT, "c", LN_EPS, "cln_g", "cln_b", l, "cs")

    def out_proj(l, tile0):
        R = ("scr", "op")
        sga = sv(0, [128, T], F32)
        sgc = sv(4224, [128, T], F32)
        m1 = sv(8448, [128, T], F32)
        m2 = sv(12672, [128, T], F32)
        for pr in range(4):
            wA, kA = wtile(l, tile0 + pr * 2 + 0, 4096)
            wC, kC = wtile(l, tile0 + pr * 2 + 1, 4096)
            vA = wA.rearrange("p (k s c) -> p k s c", k=8, s=2, c=256)
            vC = wC.rearrange("p (k s c) -> p k s c", k=8, s=2, c=256)
            for cl in range(2):
                c = pr * 2 + cl
                dense_ui(lambda k, vA=vA, cl=cl: vA[:, k, 0, cl * 128:(cl + 1) * 128], xb, "xb", kA, (0, 1, 2))
                for u, (off, n) in enumerate(UNITS):
                    P.add("act", lambda E, u=u, off=off, n=n, c=c:
                          E.activation(out=sga[:, off:off + n], in_=PS(u, n), func=AF.Sigmoid, bias=pcol("bta", l, c)),
                          r=(("ps", u), ("prm",)), w=(("sga", u),), regions=(R,))
                dense_ui(lambda k, vA=vA, cl=cl: vA[:, k, 1, cl * 128:(cl + 1) * 128], attnT, "at", kA, (3, 4, 5), regions=(RA_ATT,))
                for u, (off, n) in enumerate(UNITS):
                    P.add("dve", lambda E, u=u, off=off, n=n:
                          E.tensor_tensor(out=m1[:, off:off + n], in0=PS(3 + u, n), in1=sga[:, off:off + n], op=ALU.mult),
                          r=(("ps", 3 + u), ("sga", u)), w=(("m1", u),), regions=(R,))
                dense_ui(lambda k, vC=vC, cl=cl: vC[:, k, 0, cl * 128:(cl + 1) * 128], xb, "xb", kC, (0, 1, 2))
                for u, (off, n) in enumerate(UNITS):
                    P.add("act", lambda E, u=u, off=off, n=n, c=c:
                          E.activation(out=sgc[:, off:off + n], in_=PS(u, n), func=AF.Sigmoid, bias=pcol("btc", l, c)),
                          r=(("ps", u), ("prm",)), w=(("sgc", u),), regions=(R,))
                dense_ui(lambda k, vC=vC, cl=cl: vC[:, k, 1, cl * 128:(cl + 1) * 128], csT, "cs", kC, (3, 4, 5), regions=(RA_ATT,))
                for u, (off, n) in enumerate(UNITS):
                    P.add("dve", lambda E, u=u, off=off, n=n:
                          E.tensor_tensor(out=m2[:, off:off + n], in0=PS(3 + u, n), in1=sgc[:, off:off + n], op=ALU.mult),
                          r=(("ps", 3 + u), ("sgc", u)), w=(("m2", u),), regions=(R,))
                    P.add("dve", lambda E, u=u, off=off, n=n, c=c:
                          E.tensor_tensor(out=mixT[:, c, off:off + n], in0=m1[:, off:off + n], in1=m2[:, off:off + n], op=ALU.add),
                          r=(("m1", u), ("m2", u)), w=(("mix", c, u),), regions=(R, RA_ATT, RM))
        for tq in range(2):
            wv, wk = wtile(l, tile0 + 8 + tq, 4096)
            w3 = wv.rearrange("p (k c) -> p k c", k=8)
            for cl in range(4):
                c = tq * 4 + cl

                if c == 7:
                    def mm_u(u, w3=w3, cl=cl):
                        off, n = UNITS[u]

                        def mm(E):
                            ins = None
                            for k in range(8):
                                ins = E.matmul(PS(u, n), lhsT=w3[:, k, cl * 128:(cl + 1) * 128], rhs=mixT[:, k, off:off + n],
                                               start=(k == 0), stop=(k == 7))
                            return ins
                        P.add("pe", mm, r=wk + tuple(("mix", k, u) for k in range(8)), w=(("ps", u),), regions=(RA_ATT, RM))

                    def ev_u(u, c=c):
                        off, n = UNITS[u]
                        P.add("dve", lambda E: E.scalar_tensor_tensor(out=x32[:, c, off:off + n], in0=x32[:, c, off:off + n], scalar=ALPHA,
                                                                      in1=PS(u, n), op0=ALU.mult, op1=ALU.add),
                              r=(("ps", u), ("x32", c, u)), w=(("x32", c, u),))
                    fa = (x32, "x32", LN_EPS, "ln2_g", "ln2_b", l, "x")
                    mm_u(0); ln_stats_chunk(x32, "x32", 6, "x"); ev_u(0)
                    mm_u(1); ln_stats_chunk(x32, "x32", 7, "x", units=(0,)); ev_u(1)
                    mm_u(2); ln_stats_chunk(x32, "x32", 7, "x", units=(1,)); ev_u(2)
                    ln_stats_chunk(x32, "x32", 7, "x", units=(2,)); ln_finish(*fa)
                    continue

                def mm(E, w3=w3, cl=cl):
                    ins = None
                    for k in range(8):
                        for u, (off, n) in enumerate(UNITS):
                            ins = E.matmul(PS(u, n), lhsT=w3[:, k, cl * 128:(cl + 1) * 128], rhs=mixT[:, k, off:off + n],
                                           start=(k == 0), stop=(k == 7))
                    return ins
                P.add("pe", mm, r=wk + tuple(("mix", k, u) for k in range(8) for u in range(3)), w=tuple(("ps", u) for u in range(3)),
                      regions=(RA_ATT, RM))
                if c > 0:
                    ln_stats_chunk(x32, "x32", c - 1, "x")
                for u, (off, n) in enumerate(UNITS):
                    P.add("dve", lambda E, c=c, off=off, n=n, u=u:
                          E.scalar_tensor_tensor(out=x32[:, c, off:off + n], in0=x32[:, c, off:off + n], scalar=ALPHA,
                                                 in1=PS(u, n), op0=ALU.mult, op1=ALU.add),
                          r=(("ps", u), ("x32", c, u)), w=(("x32", c, u),))

    def dump(name, src):
        if not DEBUG:
            return
        if src.dtype == F32:
            P.add("sp", lambda E: E.dma_start(out=dbg[name], in_=src), r=tuple((kk, k, u) for kk in ("x32", "c") for k in range(8) for u in range(3)), dma="out1*")

    import os
    STOP = int(os.environ.get("KSTOP", "99"))
    NOCOPY = os.environ.get("KNOCOPY", "0") == "1"
    setup()
    for st in range(int(os.environ.get('KNST', NST))):
        if STOP >= -1:
            load_x(st)
        for l in range(2):
            if STOP >= 1:
                ffn(l, 1, 0)
            if STOP >= 2:
                qkv_attention(st, l, 19)
            if STOP >= 3:
                glu_conv(st, l, 22)
            if STOP >= 4:
                out_proj(l, 26)
            if STOP >= 5:
                ffn(l, 2, 36)
        if STOP >= 0:
            store_y(st)
    print('SBUF bytes remaining:', nc.sbuf_bytes_remaining, 'ops:', len(P.ops))
    P.finalize()
    P.emit(nc)
    es.close()
    return nc


_CACHE = {}


def kernel(**inp):
    inp = {k: np.asarray(v) for k, v in inp.items()}
    WT = pack_weights(inp)
    PAR = pack_params(inp)
    C = pack_consts()
    if "nc" not in _CACHE:
        _CACHE["nc"] = build_program()
    nc = _CACHE["nc"]
    in_maps = []
    for c in range(NCORES):
        in_maps.append({
            "xp": np.ascontiguousarray(inp["x_prompt"][c]),
            "xs": np.ascontiguousarray(inp["x_sample"][c * 16:(c + 1) * 16].reshape(64, D)),
            "ck": np.ascontiguousarray(inp["cache_k"][:, c * 16:(c + 1) * 16].reshape(2, 16, 128, 256)),
            "cv": np.ascontiguousarray(inp["cache_v"][:, c * 16:(c + 1) * 16].reshape(2, 16, 128, 256)),
            "scv": np.ascontiguousarray(inp["state_conv"][:, c * 16:(c + 1) * 16]),
            "wt": WT, "par": PAR,
            "c_ident": C["ident"], "c_dp": C["dp"], "c_dc": C["dc"], "c_dn": C["dn"],
        })
    res = run_bass_kernel_spmd(nc, in_maps, core_ids=list(range(NCORES)))
    R = res.results
    y_prompt = np.stack([R[c]["yp"] for c in range(NCORES)], 0)
    y_sample = np.concatenate([R[c]["ys"].reshape(16, 4, D) for c in range(NCORES)], 0)
    nkp = np.stack([R[c]["nkp"].reshape(2, 128, 4, 64) for c in range(NCORES)], 1)
    nvp = np.stack([R[c]["nvp"].reshape(2, 128, 4, 64) for c in range(NCORES)], 1)
    ncp = np.stack([R[c]["ncp"] for c in range(NCORES)], 1)
    nks = np.concatenate([R[c]["nks"].reshape(2, 16, 128, 4, 64) for c in range(NCORES)], 1)
    nvs = np.concatenate([R[c]["nvs"].reshape(2, 16, 128, 4, 64) for c in range(NCORES)], 1)
    ncs = np.concatenate([R[c]["ncs"] for c in range(NCORES)], 1)
    _CACHE["last"] = R
    return (y_prompt.astype(np.float32), y_sample.astype(np.float32), nkp.astype(np.float32), nvp.astype(np.float32),
            ncp.astype(np.float32), nks.astype(np.float32), nvs.astype(np.float32), ncs.astype(np.float32))
```

```python
import os
import numpy as np
import concourse.bass as bass
import concourse.mybir as mybir
from concourse.bass_utils import run_bass_kernel_spmd

F32 = mybir.dt.float32
BF16 = mybir.dt.bfloat16
ALU = mybir.AluOpType
AF = mybir.ActivationFunctionType

NCORES = 8
D = 1024
DFF = 2816
DIN = 5632
NJ = 22
TP = 1024
TS = 32
T = TP + TS
NST = 2
NSEQ = 8
UNITS = [(0, 512), (512, 512), (1024, 32)]
ALPHA = 4.0 ** 0.25
LN_EPS = 1e-5
NEG = -1.0e6
SLOPES = [2.0 ** (-8.0 * (h + 1) / 16.0) for h in range(16)]
NS = 4
SLOT = 4096
TILES_PER_LAYER = 55
DEBUG = False


def chunk_heads(c):
    g2, i = c // 4, c % 4
    return (2 * g2) * 4 + i, (2 * g2 + 1) * 4 + i


def head_of(g, i):
    return g * 4 + i


class PL:
    cols = {}
    n = 0

    @classmethod
    def add(cls, name, w):
        cls.cols[name] = (cls.n, w)
        cls.n += w


for _l in range(2):
    for nm in ["ln1_g", "ln1_b", "ln2_g", "ln2_b", "ln3_g", "ln3_b", "cln_g", "cln_b", "conv_b",
               "bq", "bga", "bgg", "bta", "btc"]:
        PL.add(f"{nm}{_l}", 8)
    PL.add(f"bk{_l}", 2)
    PL.add(f"convw{_l}", 8 * 31)
    PL.add(f"bkv_b{_l}", 512)
    PL.add(f"sink_b{_l}", 16)
    PL.add(f"sinkp{_l}", 8)
NPAR = PL.n


def pack_params(inp):
    P = np.zeros((128, NPAR), np.float32)

    def pm(v):
        return np.ascontiguousarray(v.reshape(-1, 128).T)

    for l in range(2):
        def put(name, arr):
            o, w = PL.cols[f"{name}{l}"]
            assert arr.shape == (128, w), (name, arr.shape, w)
            P[:, o:o + w] = arr
        put("ln1_g", pm(inp["ln1_g"][l])); put("ln1_b", pm(inp["ln1_b"][l]))
        put("ln2_g", pm(inp["ln2_g"][l])); put("ln2_b", pm(inp["ln2_b"][l]))
        put("ln3_g", pm(inp["ln3_g"][l])); put("ln3_b", pm(inp["ln3_b"][l]))
        put("cln_g", pm(inp["conv_ln_g"][l])); put("cln_b", pm(inp["conv_ln_b"][l]))
        put("conv_b", pm(inp["conv_dw_b"][l]))
        b = inp["b_in"][l]
        bq = b[0:1024].reshape(16, 64)
        bqp = np.zeros((128, 8), np.float32)
        for c in range(8):
            ha, hb = chunk_heads(c)
            bqp[0:64, c] = bq[ha]
            bqp[64:128, c] = bq[hb]
        put("bq", bqp)
        put("bk", pm(b[1024:1280]))
        put("bga", pm(b[1536:2560])); put("bgg", pm(b[2560:3584]))
        put("bta", pm(b[3584:4608])); put("btc", pm(b[4608:5632]))
        cw = inp["conv_dw_w"][l]
        put("convw", np.ascontiguousarray(cw.reshape(31, 8, 128).transpose(2, 1, 0).reshape(128, 248)))
        put("bkv_b", np.broadcast_to(b[1024:1536][None, :], (128, 512)).copy())
        put("sink_b", np.broadcast_to(inp["attn_sinks"][l][None, :], (128, 16)).copy())
        sp_ = np.zeros((128, 8), np.float32)
        for c in range(8):
            ha, hb = chunk_heads(c)
            sp_[0:64, c] = inp["attn_sinks"][l][ha]
            sp_[64:128, c] = inp["attn_sinks"][l][hb]
        put("sinkp", sp_)
    return P


def pack_consts():
    C = {}
    C["ident"] = np.eye(128, dtype=np.float32)
    c = np.arange(128)[:, None]
    r = np.arange(128)[None, :]
    d_prev = 128 + r - c
    d_cur = r - c
    Dp = np.zeros((128, 2, 128), np.float32)
    Dp[:, 0, :] = np.where(c >= r, -d_prev, NEG)
    Dp[:, 1, :] = np.where(c <= r, -d_cur, NEG)
    C["dp"] = Dp.reshape(128, 256)
    t = np.arange(4)[None, :]
    Dc = np.where(c >= t, -(128 + t - c), NEG).astype(np.float32)
    C["dc"] = np.ascontiguousarray(Dc)
    kb, kt = np.divmod(np.arange(32), 4)
    Dn = np.where((kb[:, None] == kb[None, :]) & (kt[:, None] <= kt[None, :]),
                  -(kt[None, :] - kt[:, None]), NEG).astype(np.float32)
    Dn_full = np.zeros((128, 32), np.float32)
    Dn_full[0:32] = Dn
    C["dn"] = Dn_full
    return C


def pack_weights(inp):
    WT = np.zeros((2, TILES_PER_LAYER, 128, SLOT), np.float32)

    def kp(w):
        return w.reshape(-1, 128, w.shape[1]).transpose(1, 0, 2)

    for l in range(2):
        ti = 0

        def put(arr):
            nonlocal ti
            a = arr.reshape(128, -1)
            WT[l, ti, :, :a.shape[1]] = a
            ti += 1

        def ffn(up, down):
            upk = kp(up)
            for t in range(11):
                tile = np.stack([upk[:, :, t * 256:(t + 1) * 256],
                                 upk[:, :, 2816 + t * 256: 2816 + (t + 1) * 256]], axis=2)
                put(tile)
            dk = kp(down)
            for op in range(4):
                for h in range(2):
                    put(dk[:, h * 11:(h + 1) * 11, op * 256:(op + 1) * 256])

        ffn(inp["ffn1_up"][l], inp["ffn1_down"][l])
        win = kp(inp["w_in"][l])
        qcols = []
        for c in range(8):
            ha, hb = chunk_heads(c)
            qcols += list(range(ha * 64, ha * 64 + 64)) + list(range(hb * 64, hb * 64 + 64))
        wq = win[:, :, qcols]
        put(wq[:, :, 0:512]); put(wq[:, :, 512:1024])
        put(win[:, :, 1024:1536])
        for t in range(4):
            put(np.stack([win[:, :, 1536 + t * 256:1536 + (t + 1) * 256],
                          win[:, :, 2560 + t * 256:2560 + (t + 1) * 256]], axis=2))
        wa = inp["w_attn_out"][l]
        rows = []
        for kc in range(8):
            ha, hb = chunk_heads(kc)
            rows.append(np.concatenate([wa[ha * 64:ha * 64 + 64], wa[hb * 64:hb * 64 + 64]], axis=0))
        wap = np.stack(rows, axis=1)
        wc = kp(inp["w_conv_out"][l])
        for pr in range(4):
            put(np.stack([win[:, :, 3584 + pr * 256:3584 + (pr + 1) * 256], wap[:, :, pr * 256:(pr + 1) * 256]], axis=2))
            put(np.stack([win[:, :, 4608 + pr * 256:4608 + (pr + 1) * 256], wc[:, :, pr * 256:(pr + 1) * 256]], axis=2))
        wo = kp(inp["w_out"][l])
        put(wo[:, :, 0:512]); put(wo[:, :, 512:1024])
        ffn(inp["ffn2_up"][l], inp["ffn2_down"][l])
        assert ti == TILES_PER_LAYER, ti
    return WT


class Prog:
    ENG = ["pe", "act", "dve", "pool", "sp"]

    def __init__(self):
        self.ops = []
        self.rr = {}

    def add(self, eng, fn, r=(), w=(), dma=None, regions=(), wtile=None):
        if dma is not None and dma.endswith("*"):
            base = dma[:-1]
            n = self.rr.get(base, 0)
            self.rr[base] = n + 1
            dma = f"{base}_{n % 8}"
        self.ops.append(dict(eng=eng, fn=fn, r=tuple(r), w=tuple(w), dma=dma, regions=tuple(regions),
                             wtile=wtile))
        return len(self.ops) - 1

    def finalize(self):
        ops = self.ops
        last_reader = {}
        for i, op in enumerate(ops):
            for k in op["r"]:
                if k[0] == "wt":
                    last_reader[k[1]] = i
        loads = {}
        for i, op in enumerate(ops):
            if op["wtile"] is not None:
                loads.setdefault(op["wtile"], []).append(i)
        load_idx = set(i for v in loads.values() for i in v)
        after = {}
        head = []
        for n in sorted(loads):
            if n < NS:
                head += loads[n]
            else:
                after.setdefault(last_reader[n - NS], []).extend(loads[n])
        order = []
        first_nonload = True
        for i, op in enumerate(ops):
            if i in load_idx:
                continue
            if first_nonload:
                order += head
                first_nonload = False
            order.append(i)
            if i in after:
                order += after[i]
        assert len(order) == len(ops)
        self.order = order
        last_write = {}
        readers = {}
        region = {}
        pos = {}
        for p, i in enumerate(order):
            pos[i] = p
        deps_of = {}
        last_dma = {}
        for i in order:
            op = ops[i]
            deps = set()
            for k in op["r"]:
                if k in last_write:
                    deps.add(last_write[k])
                if k[0] == "ps":
                    for rr in readers.get(k, ()):
                        if ops[rr]["eng"] != op["eng"]:
                            deps.add(rr)
            for k in op["w"]:
                if k in last_write:
                    deps.add(last_write[k])
                deps.update(readers.get(k, ()))
            for (rn, ident) in op["regions"]:
                st = region.setdefault(rn, dict(ident=ident, users={}, barrier=set()))
                if st["ident"] != ident:
                    st["barrier"] = set(st["users"].values())
                    st["users"] = {}
                    st["ident"] = ident
                deps |= st["barrier"]
                ukey = op["dma"] if op["dma"] else op["eng"]
                st["users"][ukey] = i
            if op["dma"]:
                if op["dma"] in last_dma:
                    deps.add(last_dma[op["dma"]])
                last_dma[op["dma"]] = i
            deps.discard(i)
            for k in op["r"]:
                readers.setdefault(k, []).append(i)
            for k in op["w"]:
                last_write[k] = i
                readers[k] = []
            deps_of[i] = deps
        signal = set()
        for i in order:
            for d in deps_of[i]:
                signal.add(d)
        eng_count = {e: 0 for e in self.ENG}
        dma_count = {}
        sigval = {}
        for i in order:
            op = ops[i]
            if op["dma"]:
                dma_count[op["dma"]] = dma_count.get(op["dma"], 0) + 1
                sigval[i] = ("dma:" + op["dma"], 16 * dma_count[op["dma"]])
            elif i in signal:
                eng_count[op["eng"]] += 1
                sigval[i] = ("eng:" + op["eng"], eng_count[op["eng"]])
        self.dma_sems = sorted(dma_count)
        self.dma_total = {k: 16 * v for k, v in dma_count.items()}
        waited = {e: {} for e in self.ENG}
        for i in order:
            op = ops[i]
            e = op["eng"]
            need = {}
            for d in deps_of[i]:
                dop = ops[d]
                if (not dop["dma"]) and dop["eng"] == e and not op["dma"] and (e == "pe" or os.environ.get("KNOSELF", "0") == "1"):
                    continue
                sname, val = sigval[d]
                if need.get(sname, 0) < val:
                    need[sname] = val
            waits = []
            for sname, val in need.items():
                if waited[e].get(sname, 0) >= val:
                    continue
                waited[e][sname] = val
                waits.append((sname, val))
            op["waits"] = waits
            op["sig"] = sigval.get(i)
        self.maxcount = dict(eng_count)

    def emit(self, nc):
        import contextlib
        ops = self.ops
        sem_names = ["eng:" + e for e in self.ENG] + ["dma:" + s for s in self.dma_sems]
        with contextlib.ExitStack() as es:
            sems = {}
            for sn in sem_names:
                sems[sn] = es.enter_context(nc.semaphore(sn.replace(":", "_")))
            block = es.enter_context(nc.Block())

            def run(eng_name, E):
                for i in self.order:
                    op = ops[i]
                    if op["eng"] != eng_name:
                        continue
                    for (sname, val) in op["waits"]:
                        E.wait_ge(sems[sname], val)
                    ins = op["fn"](E)
                    if op["sig"] is not None:
                        sname, val = op["sig"]
                        ins.then_inc(sems[sname], 16 if op["dma"] else 1)
                if eng_name == "sp":
                    for s in self.dma_sems:
                        if s.startswith("out"):
                            E.wait_ge(sems["dma:" + s], self.dma_total[s])

            @block.tensor
            def _(e):
                run("pe", e)

            @block.scalar
            def _(e):
                run("act", e)

            @block.vector
            def _(e):
                run("dve", e)

            @block.gpsimd
            def _(e):
                run("pool", e)

            @block.sync
            def _(e):
                run("sp", e)


def build_program():
    import contextlib
    nc = bass.Bass("TRN2", target_bir_lowering=False)
    es = contextlib.ExitStack()
    P = Prog()

    def din(name, shape):
        return nc.dram_tensor(name, list(shape), F32, kind="ExternalInput").ap()

    def dout(name, shape):
        return nc.dram_tensor(name, list(shape), F32, kind="ExternalOutput").ap()

    xp = din("xp", [2048, D]); xs = din("xs", [64, D])
    ck = din("ck", [2, 16, 128, 256]); cv = din("cv", [2, 16, 128, 256])
    scv = din("scv", [2, 16, 30, D])
    wt = din("wt", [2, TILES_PER_LAYER, 128, SLOT])
    par = din("par", [128, NPAR])
    c_ident = din("c_ident", [128, 128]); c_dp = din("c_dp", [128, 256])
    c_dc = din("c_dc", [128, 4]); c_dn = din("c_dn", [128, 32])
    yp = dout("yp", [2048, D]); ys = dout("ys", [64, D])
    nkp = dout("nkp", [2, 128, 256]); nvp = dout("nvp", [2, 128, 256]); ncp = dout("ncp", [2, 30, D])
    nks = dout("nks", [2, 16, 128, 256]); nvs = dout("nvs", [2, 16, 128, 256]); ncs = dout("ncs", [2, 16, 30, D])
    dbg = {}
    if DEBUG:
        for nm in ["d_x1", "d_q", "d_attn", "d_cs", "d_x2", "d_x3"]:
            dbg[nm] = dout(nm, [128, 8, T])

    def sb(name, shape, dt):
        return es.enter_context(nc.sbuf_tensor(name, list(shape), dt))

    x32 = sb("x32", [128, 8, T], F32)
    xb = sb("xb", [128, 8, T], BF16)
    ring = sb("ring", [128, NS, SLOT], BF16)
    ARENA_B = 67584
    arena = sb("arena", [128, ARENA_B // 2], BF16)
    SCR_B = 20480
    scr = sb("scr", [128, SCR_B // 2], BF16)
    prm = sb("prm", [128, NPAR], F32)
    identf = sb("identf", [128, 128], F32)
    identb = sb("identb", [128, 128], BF16)
    onesb = sb("onesb", [128, 128], BF16)
    eps1 = sb("eps1", [128, 1], F32)
    eps4 = sb("eps4", [128, 1], F32)
    dp = sb("dp", [128, 2, 128], F32)
    dc = sb("dc", [128, 4], F32)
    dn = sb("dn", [128, 32], F32)
    nsink = sb("nsink", [128, 2, 16], F32)
    bq8 = sb("bq8", [128, 2, 8], F32)
    esT = sb("esT", [128, 2, 8], F32)
    uextb = sb("uextb", [128, 2, 30 + TP], BF16)
    dg = sb("dg", [128, 31, 128], BF16)
    u32tail = sb("u32tail", [128, 8, 30], F32)
    cacc = sb("cacc", [128, TS], F32)
    sstage = sb("sstage", [128, D], F32)
    uexts = sb("uexts", [128, 8, NSEQ, 34], F32)
    ucarry = sb("ucarry", [128, 2, 8, 30], BF16)
    kcarry = sb("kcarry", [128, 2, 2, 128], BF16)
    vcarry = sb("vcarry", [128, 2, 256], BF16)
    psum = es.enter_context(nc.psum_tensor("psum", [128, 8, 512], F32))

    def av(off_b, shape, dt):
        n = int(np.prod(shape[1:]))
        if dt == BF16:
            v = arena[:, off_b // 2: off_b // 2 + n]
        else:
            v = arena[:, off_b // 2: off_b // 2 + 2 * n].bitcast(F32)
        names = "abcdef"[:len(shape) - 1]
        if len(shape) > 2:
            kw = {names[i]: shape[i + 1] for i in range(len(shape) - 1)}
            v = v.rearrange("p (%s) -> p %s" % (" ".join(names), " ".join(names)), **kw)
        return v

    def sv(off_b, shape, dt):
        n = int(np.prod(shape[1:]))
        if dt == BF16:
            v = scr[:, off_b // 2: off_b // 2 + n]
        else:
            v = scr[:, off_b // 2: off_b // 2 + 2 * n].bitcast(F32)
        names = "abcdef"[:len(shape) - 1]
        if len(shape) > 2:
            kw = {names[i]: shape[i + 1] for i in range(len(shape) - 1)}
            v = v.rearrange("p (%s) -> p %s" % (" ".join(names), " ".join(names)), **kw)
        return v

    gT = av(0, [128, NJ, T], BF16)
    qT = av(0, [128, 8, T], BF16)
    mixT = qT
    kT = av(16896, [128, 2, 128 + T], BF16)
    vtok = av(21632, [128, 9, 256], BF16)
    cT = av(0, [128, 8, T], F32)
    attnT = av(33792, [128, 8, T], BF16)
    csT = av(50688, [128, 8, T], BF16)
    RA_FFN = ("arena", "ffn")
    RA_ATT = ("arena", "att")
    RQ = ("cat", "qkv")
    RC = ("cat", "c")
    RM = ("cat", "mix")

    def pcol(name, l, k=None):
        o, w = PL.cols[f"{name}{l}"]
        if k is None:
            return prm[:, o:o + w]
        return prm[:, o + k:o + k + 1]

    def PS(b, n=512, p0=0, p1=128):
        return psum[p0:p1, b, 0:n]

    wstate = dict(n=0)

    def wtile(l, idx, length):
        n = wstate["n"]
        wstate["n"] += 1
        s = n % NS
        L = length
        src = wt[l, idx, :, 0:L].rearrange("p (a b) -> p a b", b=256)
        dst = ring[:, s, 0:L].rearrange("p (a b) -> p a b", b=256)
        P.add("pool", lambda E, dst=dst, src=src: E.dma_start(out=dst, in_=src),
              r=(), w=(("wslot", s), ("wt", n)), dma=f"w{s}", wtile=n)
        return ring[:, s, :], (("wt", n), ("wslot", s))

    def setup():
        P.add("sp", lambda E: E.dma_start(out=prm[:], in_=par), w=(("prm",),), dma="ld0")
        P.add("sp", lambda E: E.dma_start(out=identf[:], in_=c_ident), w=(("identf",),), dma="ld1")
        P.add("sp", lambda E: E.dma_start(out=dp[:].rearrange("p a b -> p (a b)"), in_=c_dp), w=(("dp",),), dma="ld2")
        P.add("sp", lambda E: E.dma_start(out=dc[:], in_=c_dc), w=(("dc",),), dma="ld3")
        P.add("sp", lambda E: E.dma_start(out=dn[:], in_=c_dn), w=(("dn",),), dma="ld4")
        P.add("dve", lambda E: E.tensor_copy(out=identb[:], in_=identf[:]), r=(("identf",),), w=(("identb",),))
        P.add("dve", lambda E: E.memset(onesb[:], 1.0), w=(("onesb",),))
        P.add("dve", lambda E: E.memset(eps1[:], LN_EPS), w=(("epsT",),))
        P.add("dve", lambda E: E.memset(eps4[:], 4.0 * LN_EPS), w=(("epsT",),))
        for l in range(2):
            P.add("dve", lambda E, l=l: E.tensor_scalar(out=nsink[:, l, :], in0=pcol("sink_b", l), scalar1=-1.0,
                                                         scalar2=None, op0=ALU.mult),
                  r=(("prm",),), w=(("nsink", l),))
            P.add("act", lambda E, l=l: E.activation(out=esT[:, l, :], in_=pcol("sinkp", l), func=AF.Exp),
                  r=(("prm",),), w=(("esT", l),))
            P.add("dve", lambda E, l=l: E.tensor_scalar(out=bq8[:, l, :], in0=pcol("bq", l), scalar1=0.125,
                                                         scalar2=None, op0=ALU.mult),
                  r=(("prm",),), w=(("bq8", l),))
        import os
        for l in range(0 if os.environ.get('KNOCOPY', '0') != '1' else 2, 2):
            P.add("sp", lambda E, l=l: E.dma_start(out=nks[l, :, 0:124, :], in_=ck[l, :, 4:128, :]), dma=f"out0a{l}")
            P.add("sp", lambda E, l=l: E.dma_start(out=nvs[l, :, 0:124, :], in_=cv[l, :, 4:128, :]), dma=f"out0b{l}")
            P.add("sp", lambda E, l=l: E.dma_start(out=ncs[l, :, 0:26, :], in_=scv[l, :, 4:30, :]), dma=f"out0c{l}")

    def load_x(st):
        R = ("scr", "xio")
        stage = sv(0, [128, 4, D], F32)
        import os
        for tb in range(int(os.environ.get("KNTB", "9"))):
            buf = tb % 4
            if tb < 8:
                rows = 128
                src = xp[st * TP + tb * 128: st * TP + (tb + 1) * 128, :]
                col0 = tb * 128
            else:
                rows = 32
                src = xs[st * TS:(st + 1) * TS, :]
                col0 = TP
            P.add("sp", lambda E, src=src, buf=buf, rows=rows: E.dma_start(out=stage[0:rows, buf, :], in_=src),
                  w=(("stage", buf, 0), ("stage", buf, 1)), dma=f"ldx{buf}", regions=(R,))
            banks = (2 * (tb % 4), 2 * (tb % 4) + 1)

            def tr(E, buf=buf, rows=rows, banks=banks):
                ins = None
                for k in range(8):
                    b = banks[k // 4]
                    ins = E.transpose(psum[:, b, (k % 4) * 128:(k % 4) * 128 + rows],
                                      stage[0:rows, buf, k * 128:(k + 1) * 128], identf[0:rows, 0:rows])
                return ins
            P.add("pe", tr, r=(("stage", buf, 0), ("stage", buf, 1), ("identf",)), w=(("ps", banks[0]), ("ps", banks[1])), regions=(R,))
            for hb in range(2):
                b = banks[hb]
                src_ps = psum[:, b, :].rearrange("p (k t) -> p k t", t=128)[:, :, 0:rows]
                wx = tuple(("x32", k, u) for k in range(hb * 4, hb * 4 + 4) for u in range(3))
                wb = tuple(("xb", k, u) for k in range(hb * 4, hb * 4 + 4) for u in range(3))
                if hb == 0:
                    P.add("act", lambda E, src_ps=src_ps, hb=hb, col0=col0, rows=rows:
                          E.activation(out=x32[:, hb * 4:(hb + 1) * 4, col0:col0 + rows], in_=src_ps, func=AF.Identity),
                          r=(("ps", b),), w=wx)
                    P.add("act", lambda E, src_ps=src_ps, hb=hb, col0=col0, rows=rows:
                          E.activation(out=xb[:, hb * 4:(hb + 1) * 4, col0:col0 + rows], in_=src_ps, func=AF.Identity),
                          r=(("ps", b),), w=wb)
                else:
                    P.add("dve", lambda E, src_ps=src_ps, hb=hb, col0=col0, rows=rows:
                          E.tensor_copy(out=x32[:, hb * 4:(hb + 1) * 4, col0:col0 + rows], in_=src_ps),
                          r=(("ps", b),), w=wx)
                    P.add("dve", lambda E, src_ps=src_ps, hb=hb, col0=col0, rows=rows:
                          E.tensor_copy(out=xb[:, hb * 4:(hb + 1) * 4, col0:col0 + rows], in_=src_ps),
                          r=(("ps", b),), w=wb)

    def store_y(st):
        R = ("scr", "xio")
        stage = sv(0, [128, 4, D], F32)
        for tb in range(9):
            buf = tb % 4
            if tb < 8:
                rows = 128
                dst = yp[st * TP + tb * 128: st * TP + (tb + 1) * 128, :]
                col0 = tb * 128
            else:
                rows = 32
                dst = ys[st * TS:(st + 1) * TS, :]
                col0 = TP
            banks = (2 * (tb % 4), 2 * (tb % 4) + 1)

            def tr(E, rows=rows, banks=banks, col0=col0):
                ins = None
                for k in range(8):
                    b = banks[k // 4]
                    ins = E.transpose(psum[0:rows, b, (k % 4) * 128:(k % 4 + 1) * 128],
                                      x32[:, k, col0:col0 + rows], identf[:, :])
                return ins
            P.add("pe", tr, r=tuple(("x32", k, u) for k in range(8) for u in range(3)) + (("identf",),),
                  w=(("ps", banks[0]), ("ps", banks[1])))
            for hb in range(2):
                b = banks[hb]
                eng = "act" if hb == 0 else "dve"
                if eng == "act":
                    P.add("act", lambda E, b=b, hb=hb, buf=buf, rows=rows:
                          E.activation(out=stage[0:rows, buf, hb * 512:(hb + 1) * 512], in_=psum[0:rows, b, :], func=AF.Identity),
                          r=(("ps", b),), w=(("stage", buf, hb),), regions=(R,))
                else:
                    P.add("dve", lambda E, b=b, hb=hb, buf=buf, rows=rows:
                          E.tensor_copy(out=stage[0:rows, buf, hb * 512:(hb + 1) * 512], in_=psum[0:rows, b, :]),
                          r=(("ps", b),), w=(("stage", buf, hb),), regions=(R,))
            P.add("sp", lambda E, dst=dst, buf=buf, rows=rows: E.dma_start(out=dst, in_=stage[0:rows, buf, :]),
                  r=(("stage", buf, 0), ("stage", buf, 1)), dma="out1*", regions=(R,))

    LNR = ("scr", "ln")
    ln_zb = sv(0, [128, 4, 512], BF16)
    ln_sq = sv(4096, [128, 4, 512], BF16)
    ln_zs2 = sv(8192, [128, 2, 64], BF16)
    ln_mt = sv(8704, [128, 512], F32)
    ln_vt = sv(10752, [128, 512], F32)
    ln_bt = sv(12800, [128, 512], F32)
    ln_cnt = dict(n=0)
    LN_BANKS = {0: (3, 4), 1: (5, 6)}

    def ln_stats_chunk(src, srckey, k, mode, units=(0, 1, 2)):
        R = LNR
        XR = (RA_ATT, RC) if mode == "cs" else ()
        for u, (off, n) in enumerate(UNITS):
            if u not in units:
                continue
            if u < 2:
                rb = ln_cnt["n"] % 4
                ln_cnt["n"] += 1
                b1, b2 = LN_BANKS[u]
                P.add("act", lambda E, k=k, rb=rb, off=off, n=n:
                      E.activation(out=ln_sq[:, rb, 0:n], in_=src[:, k, off:off + n], func=AF.Square),
                      r=((srckey, k, u),), w=(("lnsq", rb),), regions=(R,) + XR)
                P.add("dve", lambda E, k=k, rb=rb, off=off, n=n:
                      E.tensor_copy(out=ln_zb[:, rb, 0:n], in_=src[:, k, off:off + n]),
                      r=((srckey, k, u),), w=(("lnzb", rb),), regions=(R,) + XR)

                def mm(E, k=k, rb=rb, n=n, b1=b1, b2=b2):
                    E.matmul(PS(b1, n), lhsT=onesb[:, :], rhs=ln_zb[:, rb, 0:n], start=(k == 0), stop=(k == 7))
                    return E.matmul(PS(b2, n), lhsT=onesb[:, :], rhs=ln_sq[:, rb, 0:n], start=(k == 0), stop=(k == 7))
                P.add("pe", mm, r=(("lnzb", rb), ("lnsq", rb), ("onesb",)), w=(("ps", b1), ("ps", b2)), regions=(R,))
            else:
                rb = k % 2
                P.add("act", lambda E, k=k, rb=rb, off=off, n=n:
                      E.activation(out=ln_zs2[:, rb, 32:64], in_=src[:, k, off:off + n], func=AF.Square),
                      r=((srckey, k, u),), w=(("lnzs2q", rb),), regions=(R,) + XR)
                P.add("dve", lambda E, k=k, rb=rb, off=off, n=n:
                      E.tensor_copy(out=ln_zs2[:, rb, 0:32], in_=src[:, k, off:off + n]),
                      r=((srckey, k, u),), w=(("lnzs2z", rb),), regions=(R,) + XR)
                P.add("pe", lambda E, k=k, rb=rb:
                      E.matmul(psum[:, 7, 0:64], lhsT=onesb[:, :], rhs=ln_zs2[:, rb, :], start=(k == 0), stop=(k == 7)),
                      r=(("lnzs2q", rb), ("lnzs2z", rb), ("onesb",)), w=(("ps", 7),), regions=(R,))

    def ln_finish(src, srckey, eps, gname, bname, l, mode, units=(0, 1, 2)):
        R = LNR
        XR = (RA_ATT, RC) if mode == "cs" else ()
        epsT = eps1 if abs(eps - LN_EPS) < 1e-12 else eps4
        mt, vt, bt = ln_mt, ln_vt, ln_bt
        for u, (off, n) in enumerate(UNITS):
            if u not in units:
                continue
            if u < 2:
                b1, b2 = LN_BANKS[u]
                s1, s2 = PS(b1, n), PS(b2, n)
            else:
                b1 = b2 = 7
                s1, s2 = psum[:, 7, 0:32], psum[:, 7, 32:64]
            P.add("dve", lambda E, n=n, s1=s1: E.tensor_scalar(out=mt[:, 0:n], in0=s1, scalar1=1.0 / D, scalar2=None, op0=ALU.mult),
                  r=(("ps", b1),), w=(("lnm",),), regions=(R,))
            P.add("dve", lambda E, n=n: E.tensor_tensor(out=bt[:, 0:n], in0=mt[:, 0:n], in1=mt[:, 0:n], op=ALU.mult),
                  r=(("lnm",),), w=(("lnb",),), regions=(R,))
            P.add("dve", lambda E, n=n, s2=s2: E.scalar_tensor_tensor(out=vt[:, 0:n], in0=s2, scalar=1.0 / D, in1=bt[:, 0:n],
                                                                      op0=ALU.mult, op1=ALU.subtract),
                  r=(("ps", b2), ("lnb",)), w=(("lnv",),), regions=(R,))
            P.add("act", lambda E, n=n: E.activation(out=bt[:, 0:n], in_=vt[:, 0:n], func=AF.Ln, bias=epsT[:, 0:1]),
                  r=(("lnv",), ("epsT",)), w=(("lnb",),), regions=(R,))
            P.add("act", lambda E, n=n: E.activation(out=vt[:, 0:n], in_=bt[:, 0:n], func=AF.Exp, scale=-0.5),
                  r=(("lnb",),), w=(("lnv",),), regions=(R,))
            for k in range(8):
                P.add("dve", lambda E, k=k, off=off, n=n:
                      E.tensor_tensor(out=src[:, k, off:off + n], in0=src[:, k, off:off + n], in1=mt[:, 0:n], op=ALU.subtract),
                      r=((srckey, k, u), ("lnm",)), w=((srckey, k, u),), regions=(R,) + XR)
                P.add("dve", lambda E, k=k, off=off, n=n:
                      E.tensor_tensor(out=src[:, k, off:off + n], in0=src[:, k, off:off + n], in1=vt[:, 0:n], op=ALU.mult),
                      r=((srckey, k, u), ("lnv",)), w=((srckey, k, u),), regions=(R,) + XR)
                if mode == "x":
                    P.add("act", lambda E, k=k, off=off, n=n:
                          E.activation(out=xb[:, k, off:off + n], in_=src[:, k, off:off + n], func=AF.Identity,
                                       scale=pcol(gname, l, k), bias=pcol(bname, l, k)),
                          r=((srckey, k, u), ("prm",)), w=(("xb", k, u),))
                    P.add("act", lambda E, k=k, off=off, n=n:
                          E.activation(out=x32[:, k, off:off + n], in_=src[:, k, off:off + n], func=AF.Identity,
                                       scale=pcol(gname, l, k), bias=pcol(bname, l, k)),
                          r=((srckey, k, u), ("prm",)), w=(("x32", k, u),))
                else:
                    P.add("act", lambda E, k=k, off=off, n=n:
                          E.activation(out=csT[:, k, off:off + n], in_=src[:, k, off:off + n], func=AF.Silu,
                                       scale=pcol(gname, l, k), bias=pcol(bname, l, k)),
                          r=((srckey, k, u), ("prm",)), w=(("cs", k, u),), regions=(RA_ATT, RC, ("csr", "cs")))

    def layer_norm(src, srckey, eps, gname, bname, l, mode):
        for k in range(8):
            ln_stats_chunk(src, srckey, k, mode)
        ln_finish(src, srckey, eps, gname, bname, l, mode)

    def dense_ui(lhs_fn, rhs_t, rhs_key, wkeys, banks, regions=()):
        def mm(E):
            ins = None
            for k in range(8):
                lt = lhs_fn(k)
                for u, (off, n) in enumerate(UNITS):
                    ins = E.matmul(PS(banks[u], n), lhsT=lt, rhs=rhs_t[:, k, off:off + n], start=(k == 0), stop=(k == 7))
            return ins
        P.add("pe", mm, r=tuple(wkeys) + tuple((rhs_key, k, u) for k in range(8) for u in range(3)),
              w=tuple(("ps", banks[u]) for u in range(3)), regions=regions)

    def ffn(l, which, tile0):
        R = ("scr", "ffn")
        sa = sv(0, [128, 4, 512], BF16)
        cnt = 0
        for t in range(11):
            wv, wk = wtile(l, tile0 + t, 4096)
            w4 = wv.rearrange("p (k s c) -> p k s c", k=8, s=2, c=256)
            if t >= 1:
                for jj in range(2):
                    j = 2 * t + jj
                    dense_ui(lambda k, w4=w4, jj=jj: w4[:, k, 0, jj * 128:(jj + 1) * 128], xb, "xb", wk, (0, 1, 2))
                    dense_ui(lambda k, w4=w4, jj=jj: w4[:, k, 1, jj * 128:(jj + 1) * 128], xb, "xb", wk, (3, 4, 5))
                    for u, (off, n) in enumerate(UNITS):
                        rb = cnt % 4
                        cnt += 1
                        P.add("act", lambda E, u=u, rb=rb, n=n: E.activation(out=sa[:, rb, 0:n], in_=PS(u, n), func=AF.Silu),
                              r=(("ps", u),), w=(("sa", rb),), regions=(R,))
                        P.add("dve", lambda E, u=u, rb=rb, n=n, j=j, off=off:
                              E.tensor_tensor(out=gT[:, j, off:off + n], in0=PS(3 + u, n), in1=sa[:, rb, 0:n], op=ALU.mult),
                              r=(("ps", 3 + u), ("sa", rb)), w=(("g", j, u),), regions=(R, RA_FFN))
                continue
            for u, (off, n) in enumerate(UNITS):
                for jj in range(2):
                    j = 2 * t + jj
                    pb = (cnt % 3) * 2
                    rb = cnt % 4
                    cnt += 1

                    def mm(E, w4=w4, jj=jj, off=off, n=n, pb=pb):
                        ins = None
                        for s in range(2):
                            for k in range(8):
                                ins = E.matmul(PS(pb + s, n), lhsT=w4[:, k, s, jj * 128:(jj + 1) * 128],
                                               rhs=xb[:, k, off:off + n], start=(k == 0), stop=(k == 7))
                        return ins
                    P.add("pe", mm, r=wk + tuple(("xb", k, u) for k in range(8)), w=(("ps", pb), ("ps", pb + 1)))
                    P.add("act", lambda E, pb=pb, rb=rb, n=n: E.activation(out=sa[:, rb, 0:n], in_=PS(pb, n), func=AF.Silu),
                          r=(("ps", pb),), w=(("sa", rb),), regions=(R,))
                    P.add("dve", lambda E, pb=pb, rb=rb, n=n, j=j, off=off:
                          E.tensor_tensor(out=gT[:, j, off:off + n], in0=PS(pb + 1, n), in1=sa[:, rb, 0:n], op=ALU.mult),
                          r=(("ps", pb + 1), ("sa", rb)), w=(("g", j, u),), regions=(R, RA_FFN))
        ti = tile0 + 11
        for op_ in range(4):
            wv0, wk0 = wtile(l, ti, 2816); ti += 1
            wv1, wk1 = wtile(l, ti, 2816); ti += 1
            wh = [wv0[:, 0:2816].rearrange("p (j c) -> p j c", c=256), wv1[:, 0:2816].rearrange("p (j c) -> p j c", c=256)]
            for o2 in range(2):
                oc = op_ * 2 + o2
                base = 0

                if oc < 7:
                    def mm(E, wh=wh, o2=o2, base=base):
                        ins = None
                        for j in range(NJ):
                            h, jl = j // 11, j % 11
                            for u, (off, n) in enumerate(UNITS):
                                ins = E.matmul(PS(base + u, n), lhsT=wh[h][:, jl, o2 * 128:(o2 + 1) * 128],
                                               rhs=gT[:, j, off:off + n], start=(j == 0), stop=(j == NJ - 1))
                        return ins
                    P.add("pe", mm, r=wk0 + wk1 + tuple(("g", j, u) for j in range(NJ) for u in range(3)),
                          w=tuple(("ps", base + u) for u in range(3)), regions=(RA_FFN,))
                else:
                    gn, bn = ("ln1_g", "ln1_b") if which == 1 else ("ln3_g", "ln3_b")

                    def mm_u(u, wh=wh, o2=o2, base=base):
                        off, n = UNITS[u]

                        def mm(E):
                            ins = None
                            for j in range(NJ):
                                h, jl = j // 11, j % 11
                                ins = E.matmul(PS(base + u, n), lhsT=wh[h][:, jl, o2 * 128:(o2 + 1) * 128],
                                               rhs=gT[:, j, off:off + n], start=(j == 0), stop=(j == NJ - 1))
                            return ins
                        P.add("pe", mm, r=wk0 + wk1 + tuple(("g", j, u) for j in range(NJ)), w=(("ps", base + u),), regions=(RA_FFN,))

                    def ev_u(u, oc=oc, base=base):
                        off, n = UNITS[u]
                        P.add("dve", lambda E: E.scalar_tensor_tensor(out=x32[:, oc, off:off + n], in0=x32[:, oc, off:off + n], scalar=2.0 * ALPHA,
                                                                      in1=PS(base + u, n), op0=ALU.mult, op1=ALU.add),
                              r=(("ps", base + u), ("x32", oc, u)), w=(("x32", oc, u),))
                    fa = (x32, "x32", 4.0 * LN_EPS, gn, bn, l, "x")
                    mm_u(0); ln_stats_chunk(x32, "x32", 6, "x"); ev_u(0)
                    mm_u(1); ln_stats_chunk(x32, "x32", 7, "x", units=(0,)); ev_u(1)
                    mm_u(2); ln_stats_chunk(x32, "x32", 7, "x", units=(1,)); ev_u(2)
                    ln_stats_chunk(x32, "x32", 7, "x", units=(2,)); ln_finish(*fa)
                    continue
                if oc > 0:
                    ln_stats_chunk(x32, "x32", oc - 1, "x")
                for u, (off, n) in enumerate(UNITS):
                    P.add("dve", lambda E, oc=oc, off=off, n=n, base=base, u=u:
                          E.scalar_tensor_tensor(out=x32[:, oc, off:off + n], in0=x32[:, oc, off:off + n], scalar=2.0 * ALPHA,
                                                 in1=PS(base + u, n), op0=ALU.mult, op1=ALU.add),
                          r=(("ps", base + u), ("x32", oc, u)), w=(("x32", oc, u),))

    def state_dma(st, l, grp):
        P.add("sp", lambda E: E.dma_start(out=sstage[0:120, :], in_=scv[l, st * NSEQ + grp * 4: st * NSEQ + grp * 4 + 4].rearrange("b r c -> (b r) c")),
              w=(("sstage",),), dma="lds2")

    def qkv_attention(st, l, tile0):
        R = ("scr", "att")
        state_dma(st, l, 0)
        kcs = av(50688, [128, NSEQ, 256], BF16)
        vcs = av(54784, [128, NSEQ, 256], BF16)
        kcT = av(58880, [128, 2, NSEQ, 128], BF16)
        RCS_S = ("csr", "samp")
        P.add("pool", lambda E: E.dma_start(out=kcs[:], in_=ck[l, st * NSEQ:(st + 1) * NSEQ].rearrange("b k c -> k b c")),
              w=(("kcs",),), dma="ldk", regions=(RA_ATT, RCS_S))
        P.add("pool", lambda E: E.dma_start(out=vcs[:], in_=cv[l, st * NSEQ:(st + 1) * NSEQ].rearrange("b k c -> k b c")),
              w=(("vcs",),), dma="ldv", regions=(RA_ATT, RCS_S))
        pbf = psum[:, 6:8, :].rearrange("p a b -> p (a b)").bitcast(BF16)

        def trk(E):
            ins = None
            for g2 in range(2):
                for b in range(NSEQ):
                    idx = g2 * NSEQ + b
                    ins = E.transpose(pbf[:, idx * 128:(idx + 1) * 128], kcs[:, b, g2 * 128:(g2 + 1) * 128], identb[:, :])
            return ins
        P.add("pe", trk, r=(("kcs",), ("identb",)), w=(("ps", 6), ("ps", 7)), regions=(RA_ATT, RCS_S))
        for g2 in range(2):
            P.add("act", lambda E, g2=g2: E.activation(out=kcT[:, g2].rearrange("p b k -> p (b k)"), in_=pbf[:, g2 * 1024:(g2 + 1) * 1024], func=AF.Identity),
                  r=(("ps", 6 + g2),), w=(("kcT", g2),), regions=(RA_ATT, RCS_S))
        if st == 0:
            pass
        else:
            P.add("pool", lambda E: E.tensor_copy(out=kT[:, :, 0:128], in_=kcarry[:, l, :, :]),
                  r=(("kcarry", l),), w=(("kprev",),), regions=(RA_ATT, RQ))
            P.add("pool", lambda E: E.tensor_copy(out=vtok[:, 0, :], in_=vcarry[:, l, :]),
                  r=(("vcarry", l),), w=(("v", 0),), regions=(RA_ATT, RQ))
        cnt = 0
        for tq in range(2):
            wv, wk = wtile(l, tile0 + tq, 4096)
            w3 = wv.rearrange("p (k c) -> p k c", k=8)
            if tq == 1:
                for cl in range(4):
                    c = tq * 4 + cl
                    bk = tuple((cl % 2) * 3 + u for u in range(3))
                    dense_ui(lambda k, w3=w3, cl=cl: w3[:, k, cl * 128:(cl + 1) * 128], xb, "xb", wk, bk)
                    for u, (off, n) in enumerate(UNITS):
                        P.add("act", lambda E, c=c, off=off, n=n, b=bk[u]:
                              E.activation(out=qT[:, c, off:off + n], in_=PS(b, n), func=AF.Identity, scale=0.125, bias=bq8[:, l, c:c + 1]),
                              r=(("ps", bk[u]), ("bq8", l)), w=(("q", c, u),), regions=(RA_ATT, RQ))
                continue
            for u, (off, n) in enumerate(UNITS):
                for cl in range(4):
                    c = tq * 4 + cl
                    pb = cnt % 4
                    cnt += 1

                    def mm(E, w3=w3, cl=cl, off=off, n=n, pb=pb):
                        ins = None
                        for k in range(8):
                            ins = E.matmul(PS(pb, n), lhsT=w3[:, k, cl * 128:(cl + 1) * 128], rhs=xb[:, k, off:off + n],
                                           start=(k == 0), stop=(k == 7))
                        return ins
                    P.add("pe", mm, r=wk + tuple(("xb", k, u) for k in range(8)), w=(("ps", pb),))
                    P.add("act", lambda E, c=c, off=off, n=n, pb=pb:
                          E.activation(out=qT[:, c, off:off + n], in_=PS(pb, n), func=AF.Identity, scale=0.125, bias=bq8[:, l, c:c + 1]),
                          r=(("ps", pb), ("bq8", l)), w=(("q", c, u),), regions=(RA_ATT, RQ))
        wv, wk = wtile(l, tile0 + 2, 4096)
        w3 = wv.rearrange("p (k c) -> p k c", k=8)
        for g2 in range(2):
            bk = tuple((g2 % 2) * 3 + u for u in range(3))
            dense_ui(lambda k, g2=g2: w3[:, k, g2 * 128:(g2 + 1) * 128], xb, "xb", wk, bk)
            for u, (off, n) in enumerate(UNITS):
                P.add("act", lambda E, g2=g2, off=off, n=n, b=bk[u]:
                      E.activation(out=kT[:, g2, 128 + off:128 + off + n], in_=PS(b, n), func=AF.Identity,
                                   bias=pcol("bk", l, g2)),
                      r=(("ps", bk[u]), ("prm",)), w=(("k", g2, u),), regions=(RA_ATT, RQ))
        ostage = sv(0, [128, 2, 512], F32)
        o_bkv, _ = PL.cols[f"bkv_b{l}"]
        for tb in range(9):
            rows = 128 if tb < 8 else TS
            col0 = tb * 128 if tb < 8 else TP
            need_k = (tb == 8) or (st == 1 and tb == 7)
            pb = 4 + (tb % 2)
            ncol = 512 if need_k else 256
            c0 = 0 if need_k else 256

            def mm(E, col0=col0, rows=rows, pb=pb, c0=c0, ncol=ncol):
                ins = None
                for k in range(8):
                    ins = E.matmul(psum[0:rows, pb, c0:c0 + ncol], lhsT=xb[:, k, col0:col0 + rows], rhs=w3[:, k, c0:c0 + ncol],
                                   start=(k == 0), stop=(k == 7))
                return ins
            P.add("pe", mm, r=wk + tuple(("xb", k, u) for k in range(8) for u in range(3)), w=(("ps", pb),))
            vdst = vtok[:, 1 + tb, :] if tb < 8 else sb_vn[:, :]
            vkey = ("v", 1 + tb) if tb < 8 else ("vn2",)
            P.add("dve", lambda E, rows=rows, pb=pb, vdst=vdst:
                  E.tensor_tensor(out=vdst[0:rows], in0=psum[0:rows, pb, 256:512], in1=prm[0:rows, o_bkv + 256:o_bkv + 512], op=ALU.add),
                  r=(("ps", pb), ("prm",)), w=(vkey,), regions=(RA_ATT, R, RQ))
            if need_k:
                ob = tb % 2
                P.add("dve", lambda E, rows=rows, pb=pb, ob=ob:
                      E.tensor_tensor(out=ostage[0:rows, ob, :], in0=psum[0:rows, pb, 0:512], in1=prm[0:rows, o_bkv:o_bkv + 512], op=ALU.add),
                      r=(("ps", pb), ("prm",)), w=(("ostage", ob),), regions=(R,))
                if tb == 7:
                    P.add("sp", lambda E, ob=ob: E.dma_start(out=nkp[l], in_=ostage[:, ob, 0:256]), r=(("ostage", ob),), dma="out1*", regions=(R,))
                    P.add("sp", lambda E, ob=ob: E.dma_start(out=nvp[l], in_=ostage[:, ob, 256:512]), r=(("ostage", ob),), dma="out1*", regions=(R,))
                else:
                    for (dst_t, c_0) in ((nks, 0), (nvs, 256)):
                        for b in range(NSEQ):
                            P.add("sp", lambda E, ob=ob, dst_t=dst_t, c_0=c_0, b=b:
                                  E.dma_start(out=dst_t[l, st * NSEQ + b, 124:128, :], in_=ostage[4 * b:4 * b + 4, ob, c_0:c_0 + 256]),
                                  r=(("ostage", ob),), dma="out1*", regions=(R,))
        if st == 0:
            P.add("pool", lambda E: E.tensor_copy(out=kcarry[:, l, :, :], in_=kT[:, :, TP:TP + 128]),
                  r=tuple(("k", g2, 1) for g2 in range(2)), w=(("kcarry", l),), regions=(RA_ATT, RQ))
            P.add("pool", lambda E: E.tensor_copy(out=vcarry[:, l, :], in_=vtok[:, 8, :]),
                  r=(("v", 8),), w=(("vcarry", l),), regions=(RA_ATT, RQ))
        d1 = sv(0, [128, 2, 512], F32)
        etab = sv(4096, [128, 16, 2, 128], BF16)
        ptb = sv(12288, [128, 2, 2, 512], BF16)
        praw = sv(16384, [128, 2, 2, 512], BF16)
        for h in range(16):
            P.add("act", lambda E, h=h: E.activation(out=etab[:, h, :, :].rearrange("p a b -> p (a b)"),
                                                     in_=dp[:, :, :].rearrange("p a b -> p (a b)"), func=AF.Exp, scale=float(SLOPES[h])),
                  r=(("dp",),), w=(("etab", h),), regions=(R,))
        iters = [(qb, g) for qb in range(8) for g in range(4)]

        def blks_of(qb):
            return (0, 1) if (st * 8 + qb) > 0 else (1,)

        def emit_scores(n):
            qb, g = iters[n]
            g2, g1 = g // 2, g % 2
            sb0 = (n % 2) * 2
            blks = blks_of(qb)

            def mm(E):
                ins = None
                for blk in blks:
                    kcol = qb * 128 + blk * 128
                    ins = E.matmul(PS(sb0 + blk), lhsT=kT[g1 * 64:(g1 + 1) * 64, g2, kcol:kcol + 128],
                                   rhs=qT[g1 * 64:(g1 + 1) * 64, g2 * 4:(g2 + 1) * 4, qb * 128:(qb + 1) * 128],
                                   start=True, stop=True)
                return ins
            ku = 0 if qb < 4 else 1
            kreads = [("k", g2, ku)]
            if len(blks) == 2:
                kreads.append(("kprev",) if qb == 0 else ("k", g2, 0 if (qb - 1) < 4 else 1))
            P.add("pe", mm, r=tuple(kreads) + tuple(("q", g2 * 4 + i, ku) for i in range(4)),
                  w=tuple(("ps", sb0 + blk) for blk in blks), regions=(RA_ATT, RQ))

        def emit_softmax(n):
            qb, g = iters[n]
            buf = n % 2
            sb0 = buf * 2
            for blk in blks_of(qb):
                P.add("act", lambda E, buf=buf, blk=blk, sb0=sb0:
                      E.activation(out=praw[:, buf, blk, :], in_=PS(sb0 + blk), func=AF.Exp),
                      r=(("ps", sb0 + blk),), w=(("praw", buf, blk),), regions=(R,))
                P.add("dve", lambda E, buf=buf, blk=blk, g=g:
                      E.tensor_tensor(out=ptb[:, buf, blk, :].rearrange("p (i q) -> p i q", q=128),
                                      in0=praw[:, buf, blk, :].rearrange("p (i q) -> p i q", q=128),
                                      in1=etab[:, g * 4:(g + 1) * 4, blk, :], op=ALU.mult),
                      r=(("praw", buf, blk),) + tuple(("etab", g * 4 + i) for i in range(4)), w=(("pt", buf, blk),), regions=(R,))

        def emit_pv(n):
            qb, g = iters[n]
            g2, g1 = g // 2, g % 2
            buf = n % 2
            blks = blks_of(qb)

            def pv(E):
                ins = None
                for bi, blk in enumerate(blks):
                    ins = E.matmul(psum[g1 * 64:(g1 + 1) * 64, 4 + g2, :], lhsT=vtok[:, qb + blk, g * 64:(g + 1) * 64],
                                   rhs=ptb[:, buf, blk, :], start=(bi == 0), stop=(bi == len(blks) - 1))
                for bi, blk in enumerate(blks):
                    ins = E.matmul(psum[g1 * 64:(g1 + 1) * 64, 6 + g2, :], lhsT=onesb[:, 0:64],
                                   rhs=ptb[:, buf, blk, :], start=(bi == 0), stop=(bi == len(blks) - 1))
                return ins
            P.add("pe", pv, r=tuple(("pt", buf, blk) for blk in blks) + tuple(("v", qb + blk) for blk in blks) + (("onesb",),),
                  w=(("ps", 4 + g2), ("ps", 6 + g2)), regions=(RA_ATT, R, RQ))

        def emit_evac(qb, g2):
            if True:
                P.add("dve", lambda E, g2=g2:
                      E.tensor_tensor(out=d1[:, g2, :].rearrange("p (i q) -> p i q", q=128),
                                      in0=psum[:, 6 + g2, :].rearrange("p (i q) -> p i q", q=128),
                                      in1=esT[:, l, g2 * 4:(g2 + 1) * 4].unsqueeze(2).to_broadcast([128, 4, 128]), op=ALU.add),
                      r=(("ps", 6 + g2), ("esT", l)), w=(("d1", g2), ("ostage", 0), ("ostage", 1)), regions=(R,))
                P.add("act", lambda E, g2=g2: E.activation(out=d1[:, g2, :], in_=d1[:, g2, :], func=AF.Ln),
                      r=(("d1", g2),), w=(("d1", g2),), regions=(R,))
                P.add("act", lambda E, g2=g2: E.activation(out=d1[:, g2, :], in_=d1[:, g2, :], func=AF.Exp, scale=-1.0),
                      r=(("d1", g2),), w=(("d1", g2),), regions=(R,))
                P.add("dve", lambda E, g2=g2, qb=qb:
                      E.tensor_tensor(out=attnT[:, g2 * 4:(g2 + 1) * 4, qb * 128:(qb + 1) * 128],
                                      in0=psum[:, 4 + g2, :].rearrange("p (i q) -> p i q", q=128),
                                      in1=d1[:, g2, :].rearrange("p (i q) -> p i q", q=128), op=ALU.mult),
                      r=(("ps", 4 + g2), ("d1", g2)),
                      w=tuple(("at", g2 * 4 + i, 0 if qb < 4 else 1) for i in range(4)),
                      regions=(R, RA_ATT))

        emit_scores(0)
        for n in range(len(iters)):
            if n + 1 < len(iters):
                emit_scores(n + 1)
            emit_softmax(n)
            emit_pv(n)
            if n >= 1 and iters[n - 1][1] in (1, 3):
                emit_evac(iters[n - 1][0], iters[n - 1][1] // 2)
        emit_evac(iters[-1][0], iters[-1][1] // 2)
        RS = ("scr", "satt")
        scc = sv(12288, [128, NSEQ, 4, 4, 4], F32)
        scn = sv(14336, [128, 16, 32], F32)
        pcs = sv(16384, [128, NSEQ, 4, 4, 4], BF16)
        pns = sv(17408, [128, 16, 32], BF16)
        tA = sv(18432, [128, 256], F32)
        tD = sv(19456, [128, 256], F32)
        vn2 = sb_vn
        def sc_c(E):
            ins = None
            for g1 in range(2):
                for b in range(NSEQ):
                    for g2 in range(2):
                        o = (b * 2 + g2) * 16
                        ins = E.matmul(psum[:, 2 + g1, o:o + 16], lhsT=kcT[g1 * 64:(g1 + 1) * 64, g2, b, :],
                                       rhs=qT[g1 * 64:(g1 + 1) * 64, g2 * 4:(g2 + 1) * 4, TP + b * 4:TP + b * 4 + 4],
                                       start=True, stop=True)
            return ins
        P.add("pe", sc_c, r=(("kcT", 0), ("kcT", 1)) + tuple(("q", c, 2) for c in range(8)), w=(("ps", 2), ("ps", 3)), regions=(RS, RA_ATT, RQ, ("csr", "samp")))

        def sc_n(E):
            ins = None
            for g1 in range(2):
                for g2 in range(2):
                    for i in range(4):
                        o = (g2 * 4 + i) * 32
                        ins = E.matmul(psum[0:TS, 4 + g1, o:o + 32], lhsT=kT[g1 * 64:(g1 + 1) * 64, g2, 128 + TP:128 + TP + TS],
                                       rhs=qT[g1 * 64:(g1 + 1) * 64, g2 * 4 + i, TP:TP + TS], start=True, stop=True)
            return ins
        P.add("pe", sc_n, r=tuple(("k", g2, 2) for g2 in range(2)) + tuple(("q", c, 2) for c in range(8)), w=(("ps", 4), ("ps", 5)),
              regions=(RS, RA_ATT, RQ))
        for g in range(4):
            g2, g1 = g // 2, g % 2
            ps_c = psum[:, 2 + g1, 0:256].rearrange("p (b g i t) -> p b g i t", b=NSEQ, g=2, i=4)
            for i in range(4):
                h = head_of(g, i)
                P.add("dve", lambda E, g=g, g2=g2, i=i, h=h, ps_c=ps_c:
                      E.scalar_tensor_tensor(out=scc[:, :, g, i, :], in0=dc[:, :].unsqueeze(1).to_broadcast([128, NSEQ, 4]), scalar=float(SLOPES[h]),
                                             in1=ps_c[:, :, g2, i, :], op0=ALU.mult, op1=ALU.add),
                      r=(("ps", 2 + g1), ("dc",)), w=(("scc", h),), regions=(RS,))
                P.add("act", lambda E, g=g, i=i, h=h:
                      E.activation(out=pcs[:, :, g, i, :], in_=scc[:, :, g, i, :], func=AF.Exp, bias=nsink[:, l, h:h + 1]),
                      r=(("scc", h), ("nsink", l)), w=(("pcs", h),), regions=(RS,))
                o = (g2 * 4 + i) * 32
                P.add("dve", lambda E, h=h, g1=g1, o=o:
                      E.scalar_tensor_tensor(out=scn[0:TS, h, :], in0=dn[0:TS, :], scalar=float(SLOPES[h]),
                                             in1=psum[0:TS, 4 + g1, o:o + 32], op0=ALU.mult, op1=ALU.add),
                      r=(("ps", 4 + g1), ("dn",)), w=(("scn", h),), regions=(RS,))
                P.add("act", lambda E, h=h:
                      E.activation(out=pns[0:TS, h, :], in_=scn[0:TS, h, :], func=AF.Exp, bias=nsink[0:TS, l, h:h + 1]),
                      r=(("scn", h), ("nsink", l)), w=(("pns", h),), regions=(RS,))
        def pv_s(E):
            ins = None
            for g in range(4):
                g2, g1 = g // 2, g % 2
                for i in range(4):
                    h = head_of(g, i)
                    c = g2 * 4 + i
                    ins = E.matmul(psum[g1 * 64:(g1 + 1) * 64, 0, c * 32:(c + 1) * 32], lhsT=vn2[0:TS, g * 64:(g + 1) * 64],
                                   rhs=pns[0:TS, h, :], start=True, stop=True)
                    ins = E.matmul(psum[g1 * 64:(g1 + 1) * 64, 1, c * 32:(c + 1) * 32], lhsT=onesb[0:TS, 0:64],
                                   rhs=pns[0:TS, h, :], start=True, stop=True)
            for b in range(NSEQ):
                for g in range(4):
                    g2, g1 = g // 2, g % 2
                    o = 256 + (g2 * NSEQ + b) * 16
                    ins = E.matmul(psum[g1 * 64:(g1 + 1) * 64, 0, o:o + 16], lhsT=vcs[:, b, g * 64:(g + 1) * 64],
                                   rhs=pcs[:, b, g, :, :], start=True, stop=True)
                    ins = E.matmul(psum[g1 * 64:(g1 + 1) * 64, 1, o:o + 16], lhsT=onesb[:, 0:64],
                                   rhs=pcs[:, b, g, :, :], start=True, stop=True)
            return ins
        P.add("pe", pv_s, r=tuple(("pns", h) for h in range(16)) + tuple(("pcs", h) for h in range(16)) + (("vn2",), ("vcs",), ("onesb",)), w=(("ps", 0), ("ps", 1)), regions=(RS, RA_ATT, ("csr", "samp")))
        P.add("act", lambda E: E.activation(out=tA[:, :], in_=psum[:, 0, 0:256], func=AF.Identity), r=(("ps", 0),), w=(("tA",),), regions=(RS,))
        P.add("act", lambda E: E.activation(out=tD[:, :], in_=psum[:, 1, 0:256], func=AF.Identity, bias=1.0), r=(("ps", 1),), w=(("tD",),), regions=(RS,))
        for g2 in range(2):
            nb = psum[:, 0, 256 + g2 * 128:256 + (g2 + 1) * 128].rearrange("p (b i t) -> p i b t", b=NSEQ, i=4)
            db = psum[:, 1, 256 + g2 * 128:256 + (g2 + 1) * 128].rearrange("p (b i t) -> p i b t", b=NSEQ, i=4)
            ta = tA[:, g2 * 128:(g2 + 1) * 128].rearrange("p (i b t) -> p i b t", i=4, b=NSEQ)
            td = tD[:, g2 * 128:(g2 + 1) * 128].rearrange("p (i b t) -> p i b t", i=4, b=NSEQ)
            P.add("dve", lambda E, nb=nb, ta=ta: E.tensor_tensor(out=ta, in0=nb, in1=ta, op=ALU.add), r=(("ps", 0), ("tA",)), w=(("tA",),), regions=(RS,))
            P.add("dve", lambda E, db=db, td=td: E.tensor_tensor(out=td, in0=db, in1=td, op=ALU.add), r=(("ps", 1), ("tD",)), w=(("tD",),), regions=(RS,))
        P.add("dve", lambda E: E.reciprocal(out=tD[:, :], in_=tD[:, :]), r=(("tD",),), w=(("tD",),), regions=(RS,))
        P.add("dve", lambda E: E.tensor_tensor(out=attnT[:, :, TP:TP + TS], in0=tA[:, :].rearrange("p (c n) -> p c n", n=TS),
                                               in1=tD[:, :].rearrange("p (c n) -> p c n", n=TS), op=ALU.mult),
              r=(("tA",), ("tD",)), w=tuple(("at", c, 2) for c in range(8)), regions=(RS, RA_ATT))

    sb_vn = sb("sb_vn", [128, 256], BF16)

    def glu_conv(st, l, tile0):
        R = ("scr", "glu")
        sg = sv(0, [128, 4, 512], F32)
        ostg = sv(8192, [128, D], F32)
        ow, _ = PL.cols[f"convw{l}"]
        for grp in range(2):
            if grp == 1:
                state_dma(st, l, 1)

            def tr(E):
                ins = None
                for k in range(8):
                    b = 6 + k // 4
                    ins = E.transpose(psum[:, b, (k % 4) * 128:(k % 4) * 128 + 120], sstage[0:120, k * 128:(k + 1) * 128], identf[0:120, 0:120])
                return ins
            P.add("pe", tr, r=(("sstage",), ("identf",)), w=(("ps", 6), ("ps", 7)), regions=(R,))
            for hb in range(2):
                src_ps = psum[:, 6 + hb, :].rearrange("p (k t) -> p k t", t=128)[:, :, 0:120].rearrange("p k (b r) -> p k b r", r=30)
                P.add("act", lambda E, src_ps=src_ps, hb=hb, grp=grp:
                      E.activation(out=uexts[:, hb * 4:(hb + 1) * 4, grp * 4:(grp + 1) * 4, 0:30], in_=src_ps, func=AF.Identity),
                      r=(("ps", 6 + hb),), w=tuple(("uexts", k) for k in range(hb * 4, hb * 4 + 4)))
        for cb in range(2):
            pass
        cntb = dict(n=0)
        wts = {}

        def chunk(c, phase):
            t, cl = c // 2, c % 2
            ub = c % 2
            if t not in wts:
                wv, wk_ = wtile(l, tile0 + t, 4096)
                wts[t] = (wv.rearrange("p (k s c) -> p k s c", k=8, s=2, c=256), wk_)
            w4, wk = wts[t]
            if phase == 'A':
                if st == 0:
                    P.add("pool", lambda E, ub=ub: E.memset(uextb[:, ub, 0:30], 0.0), w=(("uext", ub, "pre"),))
                else:
                    P.add("pool", lambda E, ub=ub, c=c: E.tensor_copy(out=uextb[:, ub, 0:30], in_=ucarry[:, l, c, :]),
                          r=(("ucarry", l, c),), w=(("uext", ub, "pre"),))
                abank = (0, 1, 2)
                gbank = (3, 6, 7)
                dense_ui(lambda k, w4=w4, cl=cl: w4[:, k, 0, cl * 128:(cl + 1) * 128], xb, "xb", wk, abank)
                dense_ui(lambda k, w4=w4, cl=cl: w4[:, k, 1, cl * 128:(cl + 1) * 128], xb, "xb", wk, gbank)
                for u, (off, n) in enumerate(UNITS):
                    rb = cntb['n'] % 4
                    cntb['n'] += 1
                    pa, pg = abank[u], gbank[u]
                    P.add("act", lambda E, pg=pg, rb=rb, n=n, c=c:
                          E.activation(out=sg[:, rb, 0:n], in_=PS(pg, n), func=AF.Sigmoid, bias=pcol("bgg", l, c)),
                          r=(("ps", pg), ("prm",)), w=(("sg", rb),), regions=(R,))
                    if u < 2:
                        dst = uextb[:, ub, 30 + off:30 + off + n]
                        src1 = sg[:, rb, 0:n]
                        src0 = PS(pa, n)
                        wkey = ("uext", ub, u)
                    else:
                        dst = uexts[:, c, :, 30:34]
                        src1 = sg[:, rb, 0:n].rearrange("p (b t) -> p b t", t=4)
                        src0 = PS(pa, n).rearrange("p (b t) -> p b t", t=4)
                        wkey = ("uexts_new", c)
                    P.add("dve", lambda E, dst=dst, src0=src0, src1=src1, c=c:
                          E.scalar_tensor_tensor(out=dst, in0=src0, scalar=pcol("bga", l, c), in1=src1, op0=ALU.add, op1=ALU.mult),
                          r=(("ps", pa), ("sg", rb), ("prm",)), w=(wkey,), regions=(R,))
                    if st == 1 and u == 1:
                        P.add("dve", lambda E, pa=pa, rb=rb, n=n, c=c:
                              E.scalar_tensor_tensor(out=u32tail[:, c, :], in0=psum[:, pa, n - 30:n], scalar=pcol("bga", l, c),
                                                     in1=sg[:, rb, n - 30:n], op0=ALU.add, op1=ALU.mult),
                              r=(("ps", pa), ("sg", rb), ("prm",)), w=(("u32tail", c),), regions=(R,))
            if phase == 'DG':
                P.add("dve", lambda E, c=c:
                      E.tensor_tensor(out=dg[:, :, :], in0=identb[:, :].unsqueeze(1).to_broadcast([128, 31, 128]),
                                      in1=prm[:, ow + c * 31:ow + c * 31 + 31].unsqueeze(2).to_broadcast([128, 31, 128]), op=ALU.mult),
                      r=(("identb",), ("prm",)), w=(("dg",),))
            if phase == 'B':
                ukeys = (("uext", ub, "pre"), ("uext", ub, 0), ("uext", ub, 1))
                for u in range(2):
                    off = u * 512

                    def cmm(E, ub=ub, off=off, u=u):
                        ins = None
                        for j in range(31):
                            ins = E.matmul(PS(4 + u), lhsT=dg[:, j, :], rhs=uextb[:, ub, off + j:off + j + 512],
                                           start=(j == 0), stop=(j == 30))
                        return ins
                    P.add("pe", cmm, r=ukeys + (("dg",),), w=(("ps", 4 + u),))
                    P.add("act", lambda E, c=c, off=off, u=u:
                          E.activation(out=cT[:, c, off:off + 512], in_=PS(4 + u), func=AF.Identity, bias=pcol("conv_b", l, c)),
                          r=(("ps", 4 + u), ("prm",)), w=(("c", c, u),), regions=(RA_ATT, RC))

                def tap_s(j, c=c):
                    dstv = (cT[:, c, TP:TP + TS] if j % 2 == 0 else cacc[:, 0:TS]).rearrange("p (b t) -> p b t", t=4)
                    akey = (("c", c, 2),) if j % 2 == 0 else (("cacc", 1),)
                    wj = prm[:, ow + c * 31 + j:ow + c * 31 + j + 1]
                    if j == 0:
                        f = lambda E: E.tensor_scalar(out=dstv, in0=uexts[:, c, :, 0:4], scalar1=wj, scalar2=pcol("conv_b", l, c),
                                                      op0=ALU.mult, op1=ALU.add)
                        rk = ()
                    elif j == 1:
                        f = lambda E: E.tensor_scalar(out=dstv, in0=uexts[:, c, :, 1:5], scalar1=wj, scalar2=None, op0=ALU.mult)
                        rk = ()
                    else:
                        f = lambda E: E.scalar_tensor_tensor(out=dstv, in0=uexts[:, c, :, j:j + 4], scalar=wj, in1=dstv,
                                                             op0=ALU.mult, op1=ALU.add)
                        rk = akey
                    P.add("dve", f, r=(("uexts", c), ("uexts_new", c), ("prm",)) + rk, w=akey, regions=(RA_ATT, RC))
                for j in range(31):
                    tap_s(j)
                P.add("dve", lambda E, c=c: E.tensor_tensor(out=cT[:, c, TP:TP + TS], in0=cT[:, c, TP:TP + TS], in1=cacc[:, 0:TS], op=ALU.add),
                      r=(("c", c, 2), ("cacc", 1)), w=(("c", c, 2),), regions=(RA_ATT, RC))
                if st == 0:
                    P.add("pool", lambda E, ub=ub, c=c: E.tensor_copy(out=ucarry[:, l, c, :], in_=uextb[:, ub, TP:TP + 30]),
                          r=(("uext", ub, 1),), w=(("ucarry", l, c),))
                P.add("pool", lambda E, c=c: E.tensor_copy(out=sv(12288, [128, 8, TS], F32)[:, c, :].rearrange("p (b t) -> p b t", t=4),
                                                           in_=uexts[:, c, :, 30:34]),
                      r=(("uexts_new", c),), w=(("unew", c),), regions=(R,))


        chunk(0, 'A'); chunk(0, 'DG')
        for c in range(8):
            if c + 1 < 8:
                chunk(c + 1, 'A')
            chunk(c, 'B')
            if c + 1 < 8:
                chunk(c + 1, 'DG')
        unew = sv(12288, [128, 8, TS], F32)
        if st == 1:
            def trp(E):
                ins = None
                for c in range(8):
                    ins = E.transpose(psum[0:30, 4 + c // 4, (c % 4) * 128:(c % 4 + 1) * 128], u32tail[:, c, :], identf[:, :])
                return ins
            P.add("pe", trp, r=tuple(("u32tail", c) for c in range(8)) + (("identf",),), w=(("ps", 4), ("ps", 5)), regions=(R,))
            for hb in range(2):
                P.add("act", lambda E, hb=hb: E.activation(out=ostg[0:30, hb * 512:(hb + 1) * 512], in_=psum[0:30, 4 + hb, :], func=AF.Identity),
                      r=(("ps", 4 + hb),), w=(("ostg",),), regions=(R,))
            P.add("sp", lambda E: E.dma_start(out=ncp[l], in_=ostg[0:30, :]), r=(("ostg",),), dma="out1*", regions=(R,))

        def trs(E):
            ins = None
            for c in range(8):
                ins = E.transpose(psum[0:TS, 6 + c // 4, (c % 4) * 128:(c % 4 + 1) * 128], unew[:, c, :], identf[:, :])
            return ins
        P.add("pe", trs, r=tuple(("unew", c) for c in range(8)) + (("identf",),), w=(("ps", 6), ("ps", 7)), regions=(R,))
        ostg2 = sv(13312, [128, D], F32)
        for hb in range(2):
            P.add("act", lambda E, hb=hb: E.activation(out=ostg2[0:TS, hb * 512:(hb + 1) * 512], in_=psum[0:TS, 6 + hb, :], func=AF.Identity),
                  r=(("ps", 6 + hb),), w=(("ostg2",),), regions=(R,))
        for b in range(NSEQ):
            P.add("sp", lambda E, b=b: E.dma_start(out=ncs[l, st * NSEQ + b, 26:30, :], in_=ostg2[4 * b:4 * b + 4, :]),
                  r=(("ostg2",),), dma="out1*", regions=(R,))
        layer_norm(cT, "c", LN_EPS, "cln_g", "cln_b", l, "cs")

    def out_proj(l, tile0):
        R = ("scr", "op")
        sga = sv(0, [128, T], F32)
        sgc = sv(4224, [128, T], F32)
        m1 = sv(8448, [128, T], F32)
        m2 = sv(12672, [128, T], F32)
        for pr in range(4):
            wA, kA = wtile(l, tile0 + pr * 2 + 0, 4096)
            wC, kC = wtile(l, tile0 + pr * 2 + 1, 4096)
            vA = wA.rearrange("p (k s c) -> p k s c", k=8, s=2, c=256)
            vC = wC.rearrange("p (k s c) -> p k s c", k=8, s=2, c=256)
            for cl in range(2):
                c = pr * 2 + cl
                dense_ui(lambda k, vA=vA, cl=cl: vA[:, k, 0, cl * 128:(cl + 1) * 128], xb, "xb", kA, (0, 1, 2))
                for u, (off, n) in enumerate(UNITS):
                    P.add("act", lambda E, u=u, off=off, n=n, c=c:
                          E.activation(out=sga[:, off:off + n], in_=PS(u, n), func=AF.Sigmoid, bias=pcol("bta", l, c)),
                          r=(("ps", u), ("prm",)), w=(("sga", u),), regions=(R,))
                dense_ui(lambda k, vA=vA, cl=cl: vA[:, k, 1, cl * 128:(cl + 1) * 128], attnT, "at", kA, (3, 4, 5), regions=(RA_ATT,))
                for u, (off, n) in enumerate(UNITS):
                    P.add("dve", lambda E, u=u, off=off, n=n:
                          E.tensor_tensor(out=m1[:, off:off + n], in0=PS(3 + u, n), in1=sga[:, off:off + n], op=ALU.mult),
                          r=(("ps", 3 + u), ("sga", u)), w=(("m1", u),), regions=(R,))
                dense_ui(lambda k, vC=vC, cl=cl: vC[:, k, 0, cl * 128:(cl + 1) * 128], xb, "xb", kC, (0, 1, 2))
                for u, (off, n) in enumerate(UNITS):
                    P.add("act", lambda E, u=u, off=off, n=n, c=c:
                          E.activation(out=sgc[:, off:off + n], in_=PS(u, n), func=AF.Sigmoid, bias=pcol("btc", l, c)),
                          r=(("ps", u), ("prm",)), w=(("sgc", u),), regions=(R,))
                dense_ui(lambda k, vC=vC, cl=cl: vC[:, k, 1, cl * 128:(cl + 1) * 128], csT, "cs", kC, (3, 4, 5), regions=(RA_ATT, ("csr", "cs")))
                for u, (off, n) in enumerate(UNITS):
                    P.add("dve", lambda E, u=u, off=off, n=n:
                          E.tensor_tensor(out=m2[:, off:off + n], in0=PS(3 + u, n), in1=sgc[:, off:off + n], op=ALU.mult),
                          r=(("ps", 3 + u), ("sgc", u)), w=(("m2", u),), regions=(R,))
                    P.add("dve", lambda E, u=u, off=off, n=n, c=c:
                          E.tensor_tensor(out=mixT[:, c, off:off + n], in0=m1[:, off:off + n], in1=m2[:, off:off + n], op=ALU.add),
                          r=(("m1", u), ("m2", u)), w=(("mix", c, u),), regions=(R, RA_ATT, RM))
        for tq in range(2):
            wv, wk = wtile(l, tile0 + 8 + tq, 4096)
            w3 = wv.rearrange("p (k c) -> p k c", k=8)
            for cl in range(4):
                c = tq * 4 + cl

                if c == 7:
                    def mm_u(u, w3=w3, cl=cl):
                        off, n = UNITS[u]

                        def mm(E):
                            ins = None
                            for k in range(8):
                                ins = E.matmul(PS(u, n), lhsT=w3[:, k, cl * 128:(cl + 1) * 128], rhs=mixT[:, k, off:off + n],
                                               start=(k == 0), stop=(k == 7))
                            return ins
                        P.add("pe", mm, r=wk + tuple(("mix", k, u) for k in range(8)), w=(("ps", u),), regions=(RA_ATT, RM))

                    def ev_u(u, c=c):
                        off, n = UNITS[u]
                        P.add("dve", lambda E: E.scalar_tensor_tensor(out=x32[:, c, off:off + n], in0=x32[:, c, off:off + n], scalar=ALPHA,
                                                                      in1=PS(u, n), op0=ALU.mult, op1=ALU.add),
                              r=(("ps", u), ("x32", c, u)), w=(("x32", c, u),))
                    fa = (x32, "x32", LN_EPS, "ln2_g", "ln2_b", l, "x")
                    mm_u(0); ln_stats_chunk(x32, "x32", 6, "x"); ev_u(0)
                    mm_u(1); ln_stats_chunk(x32, "x32", 7, "x", units=(0,)); ev_u(1)
                    mm_u(2); ln_stats_chunk(x32, "x32", 7, "x", units=(1,)); ev_u(2)
                    ln_stats_chunk(x32, "x32", 7, "x", units=(2,)); ln_finish(*fa)
                    continue

                def mm(E, w3=w3, cl=cl):
                    ins = None
                    for k in range(8):
                        for u, (off, n) in enumerate(UNITS):
                            ins = E.matmul(PS(u, n), lhsT=w3[:, k, cl * 128:(cl + 1) * 128], rhs=mixT[:, k, off:off + n],
                                           start=(k == 0), stop=(k == 7))
                    return ins
                P.add("pe", mm, r=wk + tuple(("mix", k, u) for k in range(8) for u in range(3)), w=tuple(("ps", u) for u in range(3)),
                      regions=(RA_ATT, RM))
                if c > 0:
                    ln_stats_chunk(x32, "x32", c - 1, "x")
                for u, (off, n) in enumerate(UNITS):
                    P.add("dve", lambda E, c=c, off=off, n=n, u=u:
                          E.scalar_tensor_tensor(out=x32[:, c, off:off + n], in0=x32[:, c, off:off + n], scalar=ALPHA,
                                                 in1=PS(u, n), op0=ALU.mult, op1=ALU.add),
                          r=(("ps", u), ("x32", c, u)), w=(("x32", c, u),))

    def dump(name, src):
        if not DEBUG:
            return
        if src.dtype == F32:
            P.add("sp", lambda E: E.dma_start(out=dbg[name], in_=src), r=tuple((kk, k, u) for kk in ("x32", "c") for k in range(8) for u in range(3)), dma="out1*")

    import os
    STOP = int(os.environ.get("KSTOP", "99"))
    NOCOPY = os.environ.get("KNOCOPY", "0") == "1"
    setup()
    for st in range(int(os.environ.get('KNST', NST))):
        if STOP >= -1:
            load_x(st)
        for l in range(2):
            if STOP >= 1:
                ffn(l, 1, 0)
            if STOP >= 2:
                qkv_attention(st, l, 19)
            if STOP >= 3:
                glu_conv(st, l, 22)
            if STOP >= 4:
                out_proj(l, 26)
            if STOP >= 5:
                ffn(l, 2, 36)
        if STOP >= 0:
            store_y(st)
    print('SBUF bytes remaining:', nc.sbuf_bytes_remaining, 'ops:', len(P.ops))
    P.finalize()
    P.emit(nc)
    es.close()
    return nc


_CACHE = {}


def kernel(**inp):
    inp = {k: np.asarray(v) for k, v in inp.items()}
    WT = pack_weights(inp)
    PAR = pack_params(inp)
    C = pack_consts()
    if "nc" not in _CACHE:
        _CACHE["nc"] = build_program()
    nc = _CACHE["nc"]
    in_maps = []
    for c in range(NCORES):
        in_maps.append({
            "xp": np.ascontiguousarray(inp["x_prompt"][c]),
            "xs": np.ascontiguousarray(inp["x_sample"][c * 16:(c + 1) * 16].reshape(64, D)),
            "ck": np.ascontiguousarray(inp["cache_k"][:, c * 16:(c + 1) * 16].reshape(2, 16, 128, 256)),
            "cv": np.ascontiguousarray(inp["cache_v"][:, c * 16:(c + 1) * 16].reshape(2, 16, 128, 256)),
            "scv": np.ascontiguousarray(inp["state_conv"][:, c * 16:(c + 1) * 16]),
            "wt": WT, "par": PAR,
            "c_ident": C["ident"], "c_dp": C["dp"], "c_dc": C["dc"], "c_dn": C["dn"],
        })
    res = run_bass_kernel_spmd(nc, in_maps, core_ids=list(range(NCORES)))
    R = res.results
    y_prompt = np.stack([R[c]["yp"] for c in range(NCORES)], 0)
    y_sample = np.concatenate([R[c]["ys"].reshape(16, 4, D) for c in range(NCORES)], 0)
    nkp = np.stack([R[c]["nkp"].reshape(2, 128, 4, 64) for c in range(NCORES)], 1)
    nvp = np.stack([R[c]["nvp"].reshape(2, 128, 4, 64) for c in range(NCORES)], 1)
    ncp = np.stack([R[c]["ncp"] for c in range(NCORES)], 1)
    nks = np.concatenate([R[c]["nks"].reshape(2, 16, 128, 4, 64) for c in range(NCORES)], 1)
    nvs = np.concatenate([R[c]["nvs"].reshape(2, 16, 128, 4, 64) for c in range(NCORES)], 1)
    ncs = np.concatenate([R[c]["ncs"] for c in range(NCORES)], 1)
    _CACHE["last"] = R
    return (y_prompt.astype(np.float32), y_sample.astype(np.float32), nkp.astype(np.float32), nvp.astype(np.float32),
            ncp.astype(np.float32), nks.astype(np.float32), nvs.astype(np.float32), ncs.astype(np.float32))
```
